# Optimizing a Trainium2 kernel written in Bass

```python
import math
import jax, jax.numpy as jnp
from jax import lax
import numpy as np

D_MODEL = 1024
BATCH = 8
SEQ = 2048
DEPTH = 2
DEC_BATCH = 128
DEC_SEQ = 1
PAST_LEN = 16384
PAGE_SIZE = 128

SSM_INNER = 2 * D_MODEL
HEAD_DIM = 64
SSM_HEADS = SSM_INNER // HEAD_DIM
SSM_GROUPS = 4
D_STATE = 128
SSM_CONV = 4
CONV_DIM = SSM_INNER + 2 * SSM_GROUPS * D_STATE
SSM_PROJ = 2 * SSM_INNER + 2 * SSM_GROUPS * D_STATE + SSM_HEADS
CHUNK = 128
CFM_WIDTH = 2 * D_MODEL
CFM_KERNEL = 31
N_SSM_LAYERS = (DEPTH + 1) // 2
N_CFM_LAYERS = DEPTH // 2
EPS = 1e-6

kernel_name = "hybrid_ssd_conformer_decode_step"


def rms_norm(x, g):
    xf = x.astype(jnp.float32)
    y = xf * lax.rsqrt(jnp.mean(xf * xf, axis=-1, keepdims=True) + EPS)
    return (y * g.astype(jnp.float32)).astype(x.dtype)


def layer_norm(x, g, b):
    xf = x.astype(jnp.float32)
    mu = jnp.mean(xf, axis=-1, keepdims=True)
    xc = xf - mu
    y = xc * lax.rsqrt(jnp.mean(xc * xc, axis=-1, keepdims=True) + EPS)
    return (y * g.astype(jnp.float32) + b.astype(jnp.float32)).astype(x.dtype)


def causal_depthwise_conv(x, buf, w, b):
    k = w.shape[0]
    xp = jnp.concatenate([buf.astype(x.dtype), x], axis=1)
    out = lax.conv_general_dilated(
        xp, w[:, None, :].astype(x.dtype), window_strides=(1,), padding='VALID',
        dimension_numbers=('NWC', 'WIO', 'NWC'), feature_group_count=x.shape[-1])
    return out + b.astype(x.dtype), xp[:, xp.shape[1] - (k - 1):, :]


def ssd_scan(x, dt, A, Bm, Cm, h0, chunk):
    b, L, H, P = x.shape
    G, N = Bm.shape[-2], Bm.shape[-1]
    R = H // G
    nc = L // chunk
    f32 = jnp.float32
    xs = x.astype(f32).reshape(b, nc, chunk, G, R, P)
    dts = dt.astype(f32).reshape(b, nc, chunk, G, R)
    Bs = Bm.astype(f32).reshape(b, nc, chunk, G, N)
    Cs = Cm.astype(f32).reshape(b, nc, chunk, G, N)
    a_cum = jnp.cumsum(dts * A.reshape(G, R), axis=2)
    xdt = xs * dts[..., None]
    diff = a_cum[:, :, :, None] - a_cum[:, :, None, :]
    mask = jnp.tril(jnp.ones((chunk, chunk), dtype=bool))[:, :, None, None]
    decay = jnp.exp(jnp.where(mask, diff, -jnp.inf))
    cb = jnp.einsum('bcqgn,bckgn->bcqkg', Cs, Bs)
    y_diag = jnp.einsum('bcqkg,bcqkgr,bckgrp->bcqgrp', cb, decay, xdt)
    states = jnp.einsum('bcqgn,bcqgr,bcqgrp->bcgrpn', Bs, jnp.exp(a_cum[:, :, -1:] - a_cum), xdt)
    chunk_decay = jnp.exp(a_cum[:, :, -1])

    def step(h, inp):
        dec, st = inp
        return dec[..., None, None] * h + st, h

    h_init = h0.astype(f32).reshape(b, G, R, P, N)
    h_fin, prev = lax.scan(step, h_init, (jnp.moveaxis(chunk_decay, 1, 0), jnp.moveaxis(states, 1, 0)))
    prev = jnp.moveaxis(prev, 0, 1)
    y_off = jnp.einsum('bcqgn,bcgrpn,bcqgr->bcqgrp', Cs, prev, jnp.exp(a_cum))
    y = (y_diag + y_off).reshape(b, L, H, P)
    return y, h_fin.reshape(b, H, P, N)


def mamba2_mixer(u, conv_buf, h0, chunk, w_in, conv_w, conv_b, dt_bias, A_log, D_skip, norm_g, w_out):
    b, L, _ = u.shape
    zxbcdt = u @ w_in
    z, xbc, dt = jnp.split(zxbcdt, [SSM_INNER, SSM_INNER + CONV_DIM], axis=-1)
    xbc, new_buf = causal_depthwise_conv(xbc, conv_buf, conv_w, conv_b)
    xbc = jax.nn.silu(xbc)
    xs, Bm, Cm = jnp.split(xbc, [SSM_INNER, SSM_INNER + SSM_GROUPS * D_STATE], axis=-1)
    xs = xs.reshape(b, L, SSM_HEADS, HEAD_DIM)
    Bm = Bm.reshape(b, L, SSM_GROUPS, D_STATE)
    Cm = Cm.reshape(b, L, SSM_GROUPS, D_STATE)
    dt = jax.nn.softplus(dt.astype(jnp.float32) + dt_bias.astype(jnp.float32))
    A = -jnp.exp(A_log.astype(jnp.float32))
    y, h_new = ssd_scan(xs, dt, A, Bm, Cm, h0, chunk)
    y = y + D_skip.astype(jnp.float32)[:, None] * xs.astype(jnp.float32)
    y = y.reshape(b, L, SSM_INNER).astype(u.dtype)
    y = rms_norm(y * jax.nn.silu(z), norm_g)
    return y @ w_out, new_buf, h_new


def conformer_conv_mixer(u, conv_buf, w_in, b_in, conv_w, conv_b, ln_g, ln_b, w_out):
    proj = u @ w_in + b_in
    val, glu_gate, z = jnp.split(proj, 3, axis=-1)
    v = val * jax.nn.sigmoid(glu_gate)
    c, new_buf = causal_depthwise_conv(v, conv_buf, conv_w, conv_b)
    c = jax.nn.silu(layer_norm(c, ln_g, ln_b))
    return (c * jax.nn.silu(z)) @ w_out, new_buf


def setup_inputs(seed: int = 0) -> dict:
    key = jax.random.key(seed)
    ks = jax.random.split(key, 24)
    f32 = jnp.float32
    nrm = lambda k, s, sc: jax.random.normal(k, s, f32) * sc
    dt0 = jnp.exp(jax.random.uniform(ks[10], (N_SSM_LAYERS, SSM_HEADS), f32, math.log(1e-3), math.log(1e-1)))
    return {
        "x_prompt": nrm(ks[0], (BATCH, SEQ, D_MODEL), 1.0),
        "x_sample": nrm(ks[1], (DEC_BATCH, DEC_SEQ, D_MODEL), 1.0),
        "state_ssm": nrm(ks[2], (N_SSM_LAYERS, DEC_BATCH, SSM_HEADS, HEAD_DIM, D_STATE), 0.5),
        "state_conv_ssm": nrm(ks[3], (N_SSM_LAYERS, DEC_BATCH, SSM_CONV - 1, CONV_DIM), 1.0),
        "state_conv_cfm": nrm(ks[4], (N_CFM_LAYERS, DEC_BATCH, CFM_KERNEL - 1, CFM_WIDTH), 0.5),
        "g_pre": 1.0 + nrm(ks[5], (DEPTH, D_MODEL), 0.01),
        "g_post": 1.0 + nrm(ks[6], (DEPTH, D_MODEL), 0.01),
        "ssm_w_in": nrm(ks[7], (N_SSM_LAYERS, D_MODEL, SSM_PROJ), D_MODEL ** -0.5),
        "ssm_conv_w": nrm(ks[8], (N_SSM_LAYERS, SSM_CONV, CONV_DIM), SSM_CONV ** -0.5),
        "ssm_conv_b": nrm(ks[9], (N_SSM_LAYERS, CONV_DIM), 0.02),
        "ssm_dt_bias": dt0 + jnp.log(-jnp.expm1(-dt0)),
        "ssm_A_log": jnp.log(jax.random.uniform(ks[11], (N_SSM_LAYERS, SSM_HEADS), f32, 1.0, 16.0)),
        "ssm_D": 1.0 + nrm(ks[12], (N_SSM_LAYERS, SSM_HEADS), 0.01),
        "ssm_norm_g": 1.0 + nrm(ks[13], (N_SSM_LAYERS, SSM_INNER), 0.01),
        "ssm_w_out": nrm(ks[14], (N_SSM_LAYERS, SSM_INNER, D_MODEL), SSM_INNER ** -0.5),
        "cfm_w_in": nrm(ks[15], (N_CFM_LAYERS, D_MODEL, 3 * CFM_WIDTH), D_MODEL ** -0.5),
        "cfm_b_in": nrm(ks[16], (N_CFM_LAYERS, 3 * CFM_WIDTH), 0.02),
        "cfm_conv_w": nrm(ks[17], (N_CFM_LAYERS, CFM_KERNEL, CFM_WIDTH), CFM_KERNEL ** -0.5),
        "cfm_conv_b": nrm(ks[18], (N_CFM_LAYERS, CFM_WIDTH), 0.02),
        "cfm_ln_g": 1.0 + nrm(ks[19], (N_CFM_LAYERS, CFM_WIDTH), 0.01),
        "cfm_ln_b": nrm(ks[20], (N_CFM_LAYERS, CFM_WIDTH), 0.01),
        "cfm_w_out": nrm(ks[21], (N_CFM_LAYERS, CFM_WIDTH, D_MODEL), CFM_WIDTH ** -0.5),
    }


def reference(x_prompt, x_sample, state_ssm, state_conv_ssm, state_conv_cfm, g_pre, g_post,
              ssm_w_in, ssm_conv_w, ssm_conv_b, ssm_dt_bias, ssm_A_log, ssm_D, ssm_norm_g, ssm_w_out,
              cfm_w_in, cfm_b_in, cfm_conv_w, cfm_conv_b, cfm_ln_g, cfm_ln_b, cfm_w_out):
    xp, xs = x_prompt, x_sample
    bp = x_prompt.shape[0]
    prompt_chunk = min(CHUNK, x_prompt.shape[1])
    sample_chunk = x_sample.shape[1]
    ssm_p, cssm_p, ccfm_p, ssm_s, cssm_s, ccfm_s = [], [], [], [], [], []
    for i in range(DEPTH):
        j = i // 2
        if i % 2 == 0:
            w = (ssm_w_in[j], ssm_conv_w[j], ssm_conv_b[j], ssm_dt_bias[j], ssm_A_log[j],
                 ssm_D[j], ssm_norm_g[j], ssm_w_out[j])
            buf0 = jnp.zeros((bp, SSM_CONV - 1, CONV_DIM), xp.dtype)
            h0 = jnp.zeros((bp, SSM_HEADS, HEAD_DIM, D_STATE), jnp.float32)
            op, nb_p, nh_p = mamba2_mixer(rms_norm(xp, g_pre[i]), buf0, h0, prompt_chunk, *w)
            os_, nb_s, nh_s = mamba2_mixer(rms_norm(xs, g_pre[i]), state_conv_ssm[j], state_ssm[j], sample_chunk, *w)
            ssm_p.append(nh_p); cssm_p.append(nb_p)
            ssm_s.append(nh_s.astype(state_ssm.dtype)); cssm_s.append(nb_s)
        else:
            w = (cfm_w_in[j], cfm_b_in[j], cfm_conv_w[j], cfm_conv_b[j], cfm_ln_g[j], cfm_ln_b[j], cfm_w_out[j])
            buf0 = jnp.zeros((bp, CFM_KERNEL - 1, CFM_WIDTH), xp.dtype)
            op, nb_p = conformer_conv_mixer(rms_norm(xp, g_pre[i]), buf0, *w)
            os_, nb_s = conformer_conv_mixer(rms_norm(xs, g_pre[i]), state_conv_cfm[j], *w)
            ccfm_p.append(nb_p); ccfm_s.append(nb_s)
        xp = xp + rms_norm(op, g_post[i])
        xs = xs + rms_norm(os_, g_post[i])
    new_ssm_prompt = jnp.stack(ssm_p)
    new_conv_ssm_prompt = jnp.stack(cssm_p)
    new_conv_cfm_prompt = jnp.stack(ccfm_p)
    new_ssm_sample = jnp.stack(ssm_s)
    new_conv_ssm_sample = jnp.stack(cssm_s)
    new_conv_cfm_sample = jnp.stack(ccfm_s)
    return (xp, xs, new_ssm_prompt, new_conv_ssm_prompt, new_conv_cfm_prompt,
            new_ssm_sample, new_conv_ssm_sample, new_conv_cfm_sample)
```

```python
import numpy as np
from contextlib import ExitStack
import concourse.bass as bass
import concourse.mybir as mybir
from concourse.bass_utils import run_bass_kernel_spmd

F32 = mybir.dt.float32
BF16 = mybir.dt.bfloat16
AF = mybir.ActivationFunctionType
ALU = mybir.AluOpType
AX = mybir.AxisListType

ENGS = ("pe", "act", "dve", "pool", "sp")
EPS = 1e-6


class _Op:
    __slots__ = ("eng", "fn", "deps", "dma", "cum", "sig", "sigcount", "idx")


class Prog:
    def __init__(self, nc, es, serial=False, same_engine_sync=True):
        self.nc = nc
        self.es = es
        self.ops = {e: [] for e in ENGS}
        self.lastw = {}
        self.readers = {}
        self.dma_keys = {}
        self.serial = serial
        self.same_sync = same_engine_sync
        self.prev = None
        self.out_dmas = []
        self.fence_ops = []
        self.fenced = set()
        self.since_fence_dma = []

    def fence(self):
        f = []
        for e in ENGS:
            for op in reversed(self.ops[e]):
                if op.dma is None:
                    f.append(op)
                    break
        f.extend(self.since_fence_dma)
        self.since_fence_dma = []
        self.fence_ops = f
        self.fenced = set()

    def add(self, eng, fn, r=(), w=(), dma=None, out=False, nofence=False, nosame=False):
        op = _Op()
        op.eng = eng
        op.fn = fn
        op.dma = dma
        op.sig = False
        op.sigcount = 0
        op.cum = 0
        op.idx = len(self.ops[eng])
        deps = []
        raw = set()
        for t in r:
            if t in self.lastw:
                deps.append(self.lastw[t])
                raw.add(id(self.lastw[t]))
        for t in w:
            if t in self.lastw:
                deps.append(self.lastw[t])
            deps.extend(self.readers.get(t, ()))
        if self.serial and self.prev is not None:
            deps.append(self.prev)
            raw.add(id(self.prev))
        if eng not in self.fenced and not nofence and eng != "pe":
            deps.extend(self.fence_ops)
            self.fenced.add(eng)
        seen = set()
        ud = []
        for d in deps:
            if id(d) in seen or d is op:
                continue
            seen.add(id(d))
            if d.dma is None and d.eng == eng:
                if eng == "pe" or not self.same_sync or id(d) not in raw or nosame:
                    continue
            ud.append(d)
        op.deps = ud
        for d in ud:
            d.sig = True
        if dma is not None:
            self.dma_keys[dma] = self.dma_keys.get(dma, 0) + 1
            op.cum = self.dma_keys[dma]
            if not nofence:
                self.since_fence_dma.append(op)
        self.ops[eng].append(op)
        for t in w:
            self.lastw[t] = op
            self.readers[t] = []
        for t in r:
            self.readers.setdefault(t, []).append(op)
        self.prev = op
        if out:
            self.out_dmas.append(op)
        return op

    def emit(self):
        nc, es = self.nc, self.es
        sem_eng = {e: es.enter_context(nc.semaphore("sem_" + e)) for e in ENGS}
        sem_dma = {}
        for i, k in enumerate(self.dma_keys):
            sem_dma[k] = es.enter_context(nc.semaphore("semd_%d" % i))
        for e in ENGS:
            c = 0
            for op in self.ops[e]:
                if op.dma is None and op.sig:
                    c += 1
                    op.sigcount = c
        finals = {}
        for op in self.out_dmas:
            finals[op.dma] = max(finals.get(op.dma, 0), op.cum)
        block = es.enter_context(nc.Block())
        handles = {"pe": "tensor", "act": "scalar", "dve": "vector", "pool": "gpsimd", "sp": "sync"}
        stats = {}

        def make(e):
            def body(h):
                waited = {}
                nw = 0
                for op in self.ops[e]:
                    for d in op.deps:
                        if d.dma is not None:
                            s, v = sem_dma[d.dma], d.cum * 16
                            if d.dma == "setup":
                                v = self.dma_keys["setup"] * 16
                        else:
                            s, v = sem_eng[d.eng], d.sigcount
                        key = id(s)
                        if waited.get(key, 0) >= v:
                            continue
                        waited[key] = v
                        h.wait_ge(s, v)
                        nw += 1
                    ins = op.fn(h)
                    if op.dma is not None:
                        ins.then_inc(sem_dma[op.dma], 16)
                    elif op.sig:
                        ins.then_inc(sem_eng[e], 1)
                if e == "sp":
                    for k, c in finals.items():
                        h.wait_ge(sem_dma[k], c * 16)
                stats[e] = (len(self.ops[e]), nw)
            return body

        for e in ENGS:
            getattr(block, handles[e])(make(e))
        self.stats = stats


D = 1024
SEQ = 2048
NS = 16
HP = 2048
NH = 32
PD = 64
NST = 128
CONVD = 3072
SSM_PROJ = 5152
CW = 2048
CK = 31
GT = 512
NG = SEQ // GT
NCH = GT // 128
NSLOT = 5
NT_DVE = 0

ARENA_WORDS = 53200


def build_program(do_sample=True, serial=False, same_sync=True, dbg=False):
    nc = bass.Bass("TRN2", target_bir_lowering=False)

    def din(n, s):
        return nc.dram_tensor(n, s, F32, kind="ExternalInput").ap()

    def dout(n, s):
        return nc.dram_tensor(n, s, F32, kind="ExternalOutput").ap()

    xp = din("xp", [SEQ, D])
    xs_d = din("xs", [NS, D])
    st_ssm = din("st_ssm", [NS, HP, NST])
    st_cssm = din("st_cssm", [NS * 3, CONVD])
    st_ccfm = din("st_ccfm", [NS * 30, CW])
    g_pre = din("g_pre", [16, 128])
    g_post = din("g_post", [1, 2 * D])
    w_in0 = din("w_in0", [D, SSM_PROJ])
    cw0 = din("cw0", [96, 128])
    cb0 = din("cb0", [24, 128])
    dtb = din("dtb", [1, NH])
    alog = din("alog", [1, NH])
    dsk = din("dsk", [1, NH])
    ng0 = din("ng0", [16, 128])
    w_out0 = din("w_out0", [HP, D])
    w_in1 = din("w_in1", [D, 3 * CW])
    b_in1 = din("b_in1", [48, 128])
    ccw = din("ccw", [496, 128])
    ccb = din("ccb", [16, 128])
    lng = din("lng", [16, 128])
    lnb = din("lnb", [16, 128])
    w_out1 = din("w_out1", [CW, D])
    c_ident = din("c_ident", [128, 128])
    c_tri = din("c_tri", [128, 128])
    c_U = din("c_U", [128, 128])
    c_ones = din("c_ones", [128, 128])
    c_exp = din("c_exp", [NH, HP])
    c_blk = din("c_blk", [128, 32])

    yp = dout("yp", [SEQ, D])
    ys = dout("ys", [NS, D])
    ssm_p = dout("ssm_p", [HP, NST])
    cssm_p = dout("cssm_p", [72, 128])
    ccfm_p = dout("ccfm_p", [480, 128])
    ssm_s = dout("ssm_s", [NS, HP, NST])
    cssm_s = dout("cssm_s", [NS * 3, CONVD])
    ccfm_s = dout("ccfm_s", [NS * 30, CW])

    es = ExitStack()
    with es:
        arena = es.enter_context(nc.sbuf_tensor("arena", [128, ARENA_WORDS], F32))
        arena_t = arena
        arena = arena_t[:, :]
        pf = [es.enter_context(nc.psum_tensor("pf%d" % i, [128, 512], F32))[:, :] for i in range(6)]
        pb = [es.enter_context(nc.psum_tensor("pb%d" % i, [128, 1024], BF16))[:, :] for i in range(2)]
        P = Prog(nc, es, serial=serial, same_engine_sync=same_sync)

        class Alloc:
            def __init__(self, base, limit):
                self.base = base
                self.cur = base
                self.limit = limit
                self.peak = base

            def f32(self, n):
                a = arena[:, self.cur:self.cur + n]
                self.cur += n
                self.peak = max(self.peak, self.cur)
                assert self.cur <= self.limit, ("arena overflow", self.cur, self.limit)
                return a

            def bf16(self, n):
                w = (n + 1) // 2
                a = arena[:, self.cur:self.cur + w].bitcast(BF16)
                self.cur += w
                self.peak = max(self.peak, self.cur)
                assert self.cur <= self.limit, ("arena overflow", self.cur, self.limit)
                return a[:, 0:n]

            def reset(self):
                self.cur = self.base

        CA = Alloc(0, ARENA_WORDS)
        ident_f = CA.f32(128)
        tri_f = CA.f32(128)
        U_f = CA.f32(128)
        ones_f = CA.f32(128)
        ident_b = CA.bf16(128)
        tri_b = CA.bf16(128)
        U_b = CA.bf16(128)
        ones_b = CA.bf16(128)
        negm_b = CA.bf16(128)
        gpre = CA.f32(16)
        ngs = CA.f32(16)
        cw_sb = CA.f32(96)
        cb_sb = CA.f32(24)
        bin_sb = CA.f32(48)
        ccw_sb = CA.f32(496)
        ccb_sb = CA.f32(16)
        lng_sb = CA.f32(16)
        lnb_sb = CA.f32(16)
        gpost_bc = CA.f32(D)
        A_bc = CA.f32(NH)
        D_bc = CA.f32(NH)
        dtb_bc = CA.f32(NH)
        nhalf = CA.f32(8)
        wdt = CA.bf16(8 * 32)
        xres = CA.f32(NCH * D)
        uT = CA.bf16(8 * GT)
        wbuf = [CA.bf16(8 * 512) for _ in range(NSLOT)]
        xnb = [CA.bf16(D) for _ in range(2)]
        ssq = CA.f32(8)
        rstd = CA.f32(8)
        prevT = CA.f32(HP)
        prevTb = CA.bf16(HP)
        halo = CA.f32(72)
        vhalo = CA.bf16(16 * 30)
        stg = CA.f32(128)
        blk_f = CA.f32(32)
        halob = CA.bf16(72)
        xs_sb = arena[:, ARENA_WORDS - D:ARENA_WORDS]
        usT = CA.bf16(8 * NS)
        usT3 = usT.rearrange("p (k s) -> p k s", k=8)
        PH = Alloc(CA.cur, ARENA_WORDS)

        uT3 = uT.rearrange("p (k t) -> p k t", k=8)
        wb3 = [w.rearrange("p (k n) -> p k n", k=8) for w in wbuf]
        wdt3 = wdt.rearrange("p (k n) -> p k n", k=8)
        xres3 = xres.rearrange("p (c d) -> p c d", c=NCH)

        rot = {"list": [0, 1, 2, 3, 4, 5], "i": 0}

        def bank():
            l = rot["list"]
            b = l[rot["i"] % len(l)]
            rot["i"] += 1
            return b

        wblocks = []

        def l0_blocks():
            bl = []
            bl.append((w_in0, 0, 0))
            bl.append((w_in0, 0, 4096))
            bl.append((w_in0, 0, 4608))
            for g in range(4):
                bl.append((w_in0, 0, 2048 + 512 * g))
                if g + 1 < 4:
                    bl.append((w_in0, 0, 512 * (g + 1)))
            for nh in range(2):
                for kh in range(2):
                    bl.append((w_out0, 1024 * kh, 512 * nh))
            return bl

        def l1_blocks():
            bl = []
            for vb in range(4):
                bl.append((w_in1, 0, 512 * vb))
                bl.append((w_in1, 0, 2048 + 512 * vb))
            for zb in range(4):
                bl.append((w_in1, 0, 4096 + 512 * zb))
            for nh in range(2):
                for kh in range(2):
                    bl.append((w_out1, 1024 * kh, 512 * nh))
            return bl

        ngroups_total = NG + (1 if do_sample else 0)
        for _ in range(ngroups_total):
            wblocks.extend(l0_blocks())
            wblocks.extend(l1_blocks())
        wstate = {"issued": 0, "next": 0}

        def w_issue_upto(n):
            while wstate["issued"] < min(n, len(wblocks)):
                i = wstate["issued"]
                ap, r0, c0 = wblocks[i]
                slot = i % NSLOT
                src = ap[r0:r0 + 1024, c0:c0 + 512].rearrange("(k p) n -> p k n", p=128)
                P.add("pool", lambda h, s=slot, src=src: h.dma_start(out=wb3[s], in_=src),
                      w=[("wbuf", slot)], dma=("w", slot), nofence=True)
                wstate["issued"] += 1

        def w_take():
            i = wstate["next"]
            wstate["next"] += 1
            assert i < wstate["issued"]
            return i, i % NSLOT

        def w_done(i):
            w_issue_upto(i + NSLOT + 1)

        def sdma(out, in_, wtok):
            P.add("sp", lambda h: h.dma_start(out=out, in_=in_), w=[wtok], dma="setup")

        sdma(ident_f, c_ident[:, :], "ident_f")
        sdma(tri_f, c_tri[:, :], "tri_f")
        sdma(U_f, c_U[:, :], "U_f")
        sdma(ones_f, c_ones[:, :], "ones_f")
        sdma(blk_f, c_blk[:, :], "blk_f")
        sdma(A_bc, alog[0, :].partition_broadcast(128), "A_bc")
        sdma(D_bc, dsk[0, :].partition_broadcast(128), "D_bc")
        sdma(dtb_bc, dtb[0, :].partition_broadcast(128), "dtb_bc")
        P.add("pool", lambda h: h.dma_start(out=wdt3, in_=w_in0[:, 5120:5152].rearrange("(k p) n -> p k n", p=128)),
              w=["wdt"], dma="wdt")
        w_issue_upto(NSLOT)
        P.add("dve", lambda h: h.tensor_copy(out=ident_b, in_=ident_f), r=["ident_f"], w=["ident_b"])
        P.add("dve", lambda h: h.tensor_copy(out=tri_b, in_=tri_f), r=["tri_f"], w=["tri_b"])
        P.add("dve", lambda h: h.tensor_copy(out=U_b, in_=U_f), r=["U_f"], w=["U_b"])
        P.add("dve", lambda h: h.tensor_copy(out=ones_b, in_=ones_f), r=["ones_f"], w=["ones_b"])
        P.add("dve", lambda h: h.tensor_scalar(out=negm_b, in0=ones_f, scalar1=-1.0 / CW, scalar2=None, op0=ALU.mult),
              r=["ones_f"], w=["negm_b"])
        P.add("act", lambda h: h.activation(out=A_bc, in_=A_bc, func=AF.Exp), r=["A_bc"], w=["A_bc"])
        P.add("dve", lambda h: h.tensor_scalar(out=A_bc, in0=A_bc, scalar1=-1.0, scalar2=None, op0=ALU.mult),
              r=["A_bc"], w=["A_bc"])
        P.add("pool", lambda h: h.memset(nhalf, -0.5), w=["nhalf"])
        P.add("pool", lambda h: h.memset(halo, 0.0), w=["halo"])
        P.add("pool", lambda h: h.memset(vhalo, 0.0), w=["vhalo"])
        P.add("pool", lambda h: h.memset(halob, 0.0), w=["halob"])
        P.add("pool", lambda h: h.memset(prevT, 0.0), w=["prevT"])
        P.add("pool", lambda h: h.memset(prevTb, 0.0), w=["prevTb"])

        stgs = [stg] + [PH.f32(128) for _ in range(3)]
        stg_i = {"i": 0}

        def load_T(dram2d, rows, dst):
            r0 = 0
            while r0 < rows:
                n = min(128, rows - r0)
                si = stg_i["i"] % len(stgs)
                stg_i["i"] += 1
                sg_ = stgs[si]
                P.add("sp", lambda h, r0=r0, n=n, sg_=sg_: h.dma_start(out=sg_[0:n, :], in_=dram2d[r0:r0 + n, :]),
                      w=[("stg", si)], dma=("stg", si))
                b = bank()
                P.add("pe", lambda h, n=n, b=b, sg_=sg_: h.transpose(out=pf[b][:, 0:n], in_=sg_[0:n, :], identity=ident_f[0:n, 0:n]),
                      r=[("stg", si), "ident_f"], w=[("pf", b)])
                P.add("dve", lambda h, r0=r0, n=n, b=b: h.tensor_copy(out=dst[:, r0:r0 + n], in_=pf[b][:, 0:n]),
                      r=[("pf", b)], w=["params"])
                r0 += n

        load_T(g_pre, 16, gpre)
        load_T(ng0, 16, ngs)
        load_T(cw0, 96, cw_sb)
        load_T(cb0, 24, cb_sb)
        load_T(b_in1, 48, bin_sb)
        load_T(ccw, 496, ccw_sb)
        load_T(ccb, 16, ccb_sb)
        load_T(lng, 16, lng_sb)
        load_T(lnb, 16, lnb_sb)

        def load_gpost(layer):
            P.add("sp", lambda h: h.dma_start(out=gpost_bc, in_=g_post[0, layer * D:(layer + 1) * D].partition_broadcast(128)),
                  w=["gpost"], dma="gpost")
        cw3 = cw_sb.rearrange("p (k b) -> p k b", k=4)
        ccw3 = ccw_sb.rearrange("p (k b) -> p k b", k=CK)
        halo3 = halo.rearrange("p (k b) -> p k b", k=3)
        halob3 = halob.rearrange("p (k b) -> p k b", k=3)
        vhalo3 = vhalo.rearrange("p (b k) -> p b k", b=16)

        def pow_rstd(src, dst, n, scale, rt, wt):
            P.add("dve", lambda h: h.tensor_scalar(out=dst[:, 0:n], in0=src[:, 0:n], scalar1=scale, scalar2=EPS,
                                                   op0=ALU.mult, op1=ALU.add), r=[rt], w=[wt])
            P.add("pool", lambda h: h.tensor_tensor(out=dst[:, 0:n], in0=dst[:, 0:n], in1=nhalf[:, 0:n], op=ALU.pow),
                  r=[wt, "nhalf"], w=[wt])

        def prenorm_p1(layer, c, load_from=None):
            if load_from is not None:
                P.add("sp", lambda h: h.dma_start(out=xres3[:, c, :], in_=load_from),
                      w=[("xres", c)], dma=("xld", c))
            xb_ = xnb[c % 2]
            P.add("act", lambda h: h.activation(out=xb_, in_=xres3[:, c, :], func=AF.Square, accum_out=ssq[:, c:c + 1]),
                  r=[("xres", c)], w=[("xnb", c % 2), ("ssq", c)])
            P.add("dve", lambda h: h.tensor_scalar(out=rstd[:, c:c + 1], in0=ssq[:, c:c + 1], scalar1=1.0 / D,
                                                   scalar2=EPS, op0=ALU.mult, op1=ALU.add),
                  r=[("ssq", c)], w=[("rstd", c)])
            P.add("pool", lambda h: h.tensor_tensor(out=rstd[:, c:c + 1], in0=rstd[:, c:c + 1], in1=nhalf[:, 0:1], op=ALU.pow),
                  r=[("rstd", c), "nhalf"], w=[("rstd", c)])
            P.add("act", lambda h: h.activation(out=xb_, in_=xres3[:, c, :], func=AF.Copy, scale=rstd[:, c:c + 1]),
                  r=[("xres", c), ("rstd", c)], w=[("xnb", c % 2)])

        def prenorm_p2(layer, c):
            xb_ = xnb[c % 2]
            pbi = c % 2

            def tr(h):
                for k in range(8):
                    ins = h.transpose(out=pb[pbi][:, k * 128:(k + 1) * 128], in_=xb_[:, k * 128:(k + 1) * 128],
                                      identity=ident_b)
                return ins
            P.add("pe", tr, r=[("xnb", c % 2), "ident_b"], w=[("pb", pbi)])
            P.add("dve", lambda h: h.tensor_tensor(
                out=uT3[:, :, c * 128:(c + 1) * 128],
                in0=pb[pbi].rearrange("p (k t) -> p k t", k=8),
                in1=gpre[:, layer * 8:(layer + 1) * 8].unsqueeze(2).to_broadcast([128, 8, 128]),
                op=ALU.mult), r=[("pb", pbi), "params"], w=[("uT", c)])

        def prenorm_chunk(layer, c, load_from=None):
            prenorm_p1(layer, c, load_from)
            prenorm_p2(layer, c)

        def outproj(layer, lhs_of_chunk, lhs_tokens, PHA, after1=None, after2=None, prep_all=None, prep1=None, prep2=None):
            otmp = [PHA.f32(D) for _ in range(2)]
            ss2 = PHA.f32(4 * NCH).rearrange("p (c j) -> p c j", c=NCH)
            w_issue_upto(wstate["next"] + 4)
            blks = [w_take() for _ in range(4)]
            if prep_all is not None:
                prep_all()
            bks_of = {}

            def main(c, mid=None):
                bks = [bank(), bank()]
                bks_of[c] = bks
                ot = otmp[c % 2]
                for nh in range(2):
                    if nh == 1 and mid is not None:
                        mid()
                    def mm(h, nh=nh, b=bks[nh]):
                        for kh in range(2):
                            slot = blks[nh * 2 + kh][1]
                            for k in range(8):
                                ins = h.matmul(out=pf[b][:, :], lhsT=lhs_of_chunk(c, kh * 8 + k), rhs=wb3[slot][:, k, :],
                                               start=(kh == 0 and k == 0), stop=(kh == 1 and k == 7))
                        return ins
                    P.add("pe", mm, r=lhs_tokens(c) + [("wbuf", blks[nh * 2][1]), ("wbuf", blks[nh * 2 + 1][1])],
                          w=[("pf", bks[nh])])
                    P.add("act", lambda h, nh=nh, b=bks[nh]: h.activation(
                        out=ot[:, nh * 512:(nh + 1) * 512], in_=pf[b][:, :], func=AF.Square,
                        accum_out=ss2[:, c, nh:nh + 1]), r=[("pf", bks[nh])], w=[("otmp", c % 2, nh), ("ss2", c, nh)])

            def post(c):
                bks = bks_of[c]
                ot = otmp[c % 2]
                P.add("dve", lambda h: h.tensor_tensor(out=ss2[:, c, 2:3], in0=ss2[:, c, 0:1], in1=ss2[:, c, 1:2], op=ALU.add),
                      r=[("ss2", c, 0), ("ss2", c, 1)], w=[("ss2", c, 2)])
                P.add("dve", lambda h: h.tensor_scalar(out=ss2[:, c, 3:4], in0=ss2[:, c, 2:3], scalar1=1.0 / D, scalar2=EPS,
                                                       op0=ALU.mult, op1=ALU.add), r=[("ss2", c, 2)], w=[("ss2", c, 3)])
                P.add("pool", lambda h: h.tensor_tensor(out=ss2[:, c, 3:4], in0=ss2[:, c, 3:4], in1=nhalf[:, 0:1], op=ALU.pow),
                      r=[("ss2", c, 3), "nhalf"], w=[("ss2", c, 3)])
                for nh in range(2):
                    P.add("dve", lambda h, nh=nh, b=bks[nh]: h.scalar_tensor_tensor(
                        out=ot[:, nh * 512:(nh + 1) * 512], in0=pf[b][:, :], scalar=ss2[:, c, 3:4],
                        in1=gpost_bc[:, nh * 512:(nh + 1) * 512], op0=ALU.mult, op1=ALU.mult),
                        r=[("pf", bks[nh]), ("ss2", c, 3), "gpost"], w=[("otmp", c % 2, nh)])
                    P.add("pool", lambda h, nh=nh: h.tensor_tensor(
                        out=xres3[:, c, nh * 512:(nh + 1) * 512], in0=xres3[:, c, nh * 512:(nh + 1) * 512],
                        in1=ot[:, nh * 512:(nh + 1) * 512], op=ALU.add),
                        r=[("otmp", c % 2, nh), ("xres", c)], w=[("xres", c)])

            nop = lambda c: None
            a1 = after1 or nop
            a2 = after2 or nop
            p1 = prep1 or nop
            p2 = prep2 or nop
            p1(0)
            p2(0)
            for c in range(NCH):
                if c + 1 < NCH:
                    p1(c + 1)
                main(c, mid=(lambda c=c: p2(c + 1)) if c + 1 < NCH else None)
                if c >= 1:
                    post(c - 1)
                if c >= 2:
                    a1(c - 2)
                if c >= 3:
                    a2(c - 3)
            post(NCH - 1)
            a1(NCH - 2)
            a2(NCH - 3)
            a1(NCH - 1)
            a2(NCH - 2)
            a2(NCH - 1)
            for (i, s) in blks:
                w_done(i)

        def layer0(G):
            PH.reset()
            A = PH
            load_gpost(0)
            sz = A.f32(NCH * HP)
            sz3 = sz.rearrange("p (c f) -> p c f", c=NCH)
            BT = A.bf16(4 * GT).rearrange("p (g t) -> p g t", g=4)
            CT = A.bf16(4 * GT).rearrange("p (g t) -> p g t", g=4)
            Bt = A.bf16(NCH * 512).rearrange("p (c n) -> p c n", c=NCH)
            stage = [A.bf16(516) for _ in range(2)]
            cw32 = A.bf16(24 * 4 * 32).rearrange("p (b k m) -> p b k m", b=24, k=4)
            P.add("dve", lambda h: h.tensor_tensor(
                out=cw32, in0=blk_f.unsqueeze(1).unsqueeze(1).to_broadcast([128, 24, 4, 32]),
                in1=cw_sb.rearrange("p (k b) -> p b k", k=4).unsqueeze(3).to_broadcast([128, 24, 4, 32]), op=ALU.mult),
                r=["blk_f", "params"], w=["cw32"])
            xsf = [A.f32(512)] * 2
            dtr = A.f32(NCH * NH).rearrange("p (c h) -> p c h", c=NCH)
            dt_ = A.f32(NCH * NH).rearrange("p (c h) -> p c h", c=NCH)
            dtA = A.f32(NCH * NH).rearrange("p (c h) -> p c h", c=NCH)
            dtw = A.f32(NCH * NH).rearrange("p (c h) -> p c h", c=NCH)
            eaw = A.f32(NCH * 3 * NH).rearrange("p (c j h) -> p c j h", c=NCH, j=3)
            xtok = A.f32(NCH * 512).rearrange("p (c f) -> p c f", c=NCH)
            Rf = [A.f32(1024) for _ in range(2)]
            decT = [A.bf16(1024) for _ in range(2)]
            cbm = [A.bf16(128) for _ in range(2)]
            MT = [A.bf16(1024) for _ in range(2)]
            xdt = [A.bf16(512) for _ in range(2)]
            xds = [A.bf16(512) for _ in range(2)]
            xD = [A.bf16(512) for _ in range(2)]
            ytmp = [A.f32(512) for _ in range(2)]
            ptmp = A.f32(512)
            ynb = [A.bf16(HP)] * 2
            ynT = [A.bf16(HP).rearrange("p (k t) -> p k t", k=16) for _ in range(2)]
            ss3 = A.f32(8)

            tok0 = G * GT
            uT_toks = [("uT", c) for c in range(NCH)]

            b = bank()

            def mm_dt(h, b=b):
                for c in range(NCH):
                    for k in range(8):
                        ins = h.matmul(out=pf[b][:, c * NH:(c + 1) * NH], lhsT=uT3[:, k, c * 128:(c + 1) * 128],
                                       rhs=wdt3[:, k, :], start=(k == 0), stop=(k == 7))
                return ins
            P.add("pe", mm_dt, r=uT_toks + ["wdt"], w=[("pf", b)])
            P.add("dve", lambda h, b=b: h.tensor_tensor(
                out=dtr, in0=pf[b][:, 0:NCH * NH].rearrange("p (c h) -> p c h", c=NCH),
                in1=dtb_bc.unsqueeze(1).to_broadcast([128, NCH, NH]), op=ALU.add),
                r=[("pf", b), "dtb_bc"], w=["dtr"])
            P.add("act", lambda h: h.activation(out=dtr, in_=dtr, func=AF.Exp), r=["dtr"], w=["dtr"])
            P.add("act", lambda h: h.activation(out=dt_, in_=dtr, func=AF.Ln, bias=1.0), r=["dtr"], w=["dt"])
            P.add("dve", lambda h: h.tensor_tensor(out=dtA, in0=dt_, in1=A_bc.unsqueeze(1).to_broadcast([128, NCH, NH]),
                                                   op=ALU.mult), r=["dt", "A_bc"], w=["dtA"])
            b = bank()

            def mm_cum(h, b=b):
                for c in range(NCH):
                    for j, lt in enumerate((tri_f, U_f, ones_f)):
                        ins = h.matmul(out=pf[b][:, (c * 3 + j) * NH:(c * 3 + j + 1) * NH], lhsT=lt, rhs=dtA[:, c, :],
                                       start=True, stop=True)
                return ins
            P.add("pe", mm_cum, r=["dtA", "tri_f", "U_f", "ones_f"], w=[("pf", b)])
            P.add("act", lambda h, b=b: h.activation(out=eaw.rearrange("p c j h -> p (c j h)"), in_=pf[b][:, 0:NCH * 3 * NH],
                                                     func=AF.Exp), r=[("pf", b)], w=["eaw"])
            P.add("dve", lambda h: h.tensor_tensor(out=dtw, in0=dt_, in1=eaw[:, :, 1, :], op=ALU.mult),
                  r=["dt", "eaw"], w=["dtw"])

            def z_chunk(zb, slot, c, raw=False):
                b = bank()

                def mm(h):
                    for k in range(8):
                        ins = h.matmul(out=pf[b][:, :], lhsT=uT3[:, k, c * 128:(c + 1) * 128], rhs=wb3[slot][:, k, :],
                                       start=(k == 0), stop=(k == 7))
                    return ins
                P.add("pe", mm, r=[("uT", c), ("wbuf", slot)], w=[("pf", b)])
                P.add("act", lambda h: h.activation(out=sz3[:, c, zb * 512:(zb + 1) * 512], in_=pf[b][:, :],
                                                    func=(AF.Copy if raw else AF.Silu)),
                      r=[("pf", b)], w=[("sz", c, zb)])

            i, slot = w_take()
            for c in range(NCH):
                z_chunk(0, slot, c)
            w_done(i)

            def stA(job, n):
                b = bank()
                job["bA"] = b
                job["sidx"] = n % 2
                slot, j, cb, sidx = job["slot"], job["j"], job["cb"], n % 2

                def mm(h):
                    for k in range(8):
                        ins = h.matmul(out=pf[b][:, :], lhsT=wb3[slot][:, k, j * 128:(j + 1) * 128], rhs=uT3[:, k, :],
                                       start=(k == 0), stop=(k == 7))
                    return ins
                P.add("pe", mm, r=uT_toks + [("wbuf", slot)], w=[("pf", b)])
                st = stage[sidx]
                P.add("act", lambda h: h.activation(out=st[:, 3:515], in_=pf[b][:, :], func=AF.Copy),
                      r=[("pf", b)], w=[("stage", sidx, 1)])
                if G == NG - 1:
                    P.add("dve", lambda h: h.tensor_copy(out=halo3[:, :, cb], in_=pf[b][:, 509:512]),
                          r=[("pf", b)], w=[("halo", cb)])
                P.add("pool", lambda h: h.tensor_copy(out=st[:, 0:3], in_=halob3[:, :, cb]),
                      r=[("halob", cb)], w=[("stage", sidx, 0)])
                P.add("pool", lambda h: h.tensor_copy(out=halob3[:, :, cb], in_=st[:, 512:515]),
                      r=[("stage", sidx, 1), ("stage", sidx, 0)], w=[("halob", cb)])

            def stB(job):
                cb, sidx = job["cb"], job["sidx"]
                st = stage[sidx]
                b2 = bank()

                def mmconv(h):
                    for k in range(4):
                        for i4 in range(4):
                            ps_ = slice(32 * i4, 32 * i4 + 32)
                            ins = h.matmul(out=pf[b2][ps_, :], lhsT=cw32[ps_, cb, k, :], rhs=st[ps_, k:k + 512],
                                           start=(k == 0), stop=(k == 3), tile_position=(32 * i4, 32 * i4))
                    return ins
                P.add("pe", mmconv, r=[("stage", sidx, 0), ("stage", sidx, 1), "cw32"], w=[("pf", b2)])
                kind, g = job["kind"], job["g"]
                if kind in ("B", "C"):
                    dstT = BT if kind == "B" else CT
                    P.add("act", lambda h: h.activation(out=dstT[:, g, :], in_=pf[b2][:, :], func=AF.Silu,
                                                        bias=cb_sb[:, cb:cb + 1]),
                          r=[("pf", b2), "params"], w=[(kind + "T", g)])
                else:
                    P.add("act", lambda h: h.activation(out=xsf[0], in_=pf[b2][:, :], func=AF.Silu, bias=cb_sb[:, cb:cb + 1]),
                          r=[("pf", b2), "params"], w=[("xsf", 0)])

            def stC(job):
                if job["kind"] != "x":
                    return
                j = job["j"]
                b = bank()

                def trx(h):
                    for c in range(NCH):
                        ins = h.transpose(out=pf[b][:, c * 128:(c + 1) * 128], in_=xsf[0][:, c * 128:(c + 1) * 128],
                                          identity=ident_f)
                    return ins
                P.add("pe", trx, r=[("xsf", 0), "ident_f"], w=[("pf", b)])
                P.add("dve", lambda h: h.tensor_copy(
                    out=xtok[:, :, j * 128:(j + 1) * 128], in_=pf[b][:, :].rearrange("p (c f) -> p c f", c=NCH)),
                    r=[("pf", b)], w=[("xtok", j)])

            def run_conv_jobs(jobs):
                n = len(jobs)
                for t in range(n + 2):
                    if 0 <= t - 2 < n:
                        stC(jobs[t - 2])
                    if 0 <= t - 1 < n:
                        stB(jobs[t - 1])
                    if t < n:
                        stA(jobs[t], t)

            jobs = []
            done_after = []
            for which in ("B", "C"):
                i, slot = w_take()
                done_after.append(i)
                for g in range(4):
                    jobs.append(dict(cb=(16 if which == "B" else 20) + g, slot=slot, j=g, kind=which, g=g))
            run_conv_jobs(jobs)
            for i in done_after:
                w_done(i)
            for c in range(NCH):
                pbi = c % 2

                def trb(h, c=c, pbi=pbi):
                    for g in range(4):
                        ins = h.transpose(out=pb[pbi][:, g * 128:(g + 1) * 128], in_=BT[:, g, c * 128:(c + 1) * 128],
                                          identity=ident_b)
                    return ins
                P.add("pe", trb, r=[("BT", g) for g in range(4)] + ["ident_b"], w=[("pb", pbi)])
                P.add("dve", lambda h, c=c, pbi=pbi: h.tensor_copy(out=Bt[:, c, :], in_=pb[pbi][:, 0:512]),
                      r=[("pb", pbi)], w=[("Bt", c)])

            def ssd_group(g):
                i, slot = w_take()
                run_conv_jobs([dict(cb=g * 4 + j, slot=slot, j=j, kind="x", g=g) for j in range(4)])
                w_done(i)
                xtok_toks = [("xtok", j) for j in range(4)]
                hs = slice(g * 8, (g + 1) * 8)

                cb_bank = {}

                def s1r(c):
                    q = c % 2
                    P.add("dve", lambda h: h.tensor_tensor(
                        out=Rf[q].rearrange("p (h t) -> p h t", h=8), in0=tri_f.unsqueeze(1).to_broadcast([128, 8, 128]),
                        in1=dtA[:, c, hs].unsqueeze(2).to_broadcast([128, 8, 128]), op=ALU.mult),
                        r=["tri_f", "dtA"], w=[("Rf", q)])

                def s1a(c):
                    q = c % 2
                    dbk = [bank(), bank()]
                    for half in range(2):
                        P.add("pe", lambda h, half=half, b=dbk[half]: h.matmul(
                            out=pf[b][:, :], lhsT=U_f, rhs=Rf[q][:, half * 512:(half + 1) * 512], start=True, stop=True),
                            r=[("Rf", q), "U_f"], w=[("pf", dbk[half])])
                        P.add("act", lambda h, half=half, b=dbk[half]: h.activation(
                            out=decT[q][:, half * 512:(half + 1) * 512], in_=pf[b][:, :], func=AF.Exp),
                            r=[("pf", dbk[half])], w=[("decT", q, half)])
                    cbp = pb[q].bitcast(F32)
                    P.add("pe", lambda h: h.matmul(out=cbp[:, 0:128], lhsT=BT[:, g, c * 128:(c + 1) * 128],
                                                   rhs=CT[:, g, c * 128:(c + 1) * 128], start=True, stop=True),
                          r=[("BT", g), ("CT", g)], w=[("pb", q)])

                def s1b(c):
                    q = c % 2
                    cbp = pb[q].bitcast(F32)
                    P.add("dve", lambda h: h.tensor_tensor(out=cbm[q], in0=cbp[:, 0:128], in1=tri_f, op=ALU.mult),
                          r=[("pb", q), "tri_f"], w=[("cbm", q)])
                    P.add("dve", lambda h: h.tensor_tensor(
                        out=MT[q].rearrange("p (h t) -> p h t", h=8), in0=decT[q].rearrange("p (h t) -> p h t", h=8),
                        in1=cbm[q].unsqueeze(1).to_broadcast([128, 8, 128]), op=ALU.mult),
                        r=[("decT", q, 0), ("decT", q, 1), ("cbm", q)], w=[("MT", q)])
                    x3 = xtok[:, c, :].rearrange("p (h d) -> p h d", h=8)
                    for dst, src, tk, eng in ((xdt, dt_, "dt", "dve"), (xds, dtw, "dtw", "pool")):
                        P.add(eng, lambda h, dst=dst, src=src: h.tensor_tensor(
                            out=dst[q].rearrange("p (h d) -> p h d", h=8), in0=x3,
                            in1=src[:, c, hs].unsqueeze(2).to_broadcast([128, 8, PD]), op=ALU.mult),
                            r=xtok_toks + [tk], w=[(tk + "x", q)])
                    P.add("pool", lambda h: h.tensor_tensor(
                        out=xD[q].rearrange("p (h d) -> p h d", h=8), in0=x3,
                        in1=D_bc[:, hs].unsqueeze(2).to_broadcast([128, 8, PD]), op=ALU.mult),
                        r=xtok_toks + ["D_bc"], w=[("xD", q)])

                s2b = {}

                def s2(c):
                    q = c % 2
                    by = bank()

                    def mmy(h):
                        h.matmul(out=pf[by][:, :], lhsT=ident_b, rhs=xD[q], start=True, stop=False)
                        for hh in range(8):
                            ins = h.matmul(out=pf[by][:, hh * PD:(hh + 1) * PD], lhsT=MT[q][:, hh * 128:(hh + 1) * 128],
                                           rhs=xdt[q][:, hh * PD:(hh + 1) * PD], start=False, stop=(hh == 7))
                        return ins
                    P.add("pe", mmy, r=[("xD", q), ("MT", q), ("dtx", q), "ident_b"], w=[("pf", by)])
                    bo = bank()
                    P.add("pe", lambda h: h.matmul(out=pf[bo][:, :], lhsT=CT[:, g, c * 128:(c + 1) * 128],
                                                   rhs=prevTb[:, g * 512:(g + 1) * 512], start=True, stop=True),
                          r=[("CT", g), ("prevTb", g)], w=[("pf", bo)])
                    bs = bank()
                    P.add("pe", lambda h: h.matmul(out=pf[bs][:, :], lhsT=Bt[:, c, g * 128:(g + 1) * 128],
                                                   rhs=xds[q], start=True, stop=True),
                          r=[("Bt", c), ("dtwx", q)], w=[("pf", bs)])
                    s2b[c] = (by, bo, bs)

                def s2r(c):
                    q = c % 2
                    by, bo, bs = s2b[c]
                    yt = ytmp[q]
                    P.add("dve", lambda h: h.tensor_tensor(
                        out=yt.rearrange("p (h d) -> p h d", h=8), in0=pf[bo][:, :].rearrange("p (h d) -> p h d", h=8),
                        in1=eaw[:, c, 0, hs].unsqueeze(2).to_broadcast([128, 8, PD]), op=ALU.mult),
                        r=[("pf", bo), "eaw"], w=[("ytmp", q)])
                    P.add("dve", lambda h: h.tensor_tensor(
                        out=ptmp.rearrange("p (h d) -> p h d", h=8),
                        in0=prevT[:, g * 512:(g + 1) * 512].rearrange("p (h d) -> p h d", h=8),
                        in1=eaw[:, c, 2, hs].unsqueeze(2).to_broadcast([128, 8, PD]), op=ALU.mult),
                        r=[("prevT", g), "eaw"], w=["ptmp"])
                    P.add("dve", lambda h: h.tensor_tensor(out=prevT[:, g * 512:(g + 1) * 512], in0=ptmp, in1=pf[bs][:, :],
                                                           op=ALU.add),
                          r=["ptmp", ("pf", bs)], w=[("prevT", g)])
                    P.add("act", lambda h: h.activation(out=prevTb[:, g * 512:(g + 1) * 512], in_=prevT[:, g * 512:(g + 1) * 512],
                                                        func=AF.Copy),
                          r=[("prevT", g)], w=[("prevTb", g)])
                    P.add("dve", lambda h: h.tensor_tensor(out=yt, in0=yt, in1=pf[by][:, :], op=ALU.add),
                          r=[("pf", by), ("ytmp", q)], w=[("ytmp", q)])
                    P.add("pool", lambda h: h.tensor_tensor(
                        out=sz3[:, c, g * 512:(g + 1) * 512], in0=sz3[:, c, g * 512:(g + 1) * 512], in1=yt, op=ALU.mult),
                        r=[("ytmp", q), ("sz", c, g)], w=[("sz", c, g)])

                if g + 1 < 4:
                    iz, zslot = w_take()
                for t in range(NCH + 2):
                    if t < NCH:
                        s1r(t)
                    if 0 <= t - 2 < NCH:
                        s2(t - 2)
                    if t < NCH:
                        s1a(t)
                    if 0 <= t - 2 < NCH:
                        s2r(t - 2)
                    if 0 <= t - 1 < NCH:
                        s1b(t - 1)
                    if g + 1 < 4 and t < NCH:
                        z_chunk(g + 1, zslot, t, raw=True)
                if g + 1 < 4:
                    w_done(iz)
                    zs = sz3[:, :, (g + 1) * 512:(g + 2) * 512]
                    P.add("act", lambda h: h.activation(out=zs, in_=zs, func=AF.Silu),
                          r=[("sz", c, g + 1) for c in range(NCH)], w=[("sz", c, g + 1) for c in range(NCH)])

            for g in range(4):
                ssd_group(g)

            def prep_all():
                for c in range(NCH):
                    sz_toks = [("sz", c, g) for g in range(4)]
                    P.add("act", lambda h, c=c: h.activation(out=ynb[c % 2], in_=sz3[:, c, :], func=AF.Square,
                                                             accum_out=ss3[:, c:c + 1]),
                          r=sz_toks, w=[("ynb", 0), ("ss3", c)])
                P.add("dve", lambda h: h.tensor_scalar(out=ss3[:, 4:8], in0=ss3[:, 0:4], scalar1=1.0 / HP, scalar2=EPS,
                                                       op0=ALU.mult, op1=ALU.add), r=[("ss3", c) for c in range(4)], w=["ss3r"])
                P.add("pool", lambda h: h.tensor_tensor(out=ss3[:, 4:8], in0=ss3[:, 4:8], in1=nhalf[:, 0:4], op=ALU.pow),
                      r=["ss3r", "nhalf"], w=["ss3r"])

            def prep1(c):
                sz_toks = [("sz", c, g) for g in range(4)]
                yb = ynb[c % 2]
                P.add("act", lambda h: h.activation(out=yb, in_=sz3[:, c, :], func=AF.Copy, scale=ss3[:, 4 + c:5 + c]),
                      r=sz_toks + ["ss3r"], w=[("ynb", 0)])

            def prep2(c):
                yb = ynb[c % 2]
                yT_ = ynT[c % 2]
                for half in range(2):
                    def trn(h, half=half):
                        for k in range(8):
                            kk = half * 8 + k
                            ins = h.transpose(out=pb[half][:, k * 128:(k + 1) * 128], in_=yb[:, kk * 128:(kk + 1) * 128],
                                              identity=ident_b)
                        return ins
                    P.add("pe", trn, r=[("ynb", 0), "ident_b"], w=[("pb", half)])
                    P.add("dve", lambda h, half=half: h.tensor_tensor(
                        out=yT_[:, half * 8:(half + 1) * 8, :], in0=pb[half].rearrange("p (k t) -> p k t", k=8),
                        in1=ngs[:, half * 8:(half + 1) * 8].unsqueeze(2).to_broadcast([128, 8, 128]), op=ALU.mult),
                        r=[("pb", half), "params"], w=[("ynT", c % 2, half)])

            outproj(0, lambda c, kc: ynT[c % 2][:, kc, :], lambda c: [("ynT", c % 2, 0), ("ynT", c % 2, 1)], A,
                    prep_all=prep_all, prep1=prep1, prep2=prep2,
                    after1=lambda c: prenorm_p1(1, c), after2=lambda c: prenorm_p2(1, c))
            if P.dbg == "l0":
                P.fence()
                P.dump("xres", xres, NCH * D, [("xres", c) for c in range(4)])

        def layer1(G):
            PH.reset()
            A = PH
            load_gpost(1)
            last = (G == NG - 1)
            cbuf = A.f32(16 * GT).rearrange("p (b t) -> p b t", b=16)
            aT = A.bf16(16 * GT).rearrange("p (b t) -> p b t", b=16)
            vst = [A.bf16(544) for _ in range(2)]
            NPE = CK - NT_DVE
            dg32 = A.bf16(16 * NPE * 32).rearrange("p (b k m) -> p b k m", b=16, k=NPE)
            for b4 in range(0, 16, 4):
                P.add("dve", lambda h, b4=b4: h.tensor_tensor(
                    out=dg32[:, b4:b4 + 4], in0=blk_f.unsqueeze(1).unsqueeze(1).to_broadcast([128, 4, NPE, 32]),
                    in1=ccw_sb.rearrange("p (k b) -> p b k", k=CK)[:, b4:b4 + 4, 0:NPE].unsqueeze(3).to_broadcast([128, 4, NPE, 32]),
                    op=ALU.mult), r=["blk_f", "params"], w=[("dg32", b4)])
            sgt = [A.f32(512) for _ in range(2)]
            csq = [A.bf16(512) for _ in range(2)]
            cb16 = [A.bf16(512) for _ in range(2)]
            cacc = [[A.f32(512) for _ in range(2)] for _ in range(2)] if NT_DVE > 0 else None
            szt = [A.f32(512) for _ in range(2)]
            t1 = [A.f32(512) for _ in range(2)]
            mean = A.f32(512)
            var = A.f32(512)
            rsd = A.f32(512)
            vlast = A.f32(480)
            vlT = A.f32(128)

            uT_toks = [("uT", c) for c in range(NCH)]
            rot["list"] = [0, 1, 2, 3]
            rot["i"] = 0
            S1, S2 = 4, 5
            wslots = {}
            cbank = {}

            def stA1(blk):
                vb, j = blk // 4, blk % 4
                if j == 0:
                    wslots[vb] = (w_take(), w_take())
                (iv, sv), (ig, sg_) = wslots[vb]
                q = blk % 2
                bv, bg = bank(), bank()
                for (bk, slot) in ((bv, sv), (bg, sg_)):
                    def mm(h, bk=bk, slot=slot):
                        for k in range(8):
                            ins = h.matmul(out=pf[bk][:, :], lhsT=wb3[slot][:, k, j * 128:(j + 1) * 128], rhs=uT3[:, k, :],
                                           start=(k == 0), stop=(k == 7))
                        return ins
                    P.add("pe", mm, r=uT_toks + [("wbuf", slot)], w=[("pf", bk)])
                P.add("act", lambda h: h.activation(
                    out=sgt[q], in_=pf[bg][:, :], func=AF.Sigmoid, bias=bin_sb[:, 16 + blk:17 + blk]),
                    r=[("pf", bg), "params"], w=[("sgt", q)])
                P.add("dve", lambda h: h.scalar_tensor_tensor(
                    out=vst[q][:, 30:542], in0=pf[bv][:, :], scalar=bin_sb[:, blk:blk + 1], in1=sgt[q],
                    op0=ALU.add, op1=ALU.mult),
                    r=[("pf", bv), ("sgt", q), "params"], w=[("vst", q, 1)])
                if last:
                    P.add("dve", lambda h: h.scalar_tensor_tensor(
                        out=vlast.rearrange("p (k b) -> p k b", k=30)[:, :, blk], in0=pf[bv][:, 482:512],
                        scalar=bin_sb[:, blk:blk + 1], in1=sgt[q][:, 482:512], op0=ALU.add, op1=ALU.mult),
                        r=[("pf", bv), ("sgt", q), "params"], w=[("vlast", blk)])
                P.add("pool", lambda h: h.tensor_copy(out=vst[q][:, 0:30], in_=vhalo3[:, blk, :]),
                      r=[("vhalo", blk)], w=[("vst", q, 0)])
                P.add("pool", lambda h: h.tensor_copy(out=vhalo3[:, blk, :], in_=vst[q][:, 512:542]),
                      r=[("vst", q, 0), ("vst", q, 1)], w=[("vhalo", blk)])
                if j == 3:
                    w_done(iv)
                    w_done(ig)

            def stB1(blk):
                q = blk % 2
                bc = bank()
                cbank[blk] = bc
                npe = CK - NT_DVE

                def mmc(h):
                    for k in range(npe):
                        for i4 in range(4):
                            ps_ = slice(32 * i4, 32 * i4 + 32)
                            ins = h.matmul(out=pf[bc][ps_, :], lhsT=dg32[ps_, blk, k, :], rhs=vst[q][ps_, k:k + 512],
                                           start=(k == 0), stop=(k == npe - 1), tile_position=(32 * i4, 32 * i4))
                    return ins
                P.add("pe", mmc, r=[("dg32", (blk // 4) * 4), ("vst", q, 0), ("vst", q, 1)], w=[("pf", bc)])
                for n_, k in enumerate(range(npe, CK)):
                    a_ = cacc[q][n_ % 2]
                    if n_ < 2:
                        P.add("dve", lambda h, k=k, a_=a_: h.tensor_scalar(out=a_, in0=vst[q][:, k:k + 512],
                                                                          scalar1=ccw3[:, k, blk:blk + 1], scalar2=None, op0=ALU.mult),
                              r=[("vst", q, 0), ("vst", q, 1), "params"], w=[("cacc", q, n_ % 2)])
                    else:
                        P.add("dve", lambda h, k=k, a_=a_: h.scalar_tensor_tensor(out=a_, in0=vst[q][:, k:k + 512],
                                                                                 scalar=ccw3[:, k, blk:blk + 1], in1=a_,
                                                                                 op0=ALU.mult, op1=ALU.add),
                              r=[("vst", q, 0), ("vst", q, 1), ("cacc", q, n_ % 2), "params"], w=[("cacc", q, n_ % 2)])

            def stB1b(blk):
                q = blk % 2
                bc = cbank[blk]
                if NT_DVE > 0:
                    P.add("dve", lambda h: h.scalar_tensor_tensor(out=cacc[q][0], in0=pf[bc][:, :], scalar=ccb_sb[:, blk:blk + 1],
                                                                  in1=cacc[q][0], op0=ALU.add, op1=ALU.add),
                          r=[("pf", bc), ("cacc", q, 0), "params"], w=[("cacc", q, 0)])
                    P.add("dve", lambda h: h.tensor_tensor(out=cbuf[:, blk, :], in0=cacc[q][0], in1=cacc[q][1], op=ALU.add),
                          r=[("cacc", q, 0), ("cacc", q, 1)], w=[("cbuf", blk)])
                    P.add("act", lambda h: h.activation(out=cb16[q], in_=cbuf[:, blk, :], func=AF.Copy),
                          r=[("cbuf", blk)], w=[("cb16", q)])
                    P.add("act", lambda h: h.activation(out=csq[q], in_=cbuf[:, blk, :], func=AF.Square),
                          r=[("cbuf", blk)], w=[("csq", q)])
                else:
                    P.add("act", lambda h: h.activation(out=cb16[q], in_=pf[bc][:, :], func=AF.Identity, bias=ccb_sb[:, blk:blk + 1]),
                          r=[("pf", bc), "params"], w=[("cb16", q)])
                    P.add("act", lambda h: h.activation(out=csq[q], in_=pf[bc][:, :], func=AF.Square, bias=ccb_sb[:, blk:blk + 1]),
                          r=[("pf", bc), "params"], w=[("csq", q)])
                    P.add("act", lambda h: h.activation(out=cbuf[:, blk, :], in_=pf[bc][:, :], func=AF.Identity,
                                                        bias=ccb_sb[:, blk:blk + 1]),
                          r=[("pf", bc), "params"], w=[("cbuf", blk)])

            def stC1(blk):
                q = blk % 2
                P.add("pe", lambda h: h.matmul(out=pf[S1][:, :], lhsT=negm_b, rhs=cb16[q], start=(blk == 0), stop=(blk == 15)),
                      r=[("cb16", q), "negm_b"], w=[("pf", S1)])
                P.add("pe", lambda h: h.matmul(out=pf[S2][:, :], lhsT=ones_b, rhs=csq[q], start=(blk == 0), stop=(blk == 15)),
                      r=[("csq", q), "ones_b"], w=[("pf", S2)])

            stA1(0)
            for t in range(16 + 1):
                if 0 <= t - 1 < 16:
                    stC1(t - 1)
                if t < 16:
                    stB1(t)
                    if NT_DVE == 0:
                        stB1b(t)
                if t + 1 < 16:
                    stA1(t + 1)
                if t < 16 and NT_DVE > 0:
                    stB1b(t)
            P.add("dve", lambda h: h.tensor_copy(out=mean, in_=pf[S1][:, :]), r=[("pf", S1)], w=["mean"])
            P.add("dve", lambda h: h.tensor_tensor(out=var, in0=mean, in1=mean, op=ALU.mult), r=["mean"], w=["var"])
            P.add("dve", lambda h: h.scalar_tensor_tensor(out=var, in0=pf[S2][:, :], scalar=1.0 / CW, in1=var,
                                                          op0=ALU.mult, op1=ALU.subtract),
                  r=[("pf", S2), "var"], w=["var"])
            P.add("dve", lambda h: h.tensor_scalar(out=var, in0=var, scalar1=0.0, scalar2=EPS, op0=ALU.max, op1=ALU.add),
                  r=["var"], w=["var"])
            P.add("act", lambda h: h.activation(out=rsd, in_=var, func=AF.Ln), r=["var"], w=["rsd"])
            P.add("act", lambda h: h.activation(out=rsd, in_=rsd, func=AF.Exp, scale=-0.5), r=["rsd"], w=["rsd"])
            rot["list"] = [0, 1, 2, 3, 5]
            if last:
                for r0 in range(0, 480, 120):
                    b = bank()
                    P.add("pe", lambda h, b=b, r0=r0: h.transpose(out=pf[b][0:120, 0:128], in_=vlast[:, r0:r0 + 120],
                                                                  identity=ident_f),
                          r=[("vlast", k) for k in range(16)] + ["ident_f"], w=[("pf", b)])
                    P.add("dve", lambda h, b=b: h.tensor_copy(out=vlT[0:120, :], in_=pf[b][0:120, 0:128]),
                          r=[("pf", b)], w=["vlT"])
                    P.add("sp", lambda h, r0=r0: h.dma_start(out=ccfm_p[r0:r0 + 120, :], in_=vlT[0:120, :]),
                          r=["vlT"], w=["vlT_d"], dma="o_ccfm", out=True)
                    P.readers.setdefault("vlT", []).append(P.prev)
            for zb in range(4):
                i, slot = w_take()
                for j in range(4):
                    blk = zb * 4 + j
                    q = blk % 2
                    b = bank()

                    def mm(h, b=b, slot=slot, j=j):
                        for k in range(8):
                            ins = h.matmul(out=pf[b][:, :], lhsT=wb3[slot][:, k, j * 128:(j + 1) * 128], rhs=uT3[:, k, :],
                                           start=(k == 0), stop=(k == 7))
                        return ins
                    P.add("pe", mm, r=uT_toks + [("wbuf", slot)], w=[("pf", b)])
                    P.add("act", lambda h, b=b, q=q, blk=blk: h.activation(out=szt[q], in_=pf[b][:, :], func=AF.Silu,
                                                                         bias=bin_sb[:, 32 + blk:33 + blk]),
                          r=[("pf", b), "params"], w=[("szt", q)])
                    P.add("dve", lambda h, q=q, blk=blk: h.tensor_tensor(out=t1[q], in0=cbuf[:, blk, :], in1=pf[S1][:, :], op=ALU.add),
                          r=[("cbuf", blk), ("pf", S1)], w=[("t1", q)])
                    P.add("dve", lambda h, q=q: h.tensor_tensor(out=t1[q], in0=t1[q], in1=rsd, op=ALU.mult),
                          r=[("t1", q), "rsd"], w=[("t1", q)])
                    P.add("act", lambda h, q=q, blk=blk: h.activation(out=t1[q], in_=t1[q], func=AF.Silu,
                                                                    scale=lng_sb[:, blk:blk + 1], bias=lnb_sb[:, blk:blk + 1]),
                          r=[("t1", q), "params"], w=[("t1", q)])
                    P.add("pool", lambda h, q=q, blk=blk: h.tensor_tensor(out=aT[:, blk, :], in0=t1[q], in1=szt[q], op=ALU.mult),
                          r=[("t1", q), ("szt", q)], w=[("aT", blk)])
                w_done(i)
            rot["list"] = [0, 1, 2, 3, 4, 5]
            tok0 = G * GT

            def after1(c):
                P.add("sp", lambda h, c=c: h.dma_start(out=yp[tok0 + c * 128: tok0 + (c + 1) * 128, :], in_=xres3[:, c, :]),
                      r=[("xres", c)], w=[("yp_d", c)], dma=("o_yp", c), out=True)
                if G + 1 < NG:
                    t1_ = (G + 1) * GT + c * 128
                    prenorm_p1(0, c, load_from=xp[t1_:t1_ + 128, :])

            def after2(c):
                if G + 1 < NG:
                    prenorm_p2(0, c)
            outproj(1, lambda c, kc: aT[:, kc, c * 128:(c + 1) * 128], lambda c: [("aT", k) for k in range(16)], A,
                    after1=after1, after2=after2)


        def tt(eng, out, in0, in1, op, r, w):
            P.add(eng, lambda h: h.tensor_tensor(out=out, in0=in0, in1=in1, op=op), r=r, w=w)

        def act(out, in_, func, r, w, **kw):
            P.add("act", lambda h: h.activation(out=out, in_=in_, func=func, **kw), r=r, w=w)

        def s_prenorm(layer, A):
            xn = A.bf16(D)
            sq = A.f32(8)
            act(xn[0:16, :], xs_sb[0:16, :], AF.Square, ["xs_sb"], ["s_xn", "s_sq"], accum_out=sq[0:16, 0:1])
            P.add("dve", lambda h: h.tensor_scalar(out=sq[0:16, 1:2], in0=sq[0:16, 0:1], scalar1=1.0 / D, scalar2=EPS,
                                                   op0=ALU.mult, op1=ALU.add), r=["s_sq"], w=["s_sq"])
            tt("pool", sq[0:16, 1:2], sq[0:16, 1:2], nhalf[0:16, 0:1], ALU.pow, ["s_sq", "nhalf"], ["s_sq"])
            act(xn[0:16, :], xs_sb[0:16, :], AF.Copy, ["xs_sb", "s_sq"], ["s_xn"], scale=sq[0:16, 1:2])

            def tr(h):
                for k in range(8):
                    ins = h.transpose(out=pb[0][:, k * 16:(k + 1) * 16], in_=xn[0:16, k * 128:(k + 1) * 128],
                                      identity=ident_b[0:16, 0:16])
                return ins
            P.add("pe", tr, r=["s_xn", "ident_b"], w=[("pb", 0)])
            tt("dve", usT3, pb[0][:, 0:128].rearrange("p (k s) -> p k s", k=8),
               gpre[:, layer * 8:(layer + 1) * 8].unsqueeze(2).to_broadcast([128, 8, NS]), ALU.mult,
               [("pb", 0), "params"], ["usT"])

        def s_inproj_block(dst3, blk0, nsub=4):
            i, slot = w_take()
            b = bank()

            def mm(h):
                for j in range(nsub):
                    for k in range(8):
                        ins = h.matmul(out=pf[b][:, j * NS:(j + 1) * NS], lhsT=wb3[slot][:, k, j * 128:(j + 1) * 128],
                                       rhs=usT3[:, k, :], start=(k == 0), stop=(k == 7))
                return ins
            P.add("pe", mm, r=["usT", ("wbuf", slot)], w=[("pf", b)])
            P.add("dve", lambda h: h.tensor_copy(out=dst3[:, blk0:blk0 + nsub, :],
                                                 in_=pf[b][:, 0:nsub * NS].rearrange("p (j s) -> p j s", j=nsub)),
                  r=[("pf", b)], w=["s_proj"])
            w_done(i)

        def s_outproj(layer, aT3, A):
            otmp = A.f32(D)
            s2 = A.f32(8)
            w_issue_upto(wstate["next"] + 4)
            blks = [w_take() for _ in range(4)]
            bks = [bank(), bank()]
            for nh in range(2):
                def mm(h, nh=nh, b=bks[nh]):
                    for kh in range(2):
                        slot = blks[nh * 2 + kh][1]
                        for k in range(8):
                            ins = h.matmul(out=pf[b][0:16, :], lhsT=aT3[:, kh * 8 + k, :], rhs=wb3[slot][:, k, :],
                                           start=(kh == 0 and k == 0), stop=(kh == 1 and k == 7))
                    return ins
                P.add("pe", mm, r=["s_aT", ("wbuf", blks[nh * 2][1]), ("wbuf", blks[nh * 2 + 1][1])], w=[("pf", bks[nh])])
                act(otmp[0:16, nh * 512:(nh + 1) * 512], pf[bks[nh]][0:16, :], AF.Square, [("pf", bks[nh])],
                    [("s_otmp", nh), ("s_s2", nh)], accum_out=s2[0:16, nh:nh + 1])
            tt("dve", s2[0:16, 2:3], s2[0:16, 0:1], s2[0:16, 1:2], ALU.add, [("s_s2", 0), ("s_s2", 1)], [("s_s2", 2)])
            P.add("dve", lambda h: h.tensor_scalar(out=s2[0:16, 3:4], in0=s2[0:16, 2:3], scalar1=1.0 / D, scalar2=EPS,
                                                   op0=ALU.mult, op1=ALU.add), r=[("s_s2", 2)], w=[("s_s2", 3)])
            tt("pool", s2[0:16, 3:4], s2[0:16, 3:4], nhalf[0:16, 0:1], ALU.pow, [("s_s2", 3), "nhalf"], [("s_s2", 3)])
            for nh in range(2):
                P.add("dve", lambda h, nh=nh, b=bks[nh]: h.scalar_tensor_tensor(
                    out=otmp[0:16, nh * 512:(nh + 1) * 512], in0=pf[b][0:16, :], scalar=s2[0:16, 3:4],
                    in1=gpost_bc[0:16, nh * 512:(nh + 1) * 512], op0=ALU.mult, op1=ALU.mult),
                    r=[("pf", bks[nh]), ("s_s2", 3), "gpost"], w=[("s_otmp", nh)])
                tt("dve", xs_sb[0:16, nh * 512:(nh + 1) * 512], xs_sb[0:16, nh * 512:(nh + 1) * 512],
                   otmp[0:16, nh * 512:(nh + 1) * 512], ALU.add, [("s_otmp", nh), "xs_sb"], ["xs_sb"])
            for (i, s_) in blks:
                w_done(i)

        def s_colstats(src2d, ncols, A, tag):
            b = bank()
            P.add("pe", lambda h: h.matmul(out=pf[b][:, 0:ncols], lhsT=ones_f, rhs=src2d, start=True, stop=True),
                  r=[tag, "ones_f"], w=[("pf", b)])
            return b

        def sample_layer0():
            PH.reset()
            PH.limit = ARENA_WORDS - D
            A = PH
            load_gpost(0)
            P.add("sp", lambda h: h.dma_start(out=xs_sb[0:16, :], in_=xs_d[:, :]), w=["xs_sb"], dma="s_ld")
            s_prenorm(0, A)
            zxT = A.f32(40 * NS).rearrange("p (b s) -> p b s", b=40)
            dexp = A.f32(16 * 33).rearrange("p (t c) -> p t c", t=16)
            xc = A.f32(24 * NS).rearrange("p (b s) -> p b s", b=24)
            BCtok = A.f32(1024)
            sel = A.f32(NS * 128)
            xdtT = A.f32(16 * NS).rearrange("p (t s) -> p t s", t=16)
            yT = A.f32(16 * NS).rearrange("p (t s) -> p t s", t=16)
            s_mark = PH.cur
            s_inproj_block(zxT, 0)
            s_inproj_block(zxT, 32)
            s_inproj_block(zxT, 36)
            for g in range(4):
                s_inproj_block(zxT, 16 + g * 4)
                if g + 1 < 4:
                    s_inproj_block(zxT, (g + 1) * 4)
            dtT = A.f32(2 * NS + 1)
            colp = A.f32(4)
            P.add("sp", lambda h: h.dma_start(out=colp[0:NH, 0:1], in_=dtb.rearrange("o h -> h o")), w=["s_colp0"], dma="s_ld2")
            P.add("sp", lambda h: h.dma_start(out=colp[0:NH, 1:2], in_=alog.rearrange("o h -> h o")), w=["s_colp1"], dma="s_ld3")
            P.add("sp", lambda h: h.dma_start(out=dtT[0:NH, 32:33], in_=dsk.rearrange("o h -> h o")), w=["s_dcol"], dma="s_ld4")
            act(colp[0:NH, 1:2], colp[0:NH, 1:2], AF.Exp, ["s_colp1"], ["s_colp1"])
            P.add("dve", lambda h: h.tensor_scalar(out=colp[0:NH, 1:2], in0=colp[0:NH, 1:2], scalar1=-1.0, scalar2=None,
                                                   op0=ALU.mult), r=["s_colp1"], w=["s_colp1"])
            b = bank()

            def mmdt(h, b=b):
                for k in range(8):
                    ins = h.matmul(out=pf[b][0:NH, 0:NS], lhsT=wdt3[:, k, :], rhs=usT3[:, k, :], start=(k == 0), stop=(k == 7))
                return ins
            P.add("pe", mmdt, r=["usT", "wdt"], w=[("pf", b)])
            if P.dbg == "s0":
                rawd = A.f32(NS)
                P.add("dve", lambda h, b=b: h.tensor_copy(out=rawd[0:NH, :], in_=pf[b][0:NH, 0:NS]), r=[("pf", b)], w=["s_rawd"])
                P.dump("rawd", rawd[0:NH, :], NS, ["s_rawd"], parts=NH)
            act(dtT[0:NH, 0:NS], pf[b][0:NH, 0:NS], AF.Exp, [("pf", b), "s_colp0"], ["s_dtT"], bias=colp[0:NH, 0:1])
            act(dtT[0:NH, 0:NS], dtT[0:NH, 0:NS], AF.Ln, ["s_dtT"], ["s_dtT"], bias=1.0)
            act(dtT[0:NH, NS:2 * NS], dtT[0:NH, 0:NS], AF.Exp, ["s_dtT", "s_colp1"], ["s_dAT"], scale=colp[0:NH, 1:2])
            Eexp = A.f32(HP)
            P.add("sp", lambda h: h.dma_start(out=Eexp[0:NH, :], in_=c_exp[:, :]), w=["s_E"], dma="s_ld5")
            for half in range(2):
                b = bank()

                def mme(h, b=b, half=half):
                    for tl in range(8):
                        t = half * 8 + tl
                        ins = h.matmul(out=pf[b][:, tl * 33:(tl + 1) * 33], lhsT=Eexp[0:NH, t * 128:(t + 1) * 128],
                                       rhs=dtT[0:NH, 0:33], start=True, stop=True)
                    return ins
                P.add("pe", mme, r=["s_E", "s_dtT", "s_dAT", "s_dcol"], w=[("pf", b)])
                P.add("dve", lambda h, b=b, half=half: h.tensor_copy(
                    out=dexp[:, half * 8:(half + 1) * 8, :], in_=pf[b][:, 0:8 * 33].rearrange("p (t c) -> p t c", t=8)),
                    r=[("pf", b)], w=[("s_dexp", half)])
            dexp_t = [("s_dexp", 0), ("s_dexp", 1)]
            cst = A.f32(CONVD)
            P.add("sp", lambda h: h.dma_start(out=cst[0:48, :], in_=st_cssm[:, :]), w=["s_cst"], dma="s_ld6")
            bufT = A.f32(24 * 48).rearrange("p (b r) -> p b r", b=24)
            for b0 in range(0, 24, 8):
                b = bank()

                def trc(h, b=b, b0=b0):
                    for j in range(8):
                        ins = h.transpose(out=pf[b][:, j * 48:(j + 1) * 48], in_=cst[0:48, (b0 + j) * 128:(b0 + j + 1) * 128],
                                          identity=ident_f[0:48, 0:48])
                    return ins
                P.add("pe", trc, r=["s_cst", "ident_f"], w=[("pf", b)])
                P.add("dve", lambda h, b=b, b0=b0: h.tensor_copy(out=bufT[:, b0:b0 + 8, :],
                                                              in_=pf[b][:, 0:8 * 48].rearrange("p (j r) -> p j r", j=8)),
                      r=[("pf", b)], w=[("s_bufT", b0)])
            bufT_t = [("s_bufT", b0) for b0 in (0, 8, 16)]
            P.add("sp", lambda h: h.dma_start(out=cssm_s.rearrange("(s k) c -> s k c", k=3)[:, 0:2, :],
                                              in_=st_cssm.rearrange("(s k) c -> s k c", k=3)[:, 1:3, :]),
                  dma="o_cssm_s0", out=True)
            xbcT = zxT[:, 16:40, :]
            prod = A.f32(24 * 48)
            cwb = cw_sb.rearrange("p (k b) -> p b k", k=4)
            tt("dve", prod.rearrange("p (b s k) -> p b s k", b=24, s=NS), bufT.rearrange("p b (s k) -> p b s k", s=NS),
               cwb[:, :, 0:3].unsqueeze(2).to_broadcast([128, 24, NS, 3]), ALU.mult, bufT_t + ["params"], ["s_prod"])
            P.add("dve", lambda h: h.tensor_reduce(out=xc, in_=prod.rearrange("p (b s k) -> p b s k", b=24, s=NS),
                                                   axis=AX.X, op=ALU.add), r=["s_prod"], w=["s_xc"])
            tmp24 = A.f32(24 * NS).rearrange("p (b s) -> p b s", b=24)
            tt("dve", tmp24, xbcT, cwb[:, :, 3:4].to_broadcast([128, 24, NS]), ALU.mult, ["s_proj", "params"], ["s_tmp24"])
            tt("dve", xc, xc, tmp24, ALU.add, ["s_xc", "s_tmp24"], ["s_xc"])
            tt("dve", xc, xc, cb_sb.unsqueeze(2).to_broadcast([128, 24, NS]), ALU.add, ["s_xc", "params"], ["s_xc"])
            act(xc, xc, AF.Silu, ["s_xc"], ["s_xc"])
            xrow = A.f32(CONVD)
            for b0 in range(0, 24, 4):
                b = bank()

                def trr(h, b=b, b0=b0):
                    for j in range(4):
                        ins = h.transpose(out=pf[b][0:16, j * 128:(j + 1) * 128], in_=xbcT[:, b0 + j, :], identity=ident_f)
                    return ins
                P.add("pe", trr, r=["s_proj", "ident_f"], w=[("pf", b)])
                P.add("dve", lambda h, b=b, b0=b0: h.tensor_copy(out=xrow[0:16, b0 * 128:(b0 + 4) * 128], in_=pf[b][0:16, :]),
                      r=[("pf", b)], w=[("s_xrow", b0)])
            P.add("sp", lambda h: h.dma_start(out=cssm_s.rearrange("(s k) c -> s k c", k=3)[:, 2, :], in_=xrow[0:16, :]),
                  r=[("s_xrow", b0) for b0 in range(0, 24, 4)], dma="o_cssm_s1", out=True)
            for which, c0 in ((0, 16), (1, 20)):
                b = bank()

                def trb(h, b=b, c0=c0):
                    for j in range(4):
                        ins = h.transpose(out=pf[b][0:16, j * 128:(j + 1) * 128], in_=xc[:, c0 + j, :], identity=ident_f)
                    return ins
                P.add("pe", trb, r=["s_xc", "ident_f"], w=[("pf", b)])
                P.add("dve", lambda h, b=b, which=which: h.tensor_copy(out=BCtok[0:16, which * 512:(which + 1) * 512],
                                                                    in_=pf[b][0:16, :]),
                      r=[("pf", b)], w=[("s_BC", which)])
            P.add("dve", lambda h: h.tensor_copy(out=sel[0:16, :].rearrange("p (s m) -> p s m", s=NS),
                                                 in_=ident_f[0:16, 0:16].unsqueeze(2).to_broadcast([16, NS, 128])),
                  r=["ident_f"], w=["s_sel"])
            xT = xc[:, 0:16, :]
            tt("dve", xdtT, xT, dexp[:, :, 0:NS], ALU.mult, ["s_xc"] + dexp_t, ["s_xdtT"])
            P.fence()
            PH.cur = s_mark
            h0 = [A.f32(HP) for _ in range(2)]
            hn = [A.f32(HP) for _ in range(2)]
            tmpe = {"dve": A.f32(HP), "pool": A.f32(HP)}
            tmpc = A.f32(HP)
            bcs = [A.f32(1024) for _ in range(2)]
            def s_partB(s_):
                q = s_ % 2
                tt("dve", tmpc.rearrange("p (g j n) -> p g j n", g=4, j=4), hn[q].rearrange("p (g j n) -> p g j n", g=4, j=4),
                   bcs[q][:, 512:1024].rearrange("p (g n) -> p g n", g=4).unsqueeze(2).to_broadcast([128, 4, 4, 128]), ALU.mult,
                   [("s_bcs", q, 1), ("s_hn", q)], ["s_tmpc"])
                P.add("dve", lambda h: h.tensor_reduce(out=yT[:, :, s_], in_=tmpc.rearrange("p (t n) -> p t n", t=16),
                                                       axis=AX.X, op=ALU.add),
                      r=["s_tmpc"], w=[("s_yT", s_)])

            for s_ in range(NS):
                q = s_ % 2
                eng = "dve"
                tmpb = tmpe["dve" if s_ % 2 == 0 else "pool"]
                h03 = h0[q].rearrange("p (t n) -> p t n", t=16)
                hn3 = hn[q].rearrange("p (t n) -> p t n", t=16)
                tm3 = tmpb.rearrange("p (t n) -> p t n", t=16)
                P.add("sp", lambda h, s_=s_, h03=h03: h.dma_start(out=h03, in_=st_ssm[s_].rearrange("(t p) n -> p t n", p=128)),
                      w=[("s_h0", q)], dma=("s_h0", q))
                bB, bC = bank(), bank()
                P.add("pe", lambda h, s_=s_, bB=bB: h.matmul(out=pf[bB][:, :], lhsT=sel[0:16, s_ * 128:(s_ + 1) * 128],
                                                            rhs=BCtok[0:16, 0:512], start=True, stop=True),
                      r=["s_sel", ("s_BC", 0)], w=[("pf", bB)])
                P.add("pe", lambda h, s_=s_, bC=bC: h.matmul(out=pf[bC][:, :], lhsT=sel[0:16, s_ * 128:(s_ + 1) * 128],
                                                            rhs=BCtok[0:16, 512:1024], start=True, stop=True),
                      r=["s_sel", ("s_BC", 1)], w=[("pf", bC)])
                act(bcs[q][:, 0:512], pf[bB][:, :], AF.Copy, [("pf", bB)], [("s_bcs", q, 0)])
                act(bcs[q][:, 512:1024], pf[bC][:, :], AF.Copy, [("pf", bC)], [("s_bcs", q, 1)])
                tt("pool", hn3, h03, dexp[:, :, NS + s_:NS + s_ + 1].to_broadcast([128, 16, 128]), ALU.mult,
                   [("s_h0", q)] + dexp_t, [("s_hn", q)])
                tt(eng, tmpb.rearrange("p (g j n) -> p g j n", g=4, j=4),
                   bcs[q][:, 0:512].rearrange("p (g n) -> p g n", g=4).unsqueeze(2).to_broadcast([128, 4, 4, 128]),
                   xdtT[:, :, s_:s_ + 1].rearrange("p (g j) o -> p g j o", g=4).to_broadcast([128, 4, 4, 128]), ALU.mult,
                   [("s_bcs", q, 0), "s_xdtT"], [("s_tmp", q)])
                tt("pool", hn3, hn3, tm3, ALU.add, [("s_hn", q), ("s_tmp", q)], [("s_hn", q)])
                P.add("act", lambda h, s_=s_, hn3=hn3: h.dma_start(out=ssm_s[s_].rearrange("(t p) n -> p t n", p=128), in_=hn3),
                      r=[("s_hn", q)], dma=("o_ssm_s", q), out=True)
                if s_ >= 1:
                    s_partB(s_ - 1)
            s_partB(NS - 1)
            yT_all = [("s_yT", s_) for s_ in range(NS)]
            P.add("dve", lambda h: h.tensor_copy(out=yT[:, 0:1, 0:1], in_=yT[:, 0:1, 0:1]), r=yT_all, w=["s_yT"])
            t16 = A.f32(16 * NS).rearrange("p (t s) -> p t s", t=16)
            tt("dve", t16, xT, dexp[:, :, 32:33].to_broadcast([128, 16, NS]), ALU.mult, ["s_xc"] + dexp_t, ["s_t16"])
            tt("dve", yT, yT, t16, ALU.add, ["s_yT", "s_t16"], ["s_yT"])
            zT = zxT[:, 0:16, :]
            act(t16, zT, AF.Silu, ["s_proj", "s_t16"], ["s_t16"])
            tt("dve", yT, yT, t16, ALU.mult, ["s_yT", "s_t16"], ["s_yT"])
            tt("dve", t16, yT, yT, ALU.mult, ["s_yT"], ["s_t16"])
            b = s_colstats(t16.rearrange("p t s -> p (t s)"), 16 * NS, A, "s_t16")
            rs = A.f32(NS)
            P.add("dve", lambda h, b=b: h.tensor_reduce(out=rs, in_=pf[b][:, 0:16 * NS].rearrange("p (t s) -> p s t", t=16),
                                                   axis=AX.X, op=ALU.add), r=[("pf", b)], w=["s_rs"])
            P.add("dve", lambda h: h.tensor_scalar(out=rs, in0=rs, scalar1=1.0 / HP, scalar2=EPS, op0=ALU.mult, op1=ALU.add),
                  r=["s_rs"], w=["s_rs"])
            tt("pool", rs, rs, nhalf[:, 0:1].to_broadcast([128, NS]), ALU.pow, ["s_rs", "nhalf"], ["s_rs"])
            tt("dve", yT, yT, rs.unsqueeze(1).to_broadcast([128, 16, NS]), ALU.mult, ["s_yT", "s_rs"], ["s_yT"])
            aT = A.bf16(16 * NS).rearrange("p (t s) -> p t s", t=16)
            tt("dve", aT, yT, ngs.unsqueeze(2).to_broadcast([128, 16, NS]), ALU.mult, ["s_yT", "params"], ["s_aT"])
            if P.dbg == "s0":
                P.fence()
                P.dump("dtT", dtT[0:NH, :], 33, ["s_dtT", "s_dAT", "s_dcol"], parts=NH)
                P.dump("dexp", dexp.rearrange("p t c -> p (t c)"), 16 * 33, dexp_t)
                P.dump("zxT", zxT.rearrange("p b s -> p (b s)"), 40 * NS, ["s_proj"])
                P.dump("xc", xc.rearrange("p b s -> p (b s)"), 24 * NS, ["s_xc"])
                P.dump("yT", yT.rearrange("p t s -> p (t s)"), 16 * NS, ["s_yT"])
                P.dump("BCtok", BCtok[0:16, :], 1024, [("s_BC", 0), ("s_BC", 1)], parts=16)
                P.dump("usT", usT, 8 * NS, ["usT"])
                P.fence()
            s_outproj(0, aT, A)
            if P.dbg == "s0":
                P.fence()
                P.dump("xs1", xs_sb[0:16, :], D, ["xs_sb"], parts=16)

        def sample_layer1():
            PH.reset()
            A = PH
            load_gpost(1)
            s_prenorm(1, A)
            vgz = A.f32(48 * NS).rearrange("p (b s) -> p b s", b=48)
            for vb in range(4):
                s_inproj_block(vgz, vb * 4)
                s_inproj_block(vgz, 16 + vb * 4)
            for zb in range(4):
                s_inproj_block(vgz, 32 + zb * 4)
            tt("dve", vgz, vgz, bin_sb.unsqueeze(2).to_broadcast([128, 48, NS]), ALU.add, ["s_proj", "params"], ["s_proj"])
            sg = A.f32(16 * NS).rearrange("p (b s) -> p b s", b=16)
            act(sg, vgz[:, 16:32, :], AF.Sigmoid, ["s_proj"], ["s_sg"])
            vT = A.f32(16 * NS).rearrange("p (b s) -> p b s", b=16)
            tt("dve", vT, vgz[:, 0:16, :], sg, ALU.mult, ["s_proj", "s_sg"], ["s_vT"])
            bufT = A.f32(16 * NS * 30).rearrange("p (b s k) -> p b s k", b=16, s=NS)
            cst = [A.f32(CW) for _ in range(2)]
            for sg4 in range(4):
                q = sg4 % 2
                P.add("sp", lambda h, sg4=sg4, q=q: h.dma_start(out=cst[q][0:120, :], in_=st_ccfm[sg4 * 120:(sg4 + 1) * 120, :]),
                      w=[("s_cst1", q)], dma=("s_cst1", q))
                for b0 in range(0, 16, 4):
                    b = bank()

                    def trc(h, b=b, b0=b0, q=q):
                        for j in range(4):
                            ins = h.transpose(out=pf[b][:, j * 120:(j + 1) * 120], in_=cst[q][0:120, (b0 + j) * 128:(b0 + j + 1) * 128],
                                              identity=ident_f[0:120, 0:120])
                        return ins
                    P.add("pe", trc, r=[("s_cst1", q), "ident_f"], w=[("pf", b)])
                    P.add("dve", lambda h, b=b, b0=b0, sg4=sg4: h.tensor_copy(
                        out=bufT[:, b0:b0 + 4, sg4 * 4:(sg4 + 1) * 4, :],
                        in_=pf[b][:, 0:480].rearrange("p (j s k) -> p j s k", j=4, s=4)),
                        r=[("pf", b)], w=[("s_bufT1", sg4, b0)])
            bufT_t = [("s_bufT1", a, b0) for a in range(4) for b0 in range(0, 16, 4)]
            P.add("sp", lambda h: h.dma_start(out=ccfm_s.rearrange("(s k) c -> s k c", k=30)[:, 0:29, :],
                                              in_=st_ccfm.rearrange("(s k) c -> s k c", k=30)[:, 1:30, :]),
                  dma="o_ccfm_s0", out=True)
            vrow = A.f32(CW)
            for b0 in range(0, 16, 4):
                b = bank()

                def trr(h, b=b, b0=b0):
                    for j in range(4):
                        ins = h.transpose(out=pf[b][0:16, j * 128:(j + 1) * 128], in_=vT[:, b0 + j, :], identity=ident_f)
                    return ins
                P.add("pe", trr, r=["s_vT", "ident_f"], w=[("pf", b)])
                P.add("dve", lambda h, b=b, b0=b0: h.tensor_copy(out=vrow[0:16, b0 * 128:(b0 + 4) * 128], in_=pf[b][0:16, :]),
                      r=[("pf", b)], w=[("s_vrow", b0)])
            P.add("sp", lambda h: h.dma_start(out=ccfm_s.rearrange("(s k) c -> s k c", k=30)[:, 29, :], in_=vrow[0:16, :]),
                  r=[("s_vrow", b0) for b0 in range(0, 16, 4)], dma="o_ccfm_s1", out=True)
            ccwb = ccw_sb.rearrange("p (k b) -> p b k", k=CK)
            prod = A.f32(16 * NS * 30).rearrange("p (b s k) -> p b s k", b=16, s=NS)
            tt("dve", prod, bufT, ccwb[:, :, 0:30].unsqueeze(2).to_broadcast([128, 16, NS, 30]), ALU.mult,
               bufT_t + ["params"], ["s_prod1"])
            cT = A.f32(2 * 16 * NS)
            c3 = cT[:, 0:16 * NS].rearrange("p (b s) -> p b s", b=16)
            c2 = cT[:, 16 * NS:2 * 16 * NS].rearrange("p (b s) -> p b s", b=16)
            P.add("dve", lambda h: h.tensor_reduce(out=c3, in_=prod, axis=AX.X, op=ALU.add), r=["s_prod1"], w=["s_c"])
            t16 = A.f32(16 * NS).rearrange("p (b s) -> p b s", b=16)
            tt("dve", t16, vT, ccwb[:, :, 30:31].to_broadcast([128, 16, NS]), ALU.mult, ["s_vT", "params"], ["s_t16b"])
            tt("dve", c3, c3, t16, ALU.add, ["s_c", "s_t16b"], ["s_c"])
            tt("dve", c3, c3, ccb_sb.unsqueeze(2).to_broadcast([128, 16, NS]), ALU.add, ["s_c", "params"], ["s_c"])
            tt("dve", c2, c3, c3, ALU.mult, ["s_c"], ["s_c2"])
            b = s_colstats(cT, 2 * 16 * NS, A, "s_c2")
            st = A.f32(2 * NS)
            P.add("dve", lambda h, b=b: h.tensor_reduce(out=st.rearrange("p (j s) -> p j s", j=2),
                                                   in_=pf[b][:, 0:512].rearrange("p (j b s) -> p j s b", j=2, b=16),
                                                   axis=AX.X, op=ALU.add), r=[("pf", b)], w=["s_st"])
            P.add("dve", lambda h: h.tensor_scalar(out=st, in0=st, scalar1=1.0 / CW, scalar2=None, op0=ALU.mult),
                  r=["s_st"], w=["s_st"])
            msq = A.f32(NS)
            tt("dve", msq, st[:, 0:NS], st[:, 0:NS], ALU.mult, ["s_st"], ["s_msq"])
            tt("dve", msq, st[:, NS:2 * NS], msq, ALU.subtract, ["s_st", "s_msq"], ["s_msq"])
            P.add("dve", lambda h: h.tensor_scalar(out=msq, in0=msq, scalar1=0.0, scalar2=EPS, op0=ALU.max, op1=ALU.add),
                  r=["s_msq"], w=["s_msq"])
            tt("pool", msq, msq, nhalf[:, 0:1].to_broadcast([128, NS]), ALU.pow, ["s_msq", "nhalf"], ["s_msq"])
            tt("dve", c3, c3, st[:, 0:NS].unsqueeze(1).to_broadcast([128, 16, NS]), ALU.subtract, ["s_c", "s_st", "s_c2"], ["s_c"])
            tt("dve", c3, c3, msq.unsqueeze(1).to_broadcast([128, 16, NS]), ALU.mult, ["s_c", "s_msq"], ["s_c"])
            tt("dve", c3, c3, lng_sb.unsqueeze(2).to_broadcast([128, 16, NS]), ALU.mult, ["s_c", "params"], ["s_c"])
            tt("dve", c3, c3, lnb_sb.unsqueeze(2).to_broadcast([128, 16, NS]), ALU.add, ["s_c", "params"], ["s_c"])
            act(c3, c3, AF.Silu, ["s_c"], ["s_c"])
            act(t16, vgz[:, 32:48, :], AF.Silu, ["s_proj", "s_t16b"], ["s_t16b"])
            aT = A.bf16(16 * NS).rearrange("p (t s) -> p t s", t=16)
            tt("dve", aT, c3, t16, ALU.mult, ["s_c", "s_t16b"], ["s_aT"])
            s_outproj(1, aT, A)
            P.add("sp", lambda h: h.dma_start(out=ys[:, :], in_=xs_sb[0:16, :]), r=["xs_sb"], dma="o_ys", out=True)

        P.dbg = dbg
        if dbg:
            def dump(name, ap, n, toks, parts=128):
                d = nc.dram_tensor("dbg_" + name, [parts, n], F32, kind="ExternalOutput").ap()
                P.add("pool", lambda h: h.dma_start(out=d[:, :], in_=ap), r=toks, dma="dbg_" + name, out=True)
            P.dump = dump
        P.fence()
        for c in range(NCH):
            prenorm_chunk(0, c, load_from=xp[c * 128:(c + 1) * 128, :])
        if do_sample and dbg != "l0":
            sample_layer0()
            P.fence()
            if not dbg:
                sample_layer1()
                P.fence()
            PH.limit = ARENA_WORDS
        if dbg == "l0":
            dbg_out = {}

            def dump(name, ap, n, toks):
                d = nc.dram_tensor("dbg_" + name, [128, n], F32, kind="ExternalOutput").ap()
                P.add("pool", lambda h: h.dma_start(out=d[:, :], in_=ap), r=toks, dma="dbg_" + name, out=True)
            P.dump = dump
        for G in range(NG if not dbg else (1 if dbg == "l0" else 0)):
            layer0(G)
            P.fence()
            if dbg:
                break
            layer1(G)
            P.fence()

        PH.reset()
        if dbg:
            P.fence()
        fst = [PH.f32(128) for _ in range(2)]
        hT = PH.f32(128)
        b = bank()
        P.add("pe", lambda h, b=b: h.transpose(out=pf[b][0:72, 0:128], in_=halo[:, 0:72], identity=ident_f),
              r=[("halo", cb) for cb in range(24)] + ["ident_f"], w=[("pf", b)])
        P.add("dve", lambda h, b=b: h.tensor_copy(out=hT[0:72, :], in_=pf[b][0:72, 0:128]), r=[("pf", b)], w=["hT"])
        P.add("sp", lambda h: h.dma_start(out=cssm_p[:, :], in_=hT[0:72, :]), r=["hT"], dma="o_cssm", out=True)
        for t in range(16):
            q = t % 2
            b = bank()
            P.add("pe", lambda h, b=b, t=t: h.transpose(out=pf[b][:, 0:128], in_=prevT[:, t * 128:(t + 1) * 128], identity=ident_f),
                  r=[("prevT", t // 4), "ident_f"], w=[("pf", b)])
            P.add("dve", lambda h, b=b, q=q: h.tensor_copy(out=fst[q], in_=pf[b][:, 0:128]), r=[("pf", b)], w=[("fst", q)])
            P.add("sp", lambda h, t=t, q=q: h.dma_start(out=ssm_p[t * 128:(t + 1) * 128, :], in_=fst[q]),
                  r=[("fst", q)], w=[("fst_d", q)], dma=("o_ssm", q), out=True)
            P.readers.setdefault(("fst", q), []).append(P.prev)

        P.emit()
        info = {"stats": P.stats, "arena_peak": max(CA.peak, PH.peak)}
    return nc, info


_CACHE = {}


def _consts():
    ident = np.eye(128, dtype=np.float32)
    tri = np.triu(np.ones((128, 128), np.float32))
    U = np.tril(np.ones((128, 128), np.float32), -1)
    ones = np.ones((128, 128), np.float32)
    expm = np.zeros((NH, HP), np.float32)
    for h in range(NH):
        expm[h, h * PD:(h + 1) * PD] = 1.0
    blk = np.zeros((128, 32), np.float32)
    for p in range(128):
        blk[p, p % 32] = 1.0
    return ident, tri, U, ones, expm, blk


def kernel(x_prompt, x_sample, state_ssm, state_conv_ssm, state_conv_cfm, g_pre, g_post,
           ssm_w_in, ssm_conv_w, ssm_conv_b, ssm_dt_bias, ssm_A_log, ssm_D, ssm_norm_g, ssm_w_out,
           cfm_w_in, cfm_b_in, cfm_conv_w, cfm_conv_b, cfm_ln_g, cfm_ln_b, cfm_w_out):
    f = lambda a: np.ascontiguousarray(np.asarray(a, dtype=np.float32))
    if "nc" not in _CACHE:
        _CACHE["nc"] = build_program(do_sample=True)
    nc, info = _CACHE["nc"]
    ident, tri, U, ones, expm, blkm = _consts()
    x_prompt = f(x_prompt)
    x_sample = f(x_sample).reshape(128, D)
    state_ssm = f(state_ssm).reshape(128, HP, NST)
    state_conv_ssm = f(state_conv_ssm).reshape(128, 3, CONVD)
    state_conv_cfm = f(state_conv_cfm).reshape(128, 30, CW)
    shared = {
        "g_pre": f(g_pre).reshape(16, 128), "g_post": f(g_post).reshape(1, 2 * D),
        "w_in0": f(ssm_w_in).reshape(D, SSM_PROJ), "cw0": f(ssm_conv_w).reshape(96, 128),
        "cb0": f(ssm_conv_b).reshape(24, 128), "dtb": f(ssm_dt_bias).reshape(1, NH),
        "alog": f(ssm_A_log).reshape(1, NH), "dsk": f(ssm_D).reshape(1, NH),
        "ng0": f(ssm_norm_g).reshape(16, 128), "w_out0": f(ssm_w_out).reshape(HP, D),
        "w_in1": f(cfm_w_in).reshape(D, 3 * CW), "b_in1": f(cfm_b_in).reshape(48, 128),
        "ccw": f(cfm_conv_w).reshape(496, 128), "ccb": f(cfm_conv_b).reshape(16, 128),
        "lng": f(cfm_ln_g).reshape(16, 128), "lnb": f(cfm_ln_b).reshape(16, 128),
        "w_out1": f(cfm_w_out).reshape(CW, D),
        "c_ident": ident, "c_tri": tri, "c_U": U, "c_ones": ones, "c_exp": expm, "c_blk": blkm,
    }
    in_maps = []
    for i in range(8):
        m = dict(shared)
        m["xp"] = x_prompt[i]
        m["xs"] = x_sample[16 * i:16 * (i + 1)]
        m["st_ssm"] = state_ssm[16 * i:16 * (i + 1)]
        m["st_cssm"] = state_conv_ssm[16 * i:16 * (i + 1)].reshape(48, CONVD)
        m["st_ccfm"] = state_conv_cfm[16 * i:16 * (i + 1)].reshape(480, CW)
        in_maps.append(m)
    res = run_bass_kernel_spmd(nc, in_maps, core_ids=list(range(8)))
    R = res.results
    y_prompt = np.stack([R[i]["yp"] for i in range(8)]).reshape(8, SEQ, D)
    y_sample = np.concatenate([R[i]["ys"] for i in range(8)]).reshape(128, 1, D)
    new_ssm_p = np.stack([R[i]["ssm_p"] for i in range(8)]).reshape(1, 8, NH, PD, NST)
    new_cssm_p = np.stack([R[i]["cssm_p"] for i in range(8)]).reshape(1, 8, 3, CONVD)
    new_ccfm_p = np.stack([R[i]["ccfm_p"] for i in range(8)]).reshape(1, 8, 30, CW)
    new_ssm_s = np.concatenate([R[i]["ssm_s"] for i in range(8)]).reshape(1, 128, NH, PD, NST)
    new_cssm_s = np.concatenate([R[i]["cssm_s"] for i in range(8)]).reshape(1, 128, 3, CONVD)
    new_ccfm_s = np.concatenate([R[i]["ccfm_s"] for i in range(8)]).reshape(1, 128, 30, CW)
    return (y_prompt, y_sample, new_ssm_p, new_cssm_p, new_ccfm_p, new_ssm_s, new_cssm_s, new_ccfm_s)
```

```python
import numpy as np
from contextlib import ExitStack
import concourse.bass as bass
import concourse.mybir as mybir
from concourse.bass_utils import run_bass_kernel_spmd

F32 = mybir.dt.float32
BF16 = mybir.dt.bfloat16
AF = mybir.ActivationFunctionType
ALU = mybir.AluOpType
AX = mybir.AxisListType

ENGS = ("pe", "act", "dve", "pool", "sp")
EPS = 1e-6


class _Op:
    __slots__ = ("eng", "fn", "deps", "dma", "cum", "sig", "sigcount", "idx")


class Prog:
    def __init__(self, nc, es, serial=False, same_engine_sync=True):
        self.nc = nc
        self.es = es
        self.ops = {e: [] for e in ENGS}
        self.lastw = {}
        self.readers = {}
        self.dma_keys = {}
        self.serial = serial
        self.same_sync = same_engine_sync
        self.prev = None
        self.out_dmas = []
        self.fence_ops = []
        self.fenced = set()
        self.since_fence_dma = []

    def fence(self):
        f = []
        for e in ENGS:
            for op in reversed(self.ops[e]):
                if op.dma is None:
                    f.append(op)
                    break
        f.extend(self.since_fence_dma)
        self.since_fence_dma = []
        self.fence_ops = f
        self.fenced = set()

    def add(self, eng, fn, r=(), w=(), dma=None, out=False, nofence=False, nosame=False):
        op = _Op()
        op.eng = eng
        op.fn = fn
        op.dma = dma
        op.sig = False
        op.sigcount = 0
        op.cum = 0
        op.idx = len(self.ops[eng])
        deps = []
        raw = set()
        for t in r:
            if t in self.lastw:
                deps.append(self.lastw[t])
                raw.add(id(self.lastw[t]))
        for t in w:
            if t in self.lastw:
                deps.append(self.lastw[t])
            deps.extend(self.readers.get(t, ()))
        if self.serial and self.prev is not None:
            deps.append(self.prev)
            raw.add(id(self.prev))
        if eng not in self.fenced and not nofence and eng != "pe":
            deps.extend(self.fence_ops)
            self.fenced.add(eng)
        seen = set()
        ud = []
        for d in deps:
            if id(d) in seen or d is op:
                continue
            seen.add(id(d))
            if d.dma is None and d.eng == eng:
                if eng == "pe" or not self.same_sync or id(d) not in raw or nosame:
                    continue
            ud.append(d)
        op.deps = ud
        for d in ud:
            d.sig = True
        if dma is not None:
            self.dma_keys[dma] = self.dma_keys.get(dma, 0) + 1
            op.cum = self.dma_keys[dma]
            if not nofence:
                self.since_fence_dma.append(op)
        self.ops[eng].append(op)
        for t in w:
            self.lastw[t] = op
            self.readers[t] = []
        for t in r:
            self.readers.setdefault(t, []).append(op)
        self.prev = op
        if out:
            self.out_dmas.append(op)
        return op

    def emit(self):
        nc, es = self.nc, self.es
        sem_eng = {e: es.enter_context(nc.semaphore("sem_" + e)) for e in ENGS}
        sem_dma = {}
        for i, k in enumerate(self.dma_keys):
            sem_dma[k] = es.enter_context(nc.semaphore("semd_%d" % i))
        for e in ENGS:
            c = 0
            for op in self.ops[e]:
                if op.dma is None and op.sig:
                    c += 1
                    op.sigcount = c
        finals = {}
        for op in self.out_dmas:
            finals[op.dma] = max(finals.get(op.dma, 0), op.cum)
        block = es.enter_context(nc.Block())
        handles = {"pe": "tensor", "act": "scalar", "dve": "vector", "pool": "gpsimd", "sp": "sync"}
        stats = {}

        def make(e):
            def body(h):
                waited = {}
                nw = 0
                for op in self.ops[e]:
                    for d in op.deps:
                        if d.dma is not None:
                            s, v = sem_dma[d.dma], d.cum * 16
                            if d.dma == "setup":
                                v = self.dma_keys["setup"] * 16
                        else:
                            s, v = sem_eng[d.eng], d.sigcount
                        key = id(s)
                        if waited.get(key, 0) >= v:
                            continue
                        waited[key] = v
                        h.wait_ge(s, v)
                        nw += 1
                    ins = op.fn(h)
                    if op.dma is not None:
                        ins.then_inc(sem_dma[op.dma], 16)
                    elif op.sig:
                        ins.then_inc(sem_eng[e], 1)
                if e == "sp":
                    for k, c in finals.items():
                        h.wait_ge(sem_dma[k], c * 16)
                stats[e] = (len(self.ops[e]), nw)
            return body

        for e in ENGS:
            getattr(block, handles[e])(make(e))
        self.stats = stats


D = 1024
SEQ = 2048
NS = 16
HP = 2048
NH = 32
PD = 64
NST = 128
CONVD = 3072
SSM_PROJ = 5152
CW = 2048
CK = 31
GT = 512
NG = SEQ // GT
NCH = GT // 128
NSLOT = 5
NT_DVE = 0

ARENA_WORDS = 53200


def build_program(do_sample=True, serial=False, same_sync=True, dbg=False):
    nc = bass.Bass("TRN2", target_bir_lowering=False)

    def din(n, s):
        return nc.dram_tensor(n, s, F32, kind="ExternalInput").ap()

    def dout(n, s):
        return nc.dram_tensor(n, s, F32, kind="ExternalOutput").ap()

    xp = din("xp", [SEQ, D])
    xs_d = din("xs", [NS, D])
    st_ssm = din("st_ssm", [NS, HP, NST])
    st_cssm = din("st_cssm", [NS * 3, CONVD])
    st_ccfm = din("st_ccfm", [NS * 30, CW])
    g_pre = din("g_pre", [16, 128])
    g_post = din("g_post", [1, 2 * D])
    w_in0 = din("w_in0", [D, SSM_PROJ])
    cw0 = din("cw0", [96, 128])
    cb0 = din("cb0", [24, 128])
    dtb = din("dtb", [1, NH])
    alog = din("alog", [1, NH])
    dsk = din("dsk", [1, NH])
    ng0 = din("ng0", [16, 128])
    w_out0 = din("w_out0", [HP, D])
    w_in1 = din("w_in1", [D, 3 * CW])
    b_in1 = din("b_in1", [48, 128])
    ccw = din("ccw", [496, 128])
    ccb = din("ccb", [16, 128])
    lng = din("lng", [16, 128])
    lnb = din("lnb", [16, 128])
    w_out1 = din("w_out1", [CW, D])
    c_ident = din("c_ident", [128, 128])
    c_tri = din("c_tri", [128, 128])
    c_U = din("c_U", [128, 128])
    c_ones = din("c_ones", [128, 128])
    c_exp = din("c_exp", [NH, HP])
    c_blk = din("c_blk", [128, 32])

    yp = dout("yp", [SEQ, D])
    ys = dout("ys", [NS, D])
    ssm_p = dout("ssm_p", [HP, NST])
    cssm_p = dout("cssm_p", [72, 128])
    ccfm_p = dout("ccfm_p", [480, 128])
    ssm_s = dout("ssm_s", [NS, HP, NST])
    cssm_s = dout("cssm_s", [NS * 3, CONVD])
    ccfm_s = dout("ccfm_s", [NS * 30, CW])

    es = ExitStack()
    with es:
        arena = es.enter_context(nc.sbuf_tensor("arena", [128, ARENA_WORDS], F32))
        arena_t = arena
        arena = arena_t[:, :]
        pf = [es.enter_context(nc.psum_tensor("pf%d" % i, [128, 512], F32))[:, :] for i in range(6)]
        pb = [es.enter_context(nc.psum_tensor("pb%d" % i, [128, 1024], BF16))[:, :] for i in range(2)]
        P = Prog(nc, es, serial=serial, same_engine_sync=same_sync)

        class Alloc:
            def __init__(self, base, limit):
                self.base = base
                self.cur = base
                self.limit = limit
                self.peak = base

            def f32(self, n):
                a = arena[:, self.cur:self.cur + n]
                self.cur += n
                self.peak = max(self.peak, self.cur)
                assert self.cur <= self.limit, ("arena overflow", self.cur, self.limit)
                return a

            def bf16(self, n):
                w = (n + 1) // 2
                a = arena[:, self.cur:self.cur + w].bitcast(BF16)
                self.cur += w
                self.peak = max(self.peak, self.cur)
                assert self.cur <= self.limit, ("arena overflow", self.cur, self.limit)
                return a[:, 0:n]

            def reset(self):
                self.cur = self.base

        CA = Alloc(0, ARENA_WORDS)
        ident_f = CA.f32(128)
        tri_f = CA.f32(128)
        U_f = CA.f32(128)
        ones_f = CA.f32(128)
        ident_b = CA.bf16(128)
        tri_b = CA.bf16(128)
        U_b = CA.bf16(128)
        ones_b = CA.bf16(128)
        negm_b = CA.bf16(128)
        gpre = CA.f32(16)
        ngs = CA.f32(16)
        cw_sb = CA.f32(96)
        cb_sb = CA.f32(24)
        bin_sb = CA.f32(48)
        ccw_sb = CA.f32(496)
        ccb_sb = CA.f32(16)
        lng_sb = CA.f32(16)
        lnb_sb = CA.f32(16)
        gpost_bc = CA.f32(D)
        A_bc = CA.f32(NH)
        D_bc = CA.f32(NH)
        dtb_bc = CA.f32(NH)
        nhalf = CA.f32(8)
        wdt = CA.bf16(8 * 32)
        xres = CA.f32(NCH * D)
        uT = CA.bf16(8 * GT)
        wbuf = [CA.bf16(8 * 512) for _ in range(NSLOT)]
        xnb = [CA.bf16(D) for _ in range(2)]
        ssq = CA.f32(8)
        rstd = CA.f32(8)
        prevT = CA.f32(HP)
        prevTb = CA.bf16(HP)
        halo = CA.f32(72)
        vhalo = CA.bf16(16 * 30)
        stg = CA.f32(128)
        blk_f = CA.f32(32)
        halob = CA.bf16(72)
        xs_sb = arena[:, ARENA_WORDS - D:ARENA_WORDS]
        usT = CA.bf16(8 * NS)
        usT3 = usT.rearrange("p (k s) -> p k s", k=8)
        PH = Alloc(CA.cur, ARENA_WORDS)

        uT3 = uT.rearrange("p (k t) -> p k t", k=8)
        wb3 = [w.rearrange("p (k n) -> p k n", k=8) for w in wbuf]
        wdt3 = wdt.rearrange("p (k n) -> p k n", k=8)
        xres3 = xres.rearrange("p (c d) -> p c d", c=NCH)

        rot = {"list": [0, 1, 2, 3, 4, 5], "i": 0}

        def bank():
            l = rot["list"]
            b = l[rot["i"] % len(l)]
            rot["i"] += 1
            return b

        wblocks = []

        def l0_blocks():
            bl = []
            bl.append((w_in0, 0, 0))
            bl.append((w_in0, 0, 4096))
            bl.append((w_in0, 0, 4608))
            for g in range(4):
                bl.append((w_in0, 0, 2048 + 512 * g))
                if g + 1 < 4:
                    bl.append((w_in0, 0, 512 * (g + 1)))
            for nh in range(2):
                for kh in range(2):
                    bl.append((w_out0, 1024 * kh, 512 * nh))
            return bl

        def l1_blocks():
            bl = []
            for vb in range(4):
                bl.append((w_in1, 0, 512 * vb))
                bl.append((w_in1, 0, 2048 + 512 * vb))
            for zb in range(4):
                bl.append((w_in1, 0, 4096 + 512 * zb))
            for nh in range(2):
                for kh in range(2):
                    bl.append((w_out1, 1024 * kh, 512 * nh))
            return bl

        ngroups_total = NG + (1 if do_sample else 0)
        for _ in range(ngroups_total):
            wblocks.extend(l0_blocks())
            wblocks.extend(l1_blocks())
        wstate = {"issued": 0, "next": 0}

        def w_issue_upto(n):
            while wstate["issued"] < min(n, len(wblocks)):
                i = wstate["issued"]
                ap, r0, c0 = wblocks[i]
                slot = i % NSLOT
                src = ap[r0:r0 + 1024, c0:c0 + 512].rearrange("(k p) n -> p k n", p=128)
                P.add("pool", lambda h, s=slot, src=src: h.dma_start(out=wb3[s], in_=src),
                      w=[("wbuf", slot)], dma=("w", slot), nofence=True)
                wstate["issued"] += 1

        def w_take():
            i = wstate["next"]
            wstate["next"] += 1
            assert i < wstate["issued"]
            return i, i % NSLOT

        def w_done(i):
            w_issue_upto(i + NSLOT + 1)

        def sdma(out, in_, wtok):
            P.add("sp", lambda h: h.dma_start(out=out, in_=in_), w=[wtok], dma="setup")

        sdma(ident_f, c_ident[:, :], "ident_f")
        sdma(tri_f, c_tri[:, :], "tri_f")
        sdma(U_f, c_U[:, :], "U_f")
        sdma(ones_f, c_ones[:, :], "ones_f")
        sdma(blk_f, c_blk[:, :], "blk_f")
        sdma(A_bc, alog[0, :].partition_broadcast(128), "A_bc")
        sdma(D_bc, dsk[0, :].partition_broadcast(128), "D_bc")
        sdma(dtb_bc, dtb[0, :].partition_broadcast(128), "dtb_bc")
        P.add("pool", lambda h: h.dma_start(out=wdt3, in_=w_in0[:, 5120:5152].rearrange("(k p) n -> p k n", p=128)),
              w=["wdt"], dma="wdt")
        w_issue_upto(NSLOT)
        P.add("dve", lambda h: h.tensor_copy(out=ident_b, in_=ident_f), r=["ident_f"], w=["ident_b"])
        P.add("dve", lambda h: h.tensor_copy(out=tri_b, in_=tri_f), r=["tri_f"], w=["tri_b"])
        P.add("dve", lambda h: h.tensor_copy(out=U_b, in_=U_f), r=["U_f"], w=["U_b"])
        P.add("dve", lambda h: h.tensor_copy(out=ones_b, in_=ones_f), r=["ones_f"], w=["ones_b"])
        P.add("dve", lambda h: h.tensor_scalar(out=negm_b, in0=ones_f, scalar1=-1.0 / CW, scalar2=None, op0=ALU.mult),
              r=["ones_f"], w=["negm_b"])
        P.add("act", lambda h: h.activation(out=A_bc, in_=A_bc, func=AF.Exp), r=["A_bc"], w=["A_bc"])
        P.add("dve", lambda h: h.tensor_scalar(out=A_bc, in0=A_bc, scalar1=-1.0, scalar2=None, op0=ALU.mult),
              r=["A_bc"], w=["A_bc"])
        P.add("pool", lambda h: h.memset(nhalf, -0.5), w=["nhalf"])
        P.add("pool", lambda h: h.memset(halo, 0.0), w=["halo"])
        P.add("pool", lambda h: h.memset(vhalo, 0.0), w=["vhalo"])
        P.add("pool", lambda h: h.memset(halob, 0.0), w=["halob"])
        P.add("pool", lambda h: h.memset(prevT, 0.0), w=["prevT"])
        P.add("pool", lambda h: h.memset(prevTb, 0.0), w=["prevTb"])

        stgs = [stg] + [PH.f32(128) for _ in range(3)]
        stg_i = {"i": 0}

        def load_T(dram2d, rows, dst):
            r0 = 0
            while r0 < rows:
                n = min(128, rows - r0)
                si = stg_i["i"] % len(stgs)
                stg_i["i"] += 1
                sg_ = stgs[si]
                P.add("sp", lambda h, r0=r0, n=n, sg_=sg_: h.dma_start(out=sg_[0:n, :], in_=dram2d[r0:r0 + n, :]),
                      w=[("stg", si)], dma=("stg", si))
                b = bank()
                P.add("pe", lambda h, n=n, b=b, sg_=sg_: h.transpose(out=pf[b][:, 0:n], in_=sg_[0:n, :], identity=ident_f[0:n, 0:n]),
                      r=[("stg", si), "ident_f"], w=[("pf", b)])
                P.add("dve", lambda h, r0=r0, n=n, b=b: h.tensor_copy(out=dst[:, r0:r0 + n], in_=pf[b][:, 0:n]),
                      r=[("pf", b)], w=["params"])
                r0 += n

        load_T(g_pre, 16, gpre)
        load_T(ng0, 16, ngs)
        load_T(cw0, 96, cw_sb)
        load_T(cb0, 24, cb_sb)
        load_T(b_in1, 48, bin_sb)
        load_T(ccw, 496, ccw_sb)
        load_T(ccb, 16, ccb_sb)
        load_T(lng, 16, lng_sb)
        load_T(lnb, 16, lnb_sb)

        def load_gpost(layer):
            P.add("sp", lambda h: h.dma_start(out=gpost_bc, in_=g_post[0, layer * D:(layer + 1) * D].partition_broadcast(128)),
                  w=["gpost"], dma="gpost")
        cw3 = cw_sb.rearrange("p (k b) -> p k b", k=4)
        ccw3 = ccw_sb.rearrange("p (k b) -> p k b", k=CK)
        halo3 = halo.rearrange("p (k b) -> p k b", k=3)
        halob3 = halob.rearrange("p (k b) -> p k b", k=3)
        vhalo3 = vhalo.rearrange("p (b k) -> p b k", b=16)

        def pow_rstd(src, dst, n, scale, rt, wt):
            P.add("dve", lambda h: h.tensor_scalar(out=dst[:, 0:n], in0=src[:, 0:n], scalar1=scale, scalar2=EPS,
                                                   op0=ALU.mult, op1=ALU.add), r=[rt], w=[wt])
            P.add("pool", lambda h: h.tensor_tensor(out=dst[:, 0:n], in0=dst[:, 0:n], in1=nhalf[:, 0:n], op=ALU.pow),
                  r=[wt, "nhalf"], w=[wt])

        def prenorm_p1(layer, c, load_from=None):
            if load_from is not None:
                P.add("sp", lambda h: h.dma_start(out=xres3[:, c, :], in_=load_from),
                      w=[("xres", c)], dma=("xld", c))
            xb_ = xnb[c % 2]
            P.add("act", lambda h: h.activation(out=xb_, in_=xres3[:, c, :], func=AF.Square, accum_out=ssq[:, c:c + 1]),
                  r=[("xres", c)], w=[("xnb", c % 2), ("ssq", c)])
            P.add("dve", lambda h: h.tensor_scalar(out=rstd[:, c:c + 1], in0=ssq[:, c:c + 1], scalar1=1.0 / D,
                                                   scalar2=EPS, op0=ALU.mult, op1=ALU.add),
                  r=[("ssq", c)], w=[("rstd", c)])
            P.add("pool", lambda h: h.tensor_tensor(out=rstd[:, c:c + 1], in0=rstd[:, c:c + 1], in1=nhalf[:, 0:1], op=ALU.pow),
                  r=[("rstd", c), "nhalf"], w=[("rstd", c)])
            P.add("act", lambda h: h.activation(out=xb_, in_=xres3[:, c, :], func=AF.Copy, scale=rstd[:, c:c + 1]),
                  r=[("xres", c), ("rstd", c)], w=[("xnb", c % 2)])

        def prenorm_p2(layer, c):
            xb_ = xnb[c % 2]
            pbi = c % 2

            def tr(h):
                for k in range(8):
                    ins = h.transpose(out=pb[pbi][:, k * 128:(k + 1) * 128], in_=xb_[:, k * 128:(k + 1) * 128],
                                      identity=ident_b)
                return ins
            P.add("pe", tr, r=[("xnb", c % 2), "ident_b"], w=[("pb", pbi)])
            P.add("dve", lambda h: h.tensor_tensor(
                out=uT3[:, :, c * 128:(c + 1) * 128],
                in0=pb[pbi].rearrange("p (k t) -> p k t", k=8),
                in1=gpre[:, layer * 8:(layer + 1) * 8].unsqueeze(2).to_broadcast([128, 8, 128]),
                op=ALU.mult), r=[("pb", pbi), "params"], w=[("uT", c)])

        def prenorm_chunk(layer, c, load_from=None):
            prenorm_p1(layer, c, load_from)
            prenorm_p2(layer, c)

        def outproj(layer, lhs_of_chunk, lhs_tokens, PHA, after1=None, after2=None, prep_all=None, prep1=None, prep2=None):
            otmp = [PHA.f32(D) for _ in range(2)]
            ss2 = PHA.f32(4 * NCH).rearrange("p (c j) -> p c j", c=NCH)
            w_issue_upto(wstate["next"] + 4)
            blks = [w_take() for _ in range(4)]
            if prep_all is not None:
                prep_all()
            bks_of = {}

            def main(c, mid=None):
                bks = [bank(), bank()]
                bks_of[c] = bks
                ot = otmp[c % 2]
                for nh in range(2):
                    if nh == 1 and mid is not None:
                        mid()
                    def mm(h, nh=nh, b=bks[nh]):
                        for kh in range(2):
                            slot = blks[nh * 2 + kh][1]
                            for k in range(8):
                                ins = h.matmul(out=pf[b][:, :], lhsT=lhs_of_chunk(c, kh * 8 + k), rhs=wb3[slot][:, k, :],
                                               start=(kh == 0 and k == 0), stop=(kh == 1 and k == 7))
                        return ins
                    P.add("pe", mm, r=lhs_tokens(c) + [("wbuf", blks[nh * 2][1]), ("wbuf", blks[nh * 2 + 1][1])],
                          w=[("pf", bks[nh])])
                    P.add("act", lambda h, nh=nh, b=bks[nh]: h.activation(
                        out=ot[:, nh * 512:(nh + 1) * 512], in_=pf[b][:, :], func=AF.Square,
                        accum_out=ss2[:, c, nh:nh + 1]), r=[("pf", bks[nh])], w=[("otmp", c % 2, nh), ("ss2", c, nh)])

            def post(c):
                bks = bks_of[c]
                ot = otmp[c % 2]
                P.add("dve", lambda h: h.tensor_tensor(out=ss2[:, c, 2:3], in0=ss2[:, c, 0:1], in1=ss2[:, c, 1:2], op=ALU.add),
                      r=[("ss2", c, 0), ("ss2", c, 1)], w=[("ss2", c, 2)])
                P.add("dve", lambda h: h.tensor_scalar(out=ss2[:, c, 3:4], in0=ss2[:, c, 2:3], scalar1=1.0 / D, scalar2=EPS,
                                                       op0=ALU.mult, op1=ALU.add), r=[("ss2", c, 2)], w=[("ss2", c, 3)])
                P.add("pool", lambda h: h.tensor_tensor(out=ss2[:, c, 3:4], in0=ss2[:, c, 3:4], in1=nhalf[:, 0:1], op=ALU.pow),
                      r=[("ss2", c, 3), "nhalf"], w=[("ss2", c, 3)])
                for nh in range(2):
                    P.add("dve", lambda h, nh=nh, b=bks[nh]: h.scalar_tensor_tensor(
                        out=ot[:, nh * 512:(nh + 1) * 512], in0=pf[b][:, :], scalar=ss2[:, c, 3:4],
                        in1=gpost_bc[:, nh * 512:(nh + 1) * 512], op0=ALU.mult, op1=ALU.mult),
                        r=[("pf", bks[nh]), ("ss2", c, 3), "gpost"], w=[("otmp", c % 2, nh)])
                    P.add("pool", lambda h, nh=nh: h.tensor_tensor(
                        out=xres3[:, c, nh * 512:(nh + 1) * 512], in0=xres3[:, c, nh * 512:(nh + 1) * 512],
                        in1=ot[:, nh * 512:(nh + 1) * 512], op=ALU.add),
                        r=[("otmp", c % 2, nh), ("xres", c)], w=[("xres", c)])

            nop = lambda c: None
            a1 = after1 or nop
            a2 = after2 or nop
            p1 = prep1 or nop
            p2 = prep2 or nop
            p1(0)
            p2(0)
            for c in range(NCH):
                if c + 1 < NCH:
                    p1(c + 1)
                main(c, mid=(lambda c=c: p2(c + 1)) if c + 1 < NCH else None)
                if c >= 1:
                    post(c - 1)
                if c >= 2:
                    a1(c - 2)
                if c >= 3:
                    a2(c - 3)
            post(NCH - 1)
            a1(NCH - 2)
            a2(NCH - 3)
            a1(NCH - 1)
            a2(NCH - 2)
            a2(NCH - 1)
            for (i, s) in blks:
                w_done(i)

        def layer0(G):
            PH.reset()
            A = PH
            load_gpost(0)
            sz = A.f32(NCH * HP)
            sz3 = sz.rearrange("p (c f) -> p c f", c=NCH)
            BT = A.bf16(4 * GT).rearrange("p (g t) -> p g t", g=4)
            CT = A.bf16(4 * GT).rearrange("p (g t) -> p g t", g=4)
            Bt = A.bf16(NCH * 512).rearrange("p (c n) -> p c n", c=NCH)
            stage = [A.bf16(516) for _ in range(2)]
            cw32 = A.bf16(24 * 4 * 32).rearrange("p (b k m) -> p b k m", b=24, k=4)
            P.add("dve", lambda h: h.tensor_tensor(
                out=cw32, in0=blk_f.unsqueeze(1).unsqueeze(1).to_broadcast([128, 24, 4, 32]),
                in1=cw_sb.rearrange("p (k b) -> p b k", k=4).unsqueeze(3).to_broadcast([128, 24, 4, 32]), op=ALU.mult),
                r=["blk_f", "params"], w=["cw32"])
            xsf = [A.f32(512)] * 2
            dtr = A.f32(NCH * NH).rearrange("p (c h) -> p c h", c=NCH)
            dt_ = A.f32(NCH * NH).rearrange("p (c h) -> p c h", c=NCH)
            dtA = A.f32(NCH * NH).rearrange("p (c h) -> p c h", c=NCH)
            dtw = A.f32(NCH * NH).rearrange("p (c h) -> p c h", c=NCH)
            eaw = A.f32(NCH * 3 * NH).rearrange("p (c j h) -> p c j h", c=NCH, j=3)
            xtok = A.f32(NCH * 512).rearrange("p (c f) -> p c f", c=NCH)
            Rf = [A.f32(1024) for _ in range(2)]
            decT = [A.bf16(1024) for _ in range(2)]
            cbm = [A.bf16(128) for _ in range(2)]
            MT = [A.bf16(1024) for _ in range(2)]
            xdt = [A.bf16(512) for _ in range(2)]
            xds = [A.bf16(512) for _ in range(2)]
            xD = [A.bf16(512) for _ in range(2)]
            ytmp = [A.f32(512) for _ in range(2)]
            ptmp = A.f32(512)
            ynb = [A.bf16(HP)] * 2
            ynT = [A.bf16(HP).rearrange("p (k t) -> p k t", k=16) for _ in range(2)]
            ss3 = A.f32(8)

            tok0 = G * GT
            if G == 0:
                for c in range(NCH):
                    prenorm_chunk(0, c, load_from=xp[c * 128:(c + 1) * 128, :])
            uT_toks = [("uT", c) for c in range(NCH)]

            b = bank()

            def mm_dt(h, b=b):
                for c in range(NCH):
                    for k in range(8):
                        ins = h.matmul(out=pf[b][:, c * NH:(c + 1) * NH], lhsT=uT3[:, k, c * 128:(c + 1) * 128],
                                       rhs=wdt3[:, k, :], start=(k == 0), stop=(k == 7))
                return ins
            P.add("pe", mm_dt, r=uT_toks + ["wdt"], w=[("pf", b)])
            P.add("dve", lambda h, b=b: h.tensor_tensor(
                out=dtr, in0=pf[b][:, 0:NCH * NH].rearrange("p (c h) -> p c h", c=NCH),
                in1=dtb_bc.unsqueeze(1).to_broadcast([128, NCH, NH]), op=ALU.add),
                r=[("pf", b), "dtb_bc"], w=["dtr"])
            P.add("act", lambda h: h.activation(out=dtr, in_=dtr, func=AF.Exp), r=["dtr"], w=["dtr"])
            P.add("act", lambda h: h.activation(out=dt_, in_=dtr, func=AF.Ln, bias=1.0), r=["dtr"], w=["dt"])
            P.add("dve", lambda h: h.tensor_tensor(out=dtA, in0=dt_, in1=A_bc.unsqueeze(1).to_broadcast([128, NCH, NH]),
                                                   op=ALU.mult), r=["dt", "A_bc"], w=["dtA"])
            b = bank()

            def mm_cum(h, b=b):
                for c in range(NCH):
                    for j, lt in enumerate((tri_f, U_f, ones_f)):
                        ins = h.matmul(out=pf[b][:, (c * 3 + j) * NH:(c * 3 + j + 1) * NH], lhsT=lt, rhs=dtA[:, c, :],
                                       start=True, stop=True)
                return ins
            P.add("pe", mm_cum, r=["dtA", "tri_f", "U_f", "ones_f"], w=[("pf", b)])
            P.add("act", lambda h, b=b: h.activation(out=eaw.rearrange("p c j h -> p (c j h)"), in_=pf[b][:, 0:NCH * 3 * NH],
                                                     func=AF.Exp), r=[("pf", b)], w=["eaw"])
            P.add("dve", lambda h: h.tensor_tensor(out=dtw, in0=dt_, in1=eaw[:, :, 1, :], op=ALU.mult),
                  r=["dt", "eaw"], w=["dtw"])

            def z_chunk(zb, slot, c, raw=False):
                b = bank()

                def mm(h):
                    for k in range(8):
                        ins = h.matmul(out=pf[b][:, :], lhsT=uT3[:, k, c * 128:(c + 1) * 128], rhs=wb3[slot][:, k, :],
                                       start=(k == 0), stop=(k == 7))
                    return ins
                P.add("pe", mm, r=[("uT", c), ("wbuf", slot)], w=[("pf", b)])
                P.add("act", lambda h: h.activation(out=sz3[:, c, zb * 512:(zb + 1) * 512], in_=pf[b][:, :],
                                                    func=(AF.Copy if raw else AF.Silu)),
                      r=[("pf", b)], w=[("sz", c, zb)])

            i, slot = w_take()
            for c in range(NCH):
                z_chunk(0, slot, c)
            w_done(i)

            def stA(job, n):
                b = bank()
                job["bA"] = b
                job["sidx"] = n % 2
                slot, j, cb, sidx = job["slot"], job["j"], job["cb"], n % 2

                def mm(h):
                    for k in range(8):
                        ins = h.matmul(out=pf[b][:, :], lhsT=wb3[slot][:, k, j * 128:(j + 1) * 128], rhs=uT3[:, k, :],
                                       start=(k == 0), stop=(k == 7))
                    return ins
                P.add("pe", mm, r=uT_toks + [("wbuf", slot)], w=[("pf", b)])
                st = stage[sidx]
                P.add("act", lambda h: h.activation(out=st[:, 3:515], in_=pf[b][:, :], func=AF.Copy),
                      r=[("pf", b)], w=[("stage", sidx, 1)])
                if G == NG - 1:
                    P.add("dve", lambda h: h.tensor_copy(out=halo3[:, :, cb], in_=pf[b][:, 509:512]),
                          r=[("pf", b)], w=[("halo", cb)])
                P.add("pool", lambda h: h.tensor_copy(out=st[:, 0:3], in_=halob3[:, :, cb]),
                      r=[("halob", cb)], w=[("stage", sidx, 0)])
                P.add("pool", lambda h: h.tensor_copy(out=halob3[:, :, cb], in_=st[:, 512:515]),
                      r=[("stage", sidx, 1), ("stage", sidx, 0)], w=[("halob", cb)])

            def stB(job):
                cb, sidx = job["cb"], job["sidx"]
                st = stage[sidx]
                b2 = bank()

                def mmconv(h):
                    for k in range(4):
                        for i4 in range(4):
                            ps_ = slice(32 * i4, 32 * i4 + 32)
                            ins = h.matmul(out=pf[b2][ps_, :], lhsT=cw32[ps_, cb, k, :], rhs=st[ps_, k:k + 512],
                                           start=(k == 0), stop=(k == 3), tile_position=(32 * i4, 32 * i4))
                    return ins
                P.add("pe", mmconv, r=[("stage", sidx, 0), ("stage", sidx, 1), "cw32"], w=[("pf", b2)])
                kind, g = job["kind"], job["g"]
                if kind in ("B", "C"):
                    dstT = BT if kind == "B" else CT
                    P.add("act", lambda h: h.activation(out=dstT[:, g, :], in_=pf[b2][:, :], func=AF.Silu,
                                                        bias=cb_sb[:, cb:cb + 1]),
                          r=[("pf", b2), "params"], w=[(kind + "T", g)])
                else:
                    P.add("act", lambda h: h.activation(out=xsf[0], in_=pf[b2][:, :], func=AF.Silu, bias=cb_sb[:, cb:cb + 1]),
                          r=[("pf", b2), "params"], w=[("xsf", 0)])

            def stC(job):
                if job["kind"] != "x":
                    return
                j = job["j"]
                b = bank()

                def trx(h):
                    for c in range(NCH):
                        ins = h.transpose(out=pf[b][:, c * 128:(c + 1) * 128], in_=xsf[0][:, c * 128:(c + 1) * 128],
                                          identity=ident_f)
                    return ins
                P.add("pe", trx, r=[("xsf", 0), "ident_f"], w=[("pf", b)])
                P.add("dve", lambda h: h.tensor_copy(
                    out=xtok[:, :, j * 128:(j + 1) * 128], in_=pf[b][:, :].rearrange("p (c f) -> p c f", c=NCH)),
                    r=[("pf", b)], w=[("xtok", j)])

            def run_conv_jobs(jobs):
                n = len(jobs)
                for t in range(n + 2):
                    if 0 <= t - 2 < n:
                        stC(jobs[t - 2])
                    if 0 <= t - 1 < n:
                        stB(jobs[t - 1])
                    if t < n:
                        stA(jobs[t], t)

            jobs = []
            done_after = []
            for which in ("B", "C"):
                i, slot = w_take()
                done_after.append(i)
                for g in range(4):
                    jobs.append(dict(cb=(16 if which == "B" else 20) + g, slot=slot, j=g, kind=which, g=g))
            run_conv_jobs(jobs)
            for i in done_after:
                w_done(i)
            for c in range(NCH):
                pbi = c % 2

                def trb(h, c=c, pbi=pbi):
                    for g in range(4):
                        ins = h.transpose(out=pb[pbi][:, g * 128:(g + 1) * 128], in_=BT[:, g, c * 128:(c + 1) * 128],
                                          identity=ident_b)
                    return ins
                P.add("pe", trb, r=[("BT", g) for g in range(4)] + ["ident_b"], w=[("pb", pbi)])
                P.add("dve", lambda h, c=c, pbi=pbi: h.tensor_copy(out=Bt[:, c, :], in_=pb[pbi][:, 0:512]),
                      r=[("pb", pbi)], w=[("Bt", c)])

            def ssd_group(g):
                i, slot = w_take()
                run_conv_jobs([dict(cb=g * 4 + j, slot=slot, j=j, kind="x", g=g) for j in range(4)])
                w_done(i)
                xtok_toks = [("xtok", j) for j in range(4)]
                hs = slice(g * 8, (g + 1) * 8)

                cb_bank = {}

                def s1r(c):
                    q = c % 2
                    P.add("dve", lambda h: h.tensor_tensor(
                        out=Rf[q].rearrange("p (h t) -> p h t", h=8), in0=tri_f.unsqueeze(1).to_broadcast([128, 8, 128]),
                        in1=dtA[:, c, hs].unsqueeze(2).to_broadcast([128, 8, 128]), op=ALU.mult),
                        r=["tri_f", "dtA"], w=[("Rf", q)])

                def s1a(c):
                    q = c % 2
                    dbk = [bank(), bank()]
                    for half in range(2):
                        P.add("pe", lambda h, half=half, b=dbk[half]: h.matmul(
                            out=pf[b][:, :], lhsT=U_f, rhs=Rf[q][:, half * 512:(half + 1) * 512], start=True, stop=True),
                            r=[("Rf", q), "U_f"], w=[("pf", dbk[half])])
                        P.add("act", lambda h, half=half, b=dbk[half]: h.activation(
                            out=decT[q][:, half * 512:(half + 1) * 512], in_=pf[b][:, :], func=AF.Exp),
                            r=[("pf", dbk[half])], w=[("decT", q, half)])
                    cbp = pb[q].bitcast(F32)
                    P.add("pe", lambda h: h.matmul(out=cbp[:, 0:128], lhsT=BT[:, g, c * 128:(c + 1) * 128],
                                                   rhs=CT[:, g, c * 128:(c + 1) * 128], start=True, stop=True),
                          r=[("BT", g), ("CT", g)], w=[("pb", q)])

                def s1b(c):
                    q = c % 2
                    cbp = pb[q].bitcast(F32)
                    P.add("dve", lambda h: h.tensor_tensor(out=cbm[q], in0=cbp[:, 0:128], in1=tri_f, op=ALU.mult),
                          r=[("pb", q), "tri_f"], w=[("cbm", q)])
                    P.add("dve", lambda h: h.tensor_tensor(
                        out=MT[q].rearrange("p (h t) -> p h t", h=8), in0=decT[q].rearrange("p (h t) -> p h t", h=8),
                        in1=cbm[q].unsqueeze(1).to_broadcast([128, 8, 128]), op=ALU.mult),
                        r=[("decT", q, 0), ("decT", q, 1), ("cbm", q)], w=[("MT", q)])
                    x3 = xtok[:, c, :].rearrange("p (h d) -> p h d", h=8)
                    for dst, src, tk, eng in ((xdt, dt_, "dt", "dve"), (xds, dtw, "dtw", "pool")):
                        P.add(eng, lambda h, dst=dst, src=src: h.tensor_tensor(
                            out=dst[q].rearrange("p (h d) -> p h d", h=8), in0=x3,
                            in1=src[:, c, hs].unsqueeze(2).to_broadcast([128, 8, PD]), op=ALU.mult),
                            r=xtok_toks + [tk], w=[(tk + "x", q)])
                    P.add("pool", lambda h: h.tensor_tensor(
                        out=xD[q].rearrange("p (h d) -> p h d", h=8), in0=x3,
                        in1=D_bc[:, hs].unsqueeze(2).to_broadcast([128, 8, PD]), op=ALU.mult),
                        r=xtok_toks + ["D_bc"], w=[("xD", q)])

                s2b = {}

                def s2(c):
                    q = c % 2
                    by = bank()

                    def mmy(h):
                        h.matmul(out=pf[by][:, :], lhsT=ident_b, rhs=xD[q], start=True, stop=False)
                        for hh in range(8):
                            ins = h.matmul(out=pf[by][:, hh * PD:(hh + 1) * PD], lhsT=MT[q][:, hh * 128:(hh + 1) * 128],
                                           rhs=xdt[q][:, hh * PD:(hh + 1) * PD], start=False, stop=(hh == 7))
                        return ins
                    P.add("pe", mmy, r=[("xD", q), ("MT", q), ("dtx", q), "ident_b"], w=[("pf", by)])
                    bo = bank()
                    P.add("pe", lambda h: h.matmul(out=pf[bo][:, :], lhsT=CT[:, g, c * 128:(c + 1) * 128],
                                                   rhs=prevTb[:, g * 512:(g + 1) * 512], start=True, stop=True),
                          r=[("CT", g), ("prevTb", g)], w=[("pf", bo)])
                    bs = bank()
                    P.add("pe", lambda h: h.matmul(out=pf[bs][:, :], lhsT=Bt[:, c, g * 128:(g + 1) * 128],
                                                   rhs=xds[q], start=True, stop=True),
                          r=[("Bt", c), ("dtwx", q)], w=[("pf", bs)])
                    s2b[c] = (by, bo, bs)

                def s2r(c):
                    q = c % 2
                    by, bo, bs = s2b[c]
                    yt = ytmp[q]
                    P.add("dve", lambda h: h.tensor_tensor(
                        out=yt.rearrange("p (h d) -> p h d", h=8), in0=pf[bo][:, :].rearrange("p (h d) -> p h d", h=8),
                        in1=eaw[:, c, 0, hs].unsqueeze(2).to_broadcast([128, 8, PD]), op=ALU.mult),
                        r=[("pf", bo), "eaw"], w=[("ytmp", q)])
                    P.add("dve", lambda h: h.tensor_tensor(
                        out=ptmp.rearrange("p (h d) -> p h d", h=8),
                        in0=prevT[:, g * 512:(g + 1) * 512].rearrange("p (h d) -> p h d", h=8),
                        in1=eaw[:, c, 2, hs].unsqueeze(2).to_broadcast([128, 8, PD]), op=ALU.mult),
                        r=[("prevT", g), "eaw"], w=["ptmp"])
                    P.add("dve", lambda h: h.tensor_tensor(out=prevT[:, g * 512:(g + 1) * 512], in0=ptmp, in1=pf[bs][:, :],
                                                           op=ALU.add),
                          r=["ptmp", ("pf", bs)], w=[("prevT", g)])
                    P.add("act", lambda h: h.activation(out=prevTb[:, g * 512:(g + 1) * 512], in_=prevT[:, g * 512:(g + 1) * 512],
                                                        func=AF.Copy),
                          r=[("prevT", g)], w=[("prevTb", g)])
                    P.add("dve", lambda h: h.tensor_tensor(out=yt, in0=yt, in1=pf[by][:, :], op=ALU.add),
                          r=[("pf", by), ("ytmp", q)], w=[("ytmp", q)])
                    P.add("pool", lambda h: h.tensor_tensor(
                        out=sz3[:, c, g * 512:(g + 1) * 512], in0=sz3[:, c, g * 512:(g + 1) * 512], in1=yt, op=ALU.mult),
                        r=[("ytmp", q), ("sz", c, g)], w=[("sz", c, g)])

                if g + 1 < 4:
                    iz, zslot = w_take()
                for t in range(NCH + 2):
                    if t < NCH:
                        s1r(t)
                    if 0 <= t - 2 < NCH:
                        s2(t - 2)
                    if t < NCH:
                        s1a(t)
                    if 0 <= t - 2 < NCH:
                        s2r(t - 2)
                    if 0 <= t - 1 < NCH:
                        s1b(t - 1)
                    if g + 1 < 4 and t < NCH:
                        z_chunk(g + 1, zslot, t, raw=True)
                if g + 1 < 4:
                    w_done(iz)
                    zs = sz3[:, :, (g + 1) * 512:(g + 2) * 512]
                    P.add("act", lambda h: h.activation(out=zs, in_=zs, func=AF.Silu),
                          r=[("sz", c, g + 1) for c in range(NCH)], w=[("sz", c, g + 1) for c in range(NCH)])

            for g in range(4):
                ssd_group(g)

            def prep_all():
                for c in range(NCH):
                    sz_toks = [("sz", c, g) for g in range(4)]
                    P.add("act", lambda h, c=c: h.activation(out=ynb[c % 2], in_=sz3[:, c, :], func=AF.Square,
                                                             accum_out=ss3[:, c:c + 1]),
                          r=sz_toks, w=[("ynb", 0), ("ss3", c)])
                P.add("dve", lambda h: h.tensor_scalar(out=ss3[:, 4:8], in0=ss3[:, 0:4], scalar1=1.0 / HP, scalar2=EPS,
                                                       op0=ALU.mult, op1=ALU.add), r=[("ss3", c) for c in range(4)], w=["ss3r"])
                P.add("pool", lambda h: h.tensor_tensor(out=ss3[:, 4:8], in0=ss3[:, 4:8], in1=nhalf[:, 0:4], op=ALU.pow),
                      r=["ss3r", "nhalf"], w=["ss3r"])

            def prep1(c):
                sz_toks = [("sz", c, g) for g in range(4)]
                yb = ynb[c % 2]
                P.add("act", lambda h: h.activation(out=yb, in_=sz3[:, c, :], func=AF.Copy, scale=ss3[:, 4 + c:5 + c]),
                      r=sz_toks + ["ss3r"], w=[("ynb", 0)])

            def prep2(c):
                yb = ynb[c % 2]
                yT_ = ynT[c % 2]
                for half in range(2):
                    def trn(h, half=half):
                        for k in range(8):
                            kk = half * 8 + k
                            ins = h.transpose(out=pb[half][:, k * 128:(k + 1) * 128], in_=yb[:, kk * 128:(kk + 1) * 128],
                                              identity=ident_b)
                        return ins
                    P.add("pe", trn, r=[("ynb", 0), "ident_b"], w=[("pb", half)])
                    P.add("dve", lambda h, half=half: h.tensor_tensor(
                        out=yT_[:, half * 8:(half + 1) * 8, :], in0=pb[half].rearrange("p (k t) -> p k t", k=8),
                        in1=ngs[:, half * 8:(half + 1) * 8].unsqueeze(2).to_broadcast([128, 8, 128]), op=ALU.mult),
                        r=[("pb", half), "params"], w=[("ynT", c % 2, half)])

            outproj(0, lambda c, kc: ynT[c % 2][:, kc, :], lambda c: [("ynT", c % 2, 0), ("ynT", c % 2, 1)], A,
                    prep_all=prep_all, prep1=prep1, prep2=prep2,
                    after1=lambda c: prenorm_p1(1, c), after2=lambda c: prenorm_p2(1, c))
            if P.dbg == "l0":
                P.fence()
                P.dump("xres", xres, NCH * D, [("xres", c) for c in range(4)])

        def layer1(G):
            PH.reset()
            A = PH
            load_gpost(1)
            last = (G == NG - 1)
            cbuf = A.f32(16 * GT).rearrange("p (b t) -> p b t", b=16)
            aT = A.bf16(16 * GT).rearrange("p (b t) -> p b t", b=16)
            vst = [A.bf16(544) for _ in range(2)]
            NPE = CK - NT_DVE
            dg32 = A.bf16(16 * NPE * 32).rearrange("p (b k m) -> p b k m", b=16, k=NPE)
            for b4 in range(0, 16, 4):
                P.add("dve", lambda h, b4=b4: h.tensor_tensor(
                    out=dg32[:, b4:b4 + 4], in0=blk_f.unsqueeze(1).unsqueeze(1).to_broadcast([128, 4, NPE, 32]),
                    in1=ccw_sb.rearrange("p (k b) -> p b k", k=CK)[:, b4:b4 + 4, 0:NPE].unsqueeze(3).to_broadcast([128, 4, NPE, 32]),
                    op=ALU.mult), r=["blk_f", "params"], w=[("dg32", b4)])
            sgt = [A.f32(512) for _ in range(2)]
            csq = [A.bf16(512) for _ in range(2)]
            cb16 = [A.bf16(512) for _ in range(2)]
            cacc = [[A.f32(512) for _ in range(2)] for _ in range(2)] if NT_DVE > 0 else None
            szt = [A.f32(512) for _ in range(2)]
            t1 = [A.f32(512) for _ in range(2)]
            mean = A.f32(512)
            var = A.f32(512)
            rsd = A.f32(512)
            vlast = A.f32(480)
            vlT = A.f32(128)

            uT_toks = [("uT", c) for c in range(NCH)]
            rot["list"] = [0, 1, 2, 3]
            rot["i"] = 0
            S1, S2 = 4, 5
            wslots = {}
            cbank = {}

            def stA1(blk):
                vb, j = blk // 4, blk % 4
                if j == 0:
                    wslots[vb] = (w_take(), w_take())
                (iv, sv), (ig, sg_) = wslots[vb]
                q = blk % 2
                bv, bg = bank(), bank()
                for (bk, slot) in ((bv, sv), (bg, sg_)):
                    if blk < 2:
                        for c in range(NCH):
                            def mmc_(h, bk=bk, slot=slot, c=c):
                                for k in range(8):
                                    ins = h.matmul(out=pf[bk][:, c * 128:(c + 1) * 128], lhsT=wb3[slot][:, k, j * 128:(j + 1) * 128],
                                                   rhs=uT3[:, k, c * 128:(c + 1) * 128], start=(k == 0), stop=(k == 7))
                                return ins
                            P.add("pe", mmc_, r=[("uT", c), ("wbuf", slot)], w=[("pf", bk)])
                        continue

                    def mm(h, bk=bk, slot=slot):
                        for k in range(8):
                            ins = h.matmul(out=pf[bk][:, :], lhsT=wb3[slot][:, k, j * 128:(j + 1) * 128], rhs=uT3[:, k, :],
                                           start=(k == 0), stop=(k == 7))
                        return ins
                    P.add("pe", mm, r=uT_toks + [("wbuf", slot)], w=[("pf", bk)])
                P.add("act", lambda h: h.activation(
                    out=sgt[q], in_=pf[bg][:, :], func=AF.Sigmoid, bias=bin_sb[:, 16 + blk:17 + blk]),
                    r=[("pf", bg), "params"], w=[("sgt", q)])
                P.add("dve", lambda h: h.scalar_tensor_tensor(
                    out=vst[q][:, 30:542], in0=pf[bv][:, :], scalar=bin_sb[:, blk:blk + 1], in1=sgt[q],
                    op0=ALU.add, op1=ALU.mult),
                    r=[("pf", bv), ("sgt", q), "params"], w=[("vst", q, 1)])
                if last:
                    P.add("dve", lambda h: h.scalar_tensor_tensor(
                        out=vlast.rearrange("p (k b) -> p k b", k=30)[:, :, blk], in0=pf[bv][:, 482:512],
                        scalar=bin_sb[:, blk:blk + 1], in1=sgt[q][:, 482:512], op0=ALU.add, op1=ALU.mult),
                        r=[("pf", bv), ("sgt", q), "params"], w=[("vlast", blk)])
                P.add("pool", lambda h: h.tensor_copy(out=vst[q][:, 0:30], in_=vhalo3[:, blk, :]),
                      r=[("vhalo", blk)], w=[("vst", q, 0)])
                P.add("pool", lambda h: h.tensor_copy(out=vhalo3[:, blk, :], in_=vst[q][:, 512:542]),
                      r=[("vst", q, 0), ("vst", q, 1)], w=[("vhalo", blk)])
                if j == 3:
                    w_done(iv)
                    w_done(ig)

            def stB1(blk):
                q = blk % 2
                bc = bank()
                cbank[blk] = bc
                npe = CK - NT_DVE

                def mmc(h):
                    for k in range(npe):
                        for i4 in range(4):
                            ps_ = slice(32 * i4, 32 * i4 + 32)
                            ins = h.matmul(out=pf[bc][ps_, :], lhsT=dg32[ps_, blk, k, :], rhs=vst[q][ps_, k:k + 512],
                                           start=(k == 0), stop=(k == npe - 1), tile_position=(32 * i4, 32 * i4))
                    return ins
                P.add("pe", mmc, r=[("dg32", (blk // 4) * 4), ("vst", q, 0), ("vst", q, 1)], w=[("pf", bc)])
                for n_, k in enumerate(range(npe, CK)):
                    a_ = cacc[q][n_ % 2]
                    if n_ < 2:
                        P.add("dve", lambda h, k=k, a_=a_: h.tensor_scalar(out=a_, in0=vst[q][:, k:k + 512],
                                                                          scalar1=ccw3[:, k, blk:blk + 1], scalar2=None, op0=ALU.mult),
                              r=[("vst", q, 0), ("vst", q, 1), "params"], w=[("cacc", q, n_ % 2)])
                    else:
                        P.add("dve", lambda h, k=k, a_=a_: h.scalar_tensor_tensor(out=a_, in0=vst[q][:, k:k + 512],
                                                                                 scalar=ccw3[:, k, blk:blk + 1], in1=a_,
                                                                                 op0=ALU.mult, op1=ALU.add),
                              r=[("vst", q, 0), ("vst", q, 1), ("cacc", q, n_ % 2), "params"], w=[("cacc", q, n_ % 2)])

            def stB1b(blk):
                q = blk % 2
                bc = cbank[blk]
                if NT_DVE > 0:
                    P.add("dve", lambda h: h.scalar_tensor_tensor(out=cacc[q][0], in0=pf[bc][:, :], scalar=ccb_sb[:, blk:blk + 1],
                                                                  in1=cacc[q][0], op0=ALU.add, op1=ALU.add),
                          r=[("pf", bc), ("cacc", q, 0), "params"], w=[("cacc", q, 0)])
                    P.add("dve", lambda h: h.tensor_tensor(out=cbuf[:, blk, :], in0=cacc[q][0], in1=cacc[q][1], op=ALU.add),
                          r=[("cacc", q, 0), ("cacc", q, 1)], w=[("cbuf", blk)])
                    P.add("act", lambda h: h.activation(out=cb16[q], in_=cbuf[:, blk, :], func=AF.Copy),
                          r=[("cbuf", blk)], w=[("cb16", q)])
                    P.add("act", lambda h: h.activation(out=csq[q], in_=cbuf[:, blk, :], func=AF.Square),
                          r=[("cbuf", blk)], w=[("csq", q)])
                else:
                    P.add("act", lambda h: h.activation(out=cb16[q], in_=pf[bc][:, :], func=AF.Identity, bias=ccb_sb[:, blk:blk + 1]),
                          r=[("pf", bc), "params"], w=[("cb16", q)])
                    P.add("act", lambda h: h.activation(out=csq[q], in_=pf[bc][:, :], func=AF.Square, bias=ccb_sb[:, blk:blk + 1]),
                          r=[("pf", bc), "params"], w=[("csq", q)])
                    P.add("act", lambda h: h.activation(out=cbuf[:, blk, :], in_=pf[bc][:, :], func=AF.Identity,
                                                        bias=ccb_sb[:, blk:blk + 1]),
                          r=[("pf", bc), "params"], w=[("cbuf", blk)])

            def stC1(blk):
                q = blk % 2
                P.add("pe", lambda h: h.matmul(out=pf[S1][:, :], lhsT=negm_b, rhs=cb16[q], start=(blk == 0), stop=(blk == 15)),
                      r=[("cb16", q), "negm_b"], w=[("pf", S1)])
                P.add("pe", lambda h: h.matmul(out=pf[S2][:, :], lhsT=ones_b, rhs=csq[q], start=(blk == 0), stop=(blk == 15)),
                      r=[("csq", q), "ones_b"], w=[("pf", S2)])

            stA1(0)
            for t in range(16 + 1):
                if 0 <= t - 1 < 16:
                    stC1(t - 1)
                if t < 16:
                    stB1(t)
                    if NT_DVE == 0:
                        stB1b(t)
                if t + 1 < 16:
                    stA1(t + 1)
                if t < 16 and NT_DVE > 0:
                    stB1b(t)
            P.add("dve", lambda h: h.tensor_copy(out=mean, in_=pf[S1][:, :]), r=[("pf", S1)], w=["mean"])
            P.add("dve", lambda h: h.tensor_tensor(out=var, in0=mean, in1=mean, op=ALU.mult), r=["mean"], w=["var"])
            P.add("dve", lambda h: h.scalar_tensor_tensor(out=var, in0=pf[S2][:, :], scalar=1.0 / CW, in1=var,
                                                          op0=ALU.mult, op1=ALU.subtract),
                  r=[("pf", S2), "var"], w=["var"])
            P.add("dve", lambda h: h.tensor_scalar(out=var, in0=var, scalar1=0.0, scalar2=EPS, op0=ALU.max, op1=ALU.add),
                  r=["var"], w=["var"])
            P.add("act", lambda h: h.activation(out=rsd, in_=var, func=AF.Ln), r=["var"], w=["rsd"])
            P.add("act", lambda h: h.activation(out=rsd, in_=rsd, func=AF.Exp, scale=-0.5), r=["rsd"], w=["rsd"])
            rot["list"] = [0, 1, 2, 3, 5]
            if last:
                for r0 in range(0, 480, 120):
                    b = bank()
                    P.add("pe", lambda h, b=b, r0=r0: h.transpose(out=pf[b][0:120, 0:128], in_=vlast[:, r0:r0 + 120],
                                                                  identity=ident_f),
                          r=[("vlast", k) for k in range(16)] + ["ident_f"], w=[("pf", b)])
                    P.add("dve", lambda h, b=b: h.tensor_copy(out=vlT[0:120, :], in_=pf[b][0:120, 0:128]),
                          r=[("pf", b)], w=["vlT"])
                    P.add("sp", lambda h, r0=r0: h.dma_start(out=ccfm_p[r0:r0 + 120, :], in_=vlT[0:120, :]),
                          r=["vlT"], w=["vlT_d"], dma="o_ccfm", out=True)
                    P.readers.setdefault("vlT", []).append(P.prev)
            for zb in range(4):
                i, slot = w_take()
                for j in range(4):
                    blk = zb * 4 + j
                    q = blk % 2
                    b = bank()

                    def mm(h, b=b, slot=slot, j=j):
                        for k in range(8):
                            ins = h.matmul(out=pf[b][:, :], lhsT=wb3[slot][:, k, j * 128:(j + 1) * 128], rhs=uT3[:, k, :],
                                           start=(k == 0), stop=(k == 7))
                        return ins
                    P.add("pe", mm, r=uT_toks + [("wbuf", slot)], w=[("pf", b)])
                    P.add("act", lambda h, b=b, q=q, blk=blk: h.activation(out=szt[q], in_=pf[b][:, :], func=AF.Silu,
                                                                         bias=bin_sb[:, 32 + blk:33 + blk]),
                          r=[("pf", b), "params"], w=[("szt", q)])
                    P.add("dve", lambda h, q=q, blk=blk: h.tensor_tensor(out=t1[q], in0=cbuf[:, blk, :], in1=pf[S1][:, :], op=ALU.add),
                          r=[("cbuf", blk), ("pf", S1)], w=[("t1", q)])
                    P.add("dve", lambda h, q=q: h.tensor_tensor(out=t1[q], in0=t1[q], in1=rsd, op=ALU.mult),
                          r=[("t1", q), "rsd"], w=[("t1", q)])
                    P.add("act", lambda h, q=q, blk=blk: h.activation(out=t1[q], in_=t1[q], func=AF.Silu,
                                                                    scale=lng_sb[:, blk:blk + 1], bias=lnb_sb[:, blk:blk + 1]),
                          r=[("t1", q), "params"], w=[("t1", q)])
                    P.add("pool", lambda h, q=q, blk=blk: h.tensor_tensor(out=aT[:, blk, :], in0=t1[q], in1=szt[q], op=ALU.mult),
                          r=[("t1", q), ("szt", q)], w=[("aT", blk)])
                w_done(i)
            rot["list"] = [0, 1, 2, 3, 4, 5]
            tok0 = G * GT

            def after1(c):
                P.add("sp", lambda h, c=c: h.dma_start(out=yp[tok0 + c * 128: tok0 + (c + 1) * 128, :], in_=xres3[:, c, :]),
                      r=[("xres", c)], w=[("yp_d", c)], dma=("o_yp", c), out=True)
                if G + 1 < NG:
                    t1_ = (G + 1) * GT + c * 128
                    prenorm_p1(0, c, load_from=xp[t1_:t1_ + 128, :])

            def after2(c):
                if G + 1 < NG:
                    prenorm_p2(0, c)
            outproj(1, lambda c, kc: aT[:, kc, c * 128:(c + 1) * 128], lambda c: [("aT", k) for k in range(16)], A,
                    after1=after1, after2=after2)


        def tt(eng, out, in0, in1, op, r, w):
            P.add(eng, lambda h: h.tensor_tensor(out=out, in0=in0, in1=in1, op=op), r=r, w=w)

        def act(out, in_, func, r, w, **kw):
            P.add("act", lambda h: h.activation(out=out, in_=in_, func=func, **kw), r=r, w=w)

        def s_prenorm(layer, A):
            xn = A.bf16(D)
            sq = A.f32(8)
            act(xn[0:16, :], xs_sb[0:16, :], AF.Square, ["xs_sb"], ["s_xn", "s_sq"], accum_out=sq[0:16, 0:1])
            P.add("dve", lambda h: h.tensor_scalar(out=sq[0:16, 1:2], in0=sq[0:16, 0:1], scalar1=1.0 / D, scalar2=EPS,
                                                   op0=ALU.mult, op1=ALU.add), r=["s_sq"], w=["s_sq"])
            tt("pool", sq[0:16, 1:2], sq[0:16, 1:2], nhalf[0:16, 0:1], ALU.pow, ["s_sq", "nhalf"], ["s_sq"])
            act(xn[0:16, :], xs_sb[0:16, :], AF.Copy, ["xs_sb", "s_sq"], ["s_xn"], scale=sq[0:16, 1:2])

            def tr(h):
                for k in range(8):
                    ins = h.transpose(out=pb[0][:, k * 16:(k + 1) * 16], in_=xn[0:16, k * 128:(k + 1) * 128],
                                      identity=ident_b[0:16, 0:16])
                return ins
            P.add("pe", tr, r=["s_xn", "ident_b"], w=[("pb", 0)])
            tt("dve", usT3, pb[0][:, 0:128].rearrange("p (k s) -> p k s", k=8),
               gpre[:, layer * 8:(layer + 1) * 8].unsqueeze(2).to_broadcast([128, 8, NS]), ALU.mult,
               [("pb", 0), "params"], ["usT"])

        def s_inproj_block(dst3, blk0, nsub=4):
            i, slot = w_take()
            b = bank()

            def mm(h):
                for j in range(nsub):
                    for k in range(8):
                        ins = h.matmul(out=pf[b][:, j * NS:(j + 1) * NS], lhsT=wb3[slot][:, k, j * 128:(j + 1) * 128],
                                       rhs=usT3[:, k, :], start=(k == 0), stop=(k == 7))
                return ins
            P.add("pe", mm, r=["usT", ("wbuf", slot)], w=[("pf", b)])
            P.add("dve", lambda h: h.tensor_copy(out=dst3[:, blk0:blk0 + nsub, :],
                                                 in_=pf[b][:, 0:nsub * NS].rearrange("p (j s) -> p j s", j=nsub)),
                  r=[("pf", b)], w=["s_proj"])
            w_done(i)

        def s_outproj(layer, aT3, A):
            otmp = A.f32(D)
            s2 = A.f32(8)
            w_issue_upto(wstate["next"] + 4)
            blks = [w_take() for _ in range(4)]
            bks = [bank(), bank()]
            for nh in range(2):
                def mm(h, nh=nh, b=bks[nh]):
                    for kh in range(2):
                        slot = blks[nh * 2 + kh][1]
                        for k in range(8):
                            ins = h.matmul(out=pf[b][0:16, :], lhsT=aT3[:, kh * 8 + k, :], rhs=wb3[slot][:, k, :],
                                           start=(kh == 0 and k == 0), stop=(kh == 1 and k == 7))
                    return ins
                P.add("pe", mm, r=["s_aT", ("wbuf", blks[nh * 2][1]), ("wbuf", blks[nh * 2 + 1][1])], w=[("pf", bks[nh])])
                act(otmp[0:16, nh * 512:(nh + 1) * 512], pf[bks[nh]][0:16, :], AF.Square, [("pf", bks[nh])],
                    [("s_otmp", nh), ("s_s2", nh)], accum_out=s2[0:16, nh:nh + 1])
            tt("dve", s2[0:16, 2:3], s2[0:16, 0:1], s2[0:16, 1:2], ALU.add, [("s_s2", 0), ("s_s2", 1)], [("s_s2", 2)])
            P.add("dve", lambda h: h.tensor_scalar(out=s2[0:16, 3:4], in0=s2[0:16, 2:3], scalar1=1.0 / D, scalar2=EPS,
                                                   op0=ALU.mult, op1=ALU.add), r=[("s_s2", 2)], w=[("s_s2", 3)])
            tt("pool", s2[0:16, 3:4], s2[0:16, 3:4], nhalf[0:16, 0:1], ALU.pow, [("s_s2", 3), "nhalf"], [("s_s2", 3)])
            for nh in range(2):
                P.add("dve", lambda h, nh=nh, b=bks[nh]: h.scalar_tensor_tensor(
                    out=otmp[0:16, nh * 512:(nh + 1) * 512], in0=pf[b][0:16, :], scalar=s2[0:16, 3:4],
                    in1=gpost_bc[0:16, nh * 512:(nh + 1) * 512], op0=ALU.mult, op1=ALU.mult),
                    r=[("pf", bks[nh]), ("s_s2", 3), "gpost"], w=[("s_otmp", nh)])
                tt("dve", xs_sb[0:16, nh * 512:(nh + 1) * 512], xs_sb[0:16, nh * 512:(nh + 1) * 512],
                   otmp[0:16, nh * 512:(nh + 1) * 512], ALU.add, [("s_otmp", nh), "xs_sb"], ["xs_sb"])
            for (i, s_) in blks:
                w_done(i)

        def s_colstats(src2d, ncols, A, tag):
            b = bank()
            P.add("pe", lambda h: h.matmul(out=pf[b][:, 0:ncols], lhsT=ones_f, rhs=src2d, start=True, stop=True),
                  r=[tag, "ones_f"], w=[("pf", b)])
            return b

        def sample_layer0():
            PH.reset()
            PH.limit = ARENA_WORDS - D
            A = PH
            load_gpost(0)
            P.add("sp", lambda h: h.dma_start(out=xs_sb[0:16, :], in_=xs_d[:, :]), w=["xs_sb"], dma="s_ld")
            s_prenorm(0, A)
            zxT = A.f32(40 * NS).rearrange("p (b s) -> p b s", b=40)
            dexp = A.f32(16 * 33).rearrange("p (t c) -> p t c", t=16)
            xc = A.f32(24 * NS).rearrange("p (b s) -> p b s", b=24)
            BCtok = A.f32(1024)
            sel = A.f32(NS * 128)
            xdtT = A.f32(16 * NS).rearrange("p (t s) -> p t s", t=16)
            yT = A.f32(16 * NS).rearrange("p (t s) -> p t s", t=16)
            s_mark = PH.cur
            s_inproj_block(zxT, 0)
            s_inproj_block(zxT, 32)
            s_inproj_block(zxT, 36)
            for g in range(4):
                s_inproj_block(zxT, 16 + g * 4)
                if g + 1 < 4:
                    s_inproj_block(zxT, (g + 1) * 4)
            dtT = A.f32(2 * NS + 1)
            colp = A.f32(4)
            P.add("sp", lambda h: h.dma_start(out=colp[0:NH, 0:1], in_=dtb.rearrange("o h -> h o")), w=["s_colp0"], dma="s_ld2")
            P.add("sp", lambda h: h.dma_start(out=colp[0:NH, 1:2], in_=alog.rearrange("o h -> h o")), w=["s_colp1"], dma="s_ld3")
            P.add("sp", lambda h: h.dma_start(out=dtT[0:NH, 32:33], in_=dsk.rearrange("o h -> h o")), w=["s_dcol"], dma="s_ld4")
            act(colp[0:NH, 1:2], colp[0:NH, 1:2], AF.Exp, ["s_colp1"], ["s_colp1"])
            P.add("dve", lambda h: h.tensor_scalar(out=colp[0:NH, 1:2], in0=colp[0:NH, 1:2], scalar1=-1.0, scalar2=None,
                                                   op0=ALU.mult), r=["s_colp1"], w=["s_colp1"])
            b = bank()

            def mmdt(h, b=b):
                for k in range(8):
                    ins = h.matmul(out=pf[b][0:NH, 0:NS], lhsT=wdt3[:, k, :], rhs=usT3[:, k, :], start=(k == 0), stop=(k == 7))
                return ins
            P.add("pe", mmdt, r=["usT", "wdt"], w=[("pf", b)])
            if P.dbg == "s0":
                rawd = A.f32(NS)
                P.add("dve", lambda h, b=b: h.tensor_copy(out=rawd[0:NH, :], in_=pf[b][0:NH, 0:NS]), r=[("pf", b)], w=["s_rawd"])
                P.dump("rawd", rawd[0:NH, :], NS, ["s_rawd"], parts=NH)
            act(dtT[0:NH, 0:NS], pf[b][0:NH, 0:NS], AF.Exp, [("pf", b), "s_colp0"], ["s_dtT"], bias=colp[0:NH, 0:1])
            act(dtT[0:NH, 0:NS], dtT[0:NH, 0:NS], AF.Ln, ["s_dtT"], ["s_dtT"], bias=1.0)
            act(dtT[0:NH, NS:2 * NS], dtT[0:NH, 0:NS], AF.Exp, ["s_dtT", "s_colp1"], ["s_dAT"], scale=colp[0:NH, 1:2])
            Eexp = A.f32(HP)
            P.add("sp", lambda h: h.dma_start(out=Eexp[0:NH, :], in_=c_exp[:, :]), w=["s_E"], dma="s_ld5")
            for half in range(2):
                b = bank()

                def mme(h, b=b, half=half):
                    for tl in range(8):
                        t = half * 8 + tl
                        ins = h.matmul(out=pf[b][:, tl * 33:(tl + 1) * 33], lhsT=Eexp[0:NH, t * 128:(t + 1) * 128],
                                       rhs=dtT[0:NH, 0:33], start=True, stop=True)
                    return ins
                P.add("pe", mme, r=["s_E", "s_dtT", "s_dAT", "s_dcol"], w=[("pf", b)])
                P.add("dve", lambda h, b=b, half=half: h.tensor_copy(
                    out=dexp[:, half * 8:(half + 1) * 8, :], in_=pf[b][:, 0:8 * 33].rearrange("p (t c) -> p t c", t=8)),
                    r=[("pf", b)], w=[("s_dexp", half)])
            dexp_t = [("s_dexp", 0), ("s_dexp", 1)]
            cst = A.f32(CONVD)
            P.add("sp", lambda h: h.dma_start(out=cst[0:48, :], in_=st_cssm[:, :]), w=["s_cst"], dma="s_ld6")
            bufT = A.f32(24 * 48).rearrange("p (b r) -> p b r", b=24)
            for b0 in range(0, 24, 8):
                b = bank()

                def trc(h, b=b, b0=b0):
                    for j in range(8):
                        ins = h.transpose(out=pf[b][:, j * 48:(j + 1) * 48], in_=cst[0:48, (b0 + j) * 128:(b0 + j + 1) * 128],
                                          identity=ident_f[0:48, 0:48])
                    return ins
                P.add("pe", trc, r=["s_cst", "ident_f"], w=[("pf", b)])
                P.add("dve", lambda h, b=b, b0=b0: h.tensor_copy(out=bufT[:, b0:b0 + 8, :],
                                                              in_=pf[b][:, 0:8 * 48].rearrange("p (j r) -> p j r", j=8)),
                      r=[("pf", b)], w=[("s_bufT", b0)])
            bufT_t = [("s_bufT", b0) for b0 in (0, 8, 16)]
            P.add("sp", lambda h: h.dma_start(out=cssm_s.rearrange("(s k) c -> s k c", k=3)[:, 0:2, :],
                                              in_=st_cssm.rearrange("(s k) c -> s k c", k=3)[:, 1:3, :]),
                  dma="o_cssm_s0", out=True)
            xbcT = zxT[:, 16:40, :]
            prod = A.f32(24 * 48)
            cwb = cw_sb.rearrange("p (k b) -> p b k", k=4)
            tt("dve", prod.rearrange("p (b s k) -> p b s k", b=24, s=NS), bufT.rearrange("p b (s k) -> p b s k", s=NS),
               cwb[:, :, 0:3].unsqueeze(2).to_broadcast([128, 24, NS, 3]), ALU.mult, bufT_t + ["params"], ["s_prod"])
            P.add("dve", lambda h: h.tensor_reduce(out=xc, in_=prod.rearrange("p (b s k) -> p b s k", b=24, s=NS),
                                                   axis=AX.X, op=ALU.add), r=["s_prod"], w=["s_xc"])
            tmp24 = A.f32(24 * NS).rearrange("p (b s) -> p b s", b=24)
            tt("dve", tmp24, xbcT, cwb[:, :, 3:4].to_broadcast([128, 24, NS]), ALU.mult, ["s_proj", "params"], ["s_tmp24"])
            tt("dve", xc, xc, tmp24, ALU.add, ["s_xc", "s_tmp24"], ["s_xc"])
            tt("dve", xc, xc, cb_sb.unsqueeze(2).to_broadcast([128, 24, NS]), ALU.add, ["s_xc", "params"], ["s_xc"])
            act(xc, xc, AF.Silu, ["s_xc"], ["s_xc"])
            xrow = A.f32(CONVD)
            for b0 in range(0, 24, 4):
                b = bank()

                def trr(h, b=b, b0=b0):
                    for j in range(4):
                        ins = h.transpose(out=pf[b][0:16, j * 128:(j + 1) * 128], in_=xbcT[:, b0 + j, :], identity=ident_f)
                    return ins
                P.add("pe", trr, r=["s_proj", "ident_f"], w=[("pf", b)])
                P.add("dve", lambda h, b=b, b0=b0: h.tensor_copy(out=xrow[0:16, b0 * 128:(b0 + 4) * 128], in_=pf[b][0:16, :]),
                      r=[("pf", b)], w=[("s_xrow", b0)])
            P.add("sp", lambda h: h.dma_start(out=cssm_s.rearrange("(s k) c -> s k c", k=3)[:, 2, :], in_=xrow[0:16, :]),
                  r=[("s_xrow", b0) for b0 in range(0, 24, 4)], dma="o_cssm_s1", out=True)
            for which, c0 in ((0, 16), (1, 20)):
                b = bank()

                def trb(h, b=b, c0=c0):
                    for j in range(4):
                        ins = h.transpose(out=pf[b][0:16, j * 128:(j + 1) * 128], in_=xc[:, c0 + j, :], identity=ident_f)
                    return ins
                P.add("pe", trb, r=["s_xc", "ident_f"], w=[("pf", b)])
                P.add("dve", lambda h, b=b, which=which: h.tensor_copy(out=BCtok[0:16, which * 512:(which + 1) * 512],
                                                                    in_=pf[b][0:16, :]),
                      r=[("pf", b)], w=[("s_BC", which)])
            P.add("dve", lambda h: h.tensor_copy(out=sel[0:16, :].rearrange("p (s m) -> p s m", s=NS),
                                                 in_=ident_f[0:16, 0:16].unsqueeze(2).to_broadcast([16, NS, 128])),
                  r=["ident_f"], w=["s_sel"])
            xT = xc[:, 0:16, :]
            tt("dve", xdtT, xT, dexp[:, :, 0:NS], ALU.mult, ["s_xc"] + dexp_t, ["s_xdtT"])
            P.fence()
            PH.cur = s_mark
            h0 = [A.f32(HP) for _ in range(2)]
            hn = [A.f32(HP) for _ in range(2)]
            tmpe = {"dve": A.f32(HP), "pool": A.f32(HP)}
            tmpc = A.f32(HP)
            bcs = [A.f32(1024) for _ in range(2)]
            def s_partB(s_):
                q = s_ % 2
                tt("dve", tmpc.rearrange("p (g j n) -> p g j n", g=4, j=4), hn[q].rearrange("p (g j n) -> p g j n", g=4, j=4),
                   bcs[q][:, 512:1024].rearrange("p (g n) -> p g n", g=4).unsqueeze(2).to_broadcast([128, 4, 4, 128]), ALU.mult,
                   [("s_bcs", q, 1), ("s_hn", q)], ["s_tmpc"])
                P.add("dve", lambda h: h.tensor_reduce(out=yT[:, :, s_], in_=tmpc.rearrange("p (t n) -> p t n", t=16),
                                                       axis=AX.X, op=ALU.add),
                      r=["s_tmpc"], w=[("s_yT", s_)])

            for s_ in range(NS):
                q = s_ % 2
                eng = "dve"
                tmpb = tmpe["dve" if s_ % 2 == 0 else "pool"]
                h03 = h0[q].rearrange("p (t n) -> p t n", t=16)
                hn3 = hn[q].rearrange("p (t n) -> p t n", t=16)
                tm3 = tmpb.rearrange("p (t n) -> p t n", t=16)
                P.add("sp", lambda h, s_=s_, h03=h03: h.dma_start(out=h03, in_=st_ssm[s_].rearrange("(t p) n -> p t n", p=128)),
                      w=[("s_h0", q)], dma=("s_h0", q))
                bB, bC = bank(), bank()
                P.add("pe", lambda h, s_=s_, bB=bB: h.matmul(out=pf[bB][:, :], lhsT=sel[0:16, s_ * 128:(s_ + 1) * 128],
                                                            rhs=BCtok[0:16, 0:512], start=True, stop=True),
                      r=["s_sel", ("s_BC", 0)], w=[("pf", bB)])
                P.add("pe", lambda h, s_=s_, bC=bC: h.matmul(out=pf[bC][:, :], lhsT=sel[0:16, s_ * 128:(s_ + 1) * 128],
                                                            rhs=BCtok[0:16, 512:1024], start=True, stop=True),
                      r=["s_sel", ("s_BC", 1)], w=[("pf", bC)])
                act(bcs[q][:, 0:512], pf[bB][:, :], AF.Copy, [("pf", bB)], [("s_bcs", q, 0)])
                act(bcs[q][:, 512:1024], pf[bC][:, :], AF.Copy, [("pf", bC)], [("s_bcs", q, 1)])
                tt("pool", hn3, h03, dexp[:, :, NS + s_:NS + s_ + 1].to_broadcast([128, 16, 128]), ALU.mult,
                   [("s_h0", q)] + dexp_t, [("s_hn", q)])
                tt(eng, tmpb.rearrange("p (g j n) -> p g j n", g=4, j=4),
                   bcs[q][:, 0:512].rearrange("p (g n) -> p g n", g=4).unsqueeze(2).to_broadcast([128, 4, 4, 128]),
                   xdtT[:, :, s_:s_ + 1].rearrange("p (g j) o -> p g j o", g=4).to_broadcast([128, 4, 4, 128]), ALU.mult,
                   [("s_bcs", q, 0), "s_xdtT"], [("s_tmp", q)])
                tt("pool", hn3, hn3, tm3, ALU.add, [("s_hn", q), ("s_tmp", q)], [("s_hn", q)])
                P.add("act", lambda h, s_=s_, hn3=hn3: h.dma_start(out=ssm_s[s_].rearrange("(t p) n -> p t n", p=128), in_=hn3),
                      r=[("s_hn", q)], dma=("o_ssm_s", q), out=True)
                if s_ >= 1:
                    s_partB(s_ - 1)
            s_partB(NS - 1)
            yT_all = [("s_yT", s_) for s_ in range(NS)]
            P.add("dve", lambda h: h.tensor_copy(out=yT[:, 0:1, 0:1], in_=yT[:, 0:1, 0:1]), r=yT_all, w=["s_yT"])
            t16 = A.f32(16 * NS).rearrange("p (t s) -> p t s", t=16)
            tt("dve", t16, xT, dexp[:, :, 32:33].to_broadcast([128, 16, NS]), ALU.mult, ["s_xc"] + dexp_t, ["s_t16"])
            tt("dve", yT, yT, t16, ALU.add, ["s_yT", "s_t16"], ["s_yT"])
            zT = zxT[:, 0:16, :]
            act(t16, zT, AF.Silu, ["s_proj", "s_t16"], ["s_t16"])
            tt("dve", yT, yT, t16, ALU.mult, ["s_yT", "s_t16"], ["s_yT"])
            tt("dve", t16, yT, yT, ALU.mult, ["s_yT"], ["s_t16"])
            b = s_colstats(t16.rearrange("p t s -> p (t s)"), 16 * NS, A, "s_t16")
            rs = A.f32(NS)
            P.add("dve", lambda h, b=b: h.tensor_reduce(out=rs, in_=pf[b][:, 0:16 * NS].rearrange("p (t s) -> p s t", t=16),
                                                   axis=AX.X, op=ALU.add), r=[("pf", b)], w=["s_rs"])
            P.add("dve", lambda h: h.tensor_scalar(out=rs, in0=rs, scalar1=1.0 / HP, scalar2=EPS, op0=ALU.mult, op1=ALU.add),
                  r=["s_rs"], w=["s_rs"])
            tt("pool", rs, rs, nhalf[:, 0:1].to_broadcast([128, NS]), ALU.pow, ["s_rs", "nhalf"], ["s_rs"])
            tt("dve", yT, yT, rs.unsqueeze(1).to_broadcast([128, 16, NS]), ALU.mult, ["s_yT", "s_rs"], ["s_yT"])
            aT = A.bf16(16 * NS).rearrange("p (t s) -> p t s", t=16)
            tt("dve", aT, yT, ngs.unsqueeze(2).to_broadcast([128, 16, NS]), ALU.mult, ["s_yT", "params"], ["s_aT"])
            if P.dbg == "s0":
                P.fence()
                P.dump("dtT", dtT[0:NH, :], 33, ["s_dtT", "s_dAT", "s_dcol"], parts=NH)
                P.dump("dexp", dexp.rearrange("p t c -> p (t c)"), 16 * 33, dexp_t)
                P.dump("zxT", zxT.rearrange("p b s -> p (b s)"), 40 * NS, ["s_proj"])
                P.dump("xc", xc.rearrange("p b s -> p (b s)"), 24 * NS, ["s_xc"])
                P.dump("yT", yT.rearrange("p t s -> p (t s)"), 16 * NS, ["s_yT"])
                P.dump("BCtok", BCtok[0:16, :], 1024, [("s_BC", 0), ("s_BC", 1)], parts=16)
                P.dump("usT", usT, 8 * NS, ["usT"])
                P.fence()
            s_outproj(0, aT, A)
            if P.dbg == "s0":
                P.fence()
                P.dump("xs1", xs_sb[0:16, :], D, ["xs_sb"], parts=16)

        def sample_layer1():
            PH.reset()
            A = PH
            load_gpost(1)
            s_prenorm(1, A)
            vgz = A.f32(48 * NS).rearrange("p (b s) -> p b s", b=48)
            for vb in range(4):
                s_inproj_block(vgz, vb * 4)
                s_inproj_block(vgz, 16 + vb * 4)
            for zb in range(4):
                s_inproj_block(vgz, 32 + zb * 4)
            tt("dve", vgz, vgz, bin_sb.unsqueeze(2).to_broadcast([128, 48, NS]), ALU.add, ["s_proj", "params"], ["s_proj"])
            sg = A.f32(16 * NS).rearrange("p (b s) -> p b s", b=16)
            act(sg, vgz[:, 16:32, :], AF.Sigmoid, ["s_proj"], ["s_sg"])
            vT = A.f32(16 * NS).rearrange("p (b s) -> p b s", b=16)
            tt("dve", vT, vgz[:, 0:16, :], sg, ALU.mult, ["s_proj", "s_sg"], ["s_vT"])
            bufT = A.f32(16 * NS * 30).rearrange("p (b s k) -> p b s k", b=16, s=NS)
            cst = [A.f32(CW) for _ in range(2)]
            for sg4 in range(4):
                q = sg4 % 2
                P.add("sp", lambda h, sg4=sg4, q=q: h.dma_start(out=cst[q][0:120, :], in_=st_ccfm[sg4 * 120:(sg4 + 1) * 120, :]),
                      w=[("s_cst1", q)], dma=("s_cst1", q))
                for b0 in range(0, 16, 4):
                    b = bank()

                    def trc(h, b=b, b0=b0, q=q):
                        for j in range(4):
                            ins = h.transpose(out=pf[b][:, j * 120:(j + 1) * 120], in_=cst[q][0:120, (b0 + j) * 128:(b0 + j + 1) * 128],
                                              identity=ident_f[0:120, 0:120])
                        return ins
                    P.add("pe", trc, r=[("s_cst1", q), "ident_f"], w=[("pf", b)])
                    P.add("dve", lambda h, b=b, b0=b0, sg4=sg4: h.tensor_copy(
                        out=bufT[:, b0:b0 + 4, sg4 * 4:(sg4 + 1) * 4, :],
                        in_=pf[b][:, 0:480].rearrange("p (j s k) -> p j s k", j=4, s=4)),
                        r=[("pf", b)], w=[("s_bufT1", sg4, b0)])
            bufT_t = [("s_bufT1", a, b0) for a in range(4) for b0 in range(0, 16, 4)]
            P.add("sp", lambda h: h.dma_start(out=ccfm_s.rearrange("(s k) c -> s k c", k=30)[:, 0:29, :],
                                              in_=st_ccfm.rearrange("(s k) c -> s k c", k=30)[:, 1:30, :]),
                  dma="o_ccfm_s0", out=True)
            vrow = A.f32(CW)
            for b0 in range(0, 16, 4):
                b = bank()

                def trr(h, b=b, b0=b0):
                    for j in range(4):
                        ins = h.transpose(out=pf[b][0:16, j * 128:(j + 1) * 128], in_=vT[:, b0 + j, :], identity=ident_f)
                    return ins
                P.add("pe", trr, r=["s_vT", "ident_f"], w=[("pf", b)])
                P.add("dve", lambda h, b=b, b0=b0: h.tensor_copy(out=vrow[0:16, b0 * 128:(b0 + 4) * 128], in_=pf[b][0:16, :]),
                      r=[("pf", b)], w=[("s_vrow", b0)])
            P.add("sp", lambda h: h.dma_start(out=ccfm_s.rearrange("(s k) c -> s k c", k=30)[:, 29, :], in_=vrow[0:16, :]),
                  r=[("s_vrow", b0) for b0 in range(0, 16, 4)], dma="o_ccfm_s1", out=True)
            ccwb = ccw_sb.rearrange("p (k b) -> p b k", k=CK)
            prod = A.f32(16 * NS * 30).rearrange("p (b s k) -> p b s k", b=16, s=NS)
            tt("dve", prod, bufT, ccwb[:, :, 0:30].unsqueeze(2).to_broadcast([128, 16, NS, 30]), ALU.mult,
               bufT_t + ["params"], ["s_prod1"])
            cT = A.f32(2 * 16 * NS)
            c3 = cT[:, 0:16 * NS].rearrange("p (b s) -> p b s", b=16)
            c2 = cT[:, 16 * NS:2 * 16 * NS].rearrange("p (b s) -> p b s", b=16)
            P.add("dve", lambda h: h.tensor_reduce(out=c3, in_=prod, axis=AX.X, op=ALU.add), r=["s_prod1"], w=["s_c"])
            t16 = A.f32(16 * NS).rearrange("p (b s) -> p b s", b=16)
            tt("dve", t16, vT, ccwb[:, :, 30:31].to_broadcast([128, 16, NS]), ALU.mult, ["s_vT", "params"], ["s_t16b"])
            tt("dve", c3, c3, t16, ALU.add, ["s_c", "s_t16b"], ["s_c"])
            tt("dve", c3, c3, ccb_sb.unsqueeze(2).to_broadcast([128, 16, NS]), ALU.add, ["s_c", "params"], ["s_c"])
            tt("dve", c2, c3, c3, ALU.mult, ["s_c"], ["s_c2"])
            b = s_colstats(cT, 2 * 16 * NS, A, "s_c2")
            st = A.f32(2 * NS)
            P.add("dve", lambda h, b=b: h.tensor_reduce(out=st.rearrange("p (j s) -> p j s", j=2),
                                                   in_=pf[b][:, 0:512].rearrange("p (j b s) -> p j s b", j=2, b=16),
                                                   axis=AX.X, op=ALU.add), r=[("pf", b)], w=["s_st"])
            P.add("dve", lambda h: h.tensor_scalar(out=st, in0=st, scalar1=1.0 / CW, scalar2=None, op0=ALU.mult),
                  r=["s_st"], w=["s_st"])
            msq = A.f32(NS)
            tt("dve", msq, st[:, 0:NS], st[:, 0:NS], ALU.mult, ["s_st"], ["s_msq"])
            tt("dve", msq, st[:, NS:2 * NS], msq, ALU.subtract, ["s_st", "s_msq"], ["s_msq"])
            P.add("dve", lambda h: h.tensor_scalar(out=msq, in0=msq, scalar1=0.0, scalar2=EPS, op0=ALU.max, op1=ALU.add),
                  r=["s_msq"], w=["s_msq"])
            tt("pool", msq, msq, nhalf[:, 0:1].to_broadcast([128, NS]), ALU.pow, ["s_msq", "nhalf"], ["s_msq"])
            tt("dve", c3, c3, st[:, 0:NS].unsqueeze(1).to_broadcast([128, 16, NS]), ALU.subtract, ["s_c", "s_st", "s_c2"], ["s_c"])
            tt("dve", c3, c3, msq.unsqueeze(1).to_broadcast([128, 16, NS]), ALU.mult, ["s_c", "s_msq"], ["s_c"])
            tt("dve", c3, c3, lng_sb.unsqueeze(2).to_broadcast([128, 16, NS]), ALU.mult, ["s_c", "params"], ["s_c"])
            tt("dve", c3, c3, lnb_sb.unsqueeze(2).to_broadcast([128, 16, NS]), ALU.add, ["s_c", "params"], ["s_c"])
            act(c3, c3, AF.Silu, ["s_c"], ["s_c"])
            act(t16, vgz[:, 32:48, :], AF.Silu, ["s_proj", "s_t16b"], ["s_t16b"])
            aT = A.bf16(16 * NS).rearrange("p (t s) -> p t s", t=16)
            tt("dve", aT, c3, t16, ALU.mult, ["s_c", "s_t16b"], ["s_aT"])
            s_outproj(1, aT, A)
            P.add("sp", lambda h: h.dma_start(out=ys[:, :], in_=xs_sb[0:16, :]), r=["xs_sb"], dma="o_ys", out=True)

        P.dbg = dbg
        if dbg:
            def dump(name, ap, n, toks, parts=128):
                d = nc.dram_tensor("dbg_" + name, [parts, n], F32, kind="ExternalOutput").ap()
                P.add("pool", lambda h: h.dma_start(out=d[:, :], in_=ap), r=toks, dma="dbg_" + name, out=True)
            P.dump = dump
        P.fence()
        if do_sample and dbg != "l0":
            sample_layer0()
            P.fence()
            if not dbg:
                sample_layer1()
                P.fence()
            PH.limit = ARENA_WORDS
        if dbg == "l0":
            dbg_out = {}

            def dump(name, ap, n, toks):
                d = nc.dram_tensor("dbg_" + name, [128, n], F32, kind="ExternalOutput").ap()
                P.add("pool", lambda h: h.dma_start(out=d[:, :], in_=ap), r=toks, dma="dbg_" + name, out=True)
            P.dump = dump
        for G in range(NG if not dbg else (1 if dbg == "l0" else 0)):
            layer0(G)
            P.fence()
            if dbg:
                break
            layer1(G)
            P.fence()

        PH.reset()
        if dbg:
            P.fence()
        fst = [PH.f32(128) for _ in range(2)]
        hT = PH.f32(128)
        b = bank()
        P.add("pe", lambda h, b=b: h.transpose(out=pf[b][0:72, 0:128], in_=halo[:, 0:72], identity=ident_f),
              r=[("halo", cb) for cb in range(24)] + ["ident_f"], w=[("pf", b)])
        P.add("dve", lambda h, b=b: h.tensor_copy(out=hT[0:72, :], in_=pf[b][0:72, 0:128]), r=[("pf", b)], w=["hT"])
        P.add("sp", lambda h: h.dma_start(out=cssm_p[:, :], in_=hT[0:72, :]), r=["hT"], dma="o_cssm", out=True)
        for t in range(16):
            q = t % 2
            b = bank()
            P.add("pe", lambda h, b=b, t=t: h.transpose(out=pf[b][:, 0:128], in_=prevT[:, t * 128:(t + 1) * 128], identity=ident_f),
                  r=[("prevT", t // 4), "ident_f"], w=[("pf", b)])
            P.add("dve", lambda h, b=b, q=q: h.tensor_copy(out=fst[q], in_=pf[b][:, 0:128]), r=[("pf", b)], w=[("fst", q)])
            P.add("sp", lambda h, t=t, q=q: h.dma_start(out=ssm_p[t * 128:(t + 1) * 128, :], in_=fst[q]),
                  r=[("fst", q)], w=[("fst_d", q)], dma=("o_ssm", q), out=True)
            P.readers.setdefault(("fst", q), []).append(P.prev)

        P.emit()
        info = {"stats": P.stats, "arena_peak": max(CA.peak, PH.peak)}
    return nc, info


_CACHE = {}


def _consts():
    ident = np.eye(128, dtype=np.float32)
    tri = np.triu(np.ones((128, 128), np.float32))
    U = np.tril(np.ones((128, 128), np.float32), -1)
    ones = np.ones((128, 128), np.float32)
    expm = np.zeros((NH, HP), np.float32)
    for h in range(NH):
        expm[h, h * PD:(h + 1) * PD] = 1.0
    blk = np.zeros((128, 32), np.float32)
    for p in range(128):
        blk[p, p % 32] = 1.0
    return ident, tri, U, ones, expm, blk


def kernel(x_prompt, x_sample, state_ssm, state_conv_ssm, state_conv_cfm, g_pre, g_post,
           ssm_w_in, ssm_conv_w, ssm_conv_b, ssm_dt_bias, ssm_A_log, ssm_D, ssm_norm_g, ssm_w_out,
           cfm_w_in, cfm_b_in, cfm_conv_w, cfm_conv_b, cfm_ln_g, cfm_ln_b, cfm_w_out):
    f = lambda a: np.ascontiguousarray(np.asarray(a, dtype=np.float32))
    if "nc" not in _CACHE:
        _CACHE["nc"] = build_program(do_sample=True)
    nc, info = _CACHE["nc"]
    ident, tri, U, ones, expm, blkm = _consts()
    x_prompt = f(x_prompt)
    x_sample = f(x_sample).reshape(128, D)
    state_ssm = f(state_ssm).reshape(128, HP, NST)
    state_conv_ssm = f(state_conv_ssm).reshape(128, 3, CONVD)
    state_conv_cfm = f(state_conv_cfm).reshape(128, 30, CW)
    shared = {
        "g_pre": f(g_pre).reshape(16, 128), "g_post": f(g_post).reshape(1, 2 * D),
        "w_in0": f(ssm_w_in).reshape(D, SSM_PROJ), "cw0": f(ssm_conv_w).reshape(96, 128),
        "cb0": f(ssm_conv_b).reshape(24, 128), "dtb": f(ssm_dt_bias).reshape(1, NH),
        "alog": f(ssm_A_log).reshape(1, NH), "dsk": f(ssm_D).reshape(1, NH),
        "ng0": f(ssm_norm_g).reshape(16, 128), "w_out0": f(ssm_w_out).reshape(HP, D),
        "w_in1": f(cfm_w_in).reshape(D, 3 * CW), "b_in1": f(cfm_b_in).reshape(48, 128),
        "ccw": f(cfm_conv_w).reshape(496, 128), "ccb": f(cfm_conv_b).reshape(16, 128),
        "lng": f(cfm_ln_g).reshape(16, 128), "lnb": f(cfm_ln_b).reshape(16, 128),
        "w_out1": f(cfm_w_out).reshape(CW, D),
        "c_ident": ident, "c_tri": tri, "c_U": U, "c_ones": ones, "c_exp": expm, "c_blk": blkm,
    }
    in_maps = []
    for i in range(8):
        m = dict(shared)
        m["xp"] = x_prompt[i]
        m["xs"] = x_sample[16 * i:16 * (i + 1)]
        m["st_ssm"] = state_ssm[16 * i:16 * (i + 1)]
        m["st_cssm"] = state_conv_ssm[16 * i:16 * (i + 1)].reshape(48, CONVD)
        m["st_ccfm"] = state_conv_cfm[16 * i:16 * (i + 1)].reshape(480, CW)
        in_maps.append(m)
    res = run_bass_kernel_spmd(nc, in_maps, core_ids=list(range(8)))
    R = res.results
    y_prompt = np.stack([R[i]["yp"] for i in range(8)]).reshape(8, SEQ, D)
    y_sample = np.concatenate([R[i]["ys"] for i in range(8)]).reshape(128, 1, D)
    new_ssm_p = np.stack([R[i]["ssm_p"] for i in range(8)]).reshape(1, 8, NH, PD, NST)
    new_cssm_p = np.stack([R[i]["cssm_p"] for i in range(8)]).reshape(1, 8, 3, CONVD)
    new_ccfm_p = np.stack([R[i]["ccfm_p"] for i in range(8)]).reshape(1, 8, 30, CW)
    new_ssm_s = np.concatenate([R[i]["ssm_s"] for i in range(8)]).reshape(1, 128, NH, PD, NST)
    new_cssm_s = np.concatenate([R[i]["cssm_s"] for i in range(8)]).reshape(1, 128, 3, CONVD)
    new_ccfm_s = np.concatenate([R[i]["ccfm_s"] for i in range(8)]).reshape(1, 128, 30, CW)
    return (y_prompt, y_sample, new_ssm_p, new_cssm_p, new_ccfm_p, new_ssm_s, new_cssm_s, new_ccfm_s)
```
